# Optimizing a Trainium2 kernel written in Bass

```python
import jax, jax.numpy as jnp
from jax import lax
import numpy as np

D_MODEL = 1024
BATCH = 8
SEQ = 2048
DEPTH = 2
DEC_BATCH = 128
DEC_SEQ = 1
PAST_LEN = 16384
PAGE_SIZE = 128

N_META = 16
D_MIX = D_MODEL
DN_HEADS = 4
DN_DK = D_MIX // 8
DN_DV = D_MIX // 8
DN_W = DN_HEADS * DN_DV
LRU_W = D_MIX // 4
LRU_BLOCKS = 4
LRU_BD = LRU_W // LRU_BLOCKS
LRU_C = 8.0
RET_HEADS = 4
RET_DK = D_MIX // 16
RET_DV = D_MIX // 16
RET_W = RET_HEADS * RET_DV
CONV_W = 4
CHUNK = 64
D_FF = 128 * ((8 * D_MODEL // 3 + 127) // 128)
ROPE_BASE = 10000.0
EPS = 1e-6
IN_SIZES = (3 * DN_W, DN_HEADS, DN_HEADS, DN_W,
            LRU_W, LRU_W,
            RET_HEADS * RET_DK, RET_HEADS * RET_DK, RET_W, RET_W)
D_IN = sum(IN_SIZES)

kernel_name = 'hybrid_deltanet_rglru_retention_step'


def _split_offsets():
    out, s = [], 0
    for n in IN_SIZES[:-1]:
        s += n
        out.append(s)
    return out


def _rmsnorm(x, w):
    xf = x.astype(jnp.float32)
    y = xf * lax.rsqrt(jnp.mean(xf * xf, axis=-1, keepdims=True) + EPS)
    return (y * w.astype(jnp.float32)).astype(x.dtype)


def _swiglu(x, w_in, w_out):
    gate, up = jnp.split(x @ w_in, 2, axis=-1)
    return (jax.nn.silu(gate) * up) @ w_out


def _causal_conv(x, buf, w):
    T = x.shape[1]
    xp = jnp.concatenate([buf.astype(x.dtype), x], axis=1)
    y = xp[:, 0:T] * w[0]
    for j in range(1, CONV_W):
        y = y + xp[:, j:j + T] * w[j]
    return y, xp[:, T:]


def _l2norm(x):
    return x * lax.rsqrt(jnp.sum(x * x, axis=-1, keepdims=True) + EPS)


def _rope(x, pos):
    half = x.shape[-1] // 2
    inv = ROPE_BASE ** (-jnp.arange(half, dtype=jnp.float32) / half)
    ang = pos.astype(jnp.float32)[:, None] * inv[None, :]
    cos = jnp.cos(ang)[None, :, None, :]
    sin = jnp.sin(ang)[None, :, None, :]
    x1, x2 = x[..., :half], x[..., half:]
    return jnp.concatenate([x1 * cos - x2 * sin, x1 * sin + x2 * cos], axis=-1)


def _decay_masks(G):
    C = G.shape[-1]
    t = jnp.arange(C)
    diff = G[..., :, None] - G[..., None, :]
    strict = jnp.exp(jnp.where(t[:, None] > t[None, :], diff, -jnp.inf))
    incl = jnp.exp(jnp.where(t[:, None] >= t[None, :], diff, -jnp.inf))
    return strict, incl


def _delta_step(S0, q, k, v, g, beta):
    dv = v.shape[-1]
    G = jnp.cumsum(g, axis=-1)
    eG = jnp.exp(G)
    dec_strict, dec_incl = _decay_masks(G)
    A = beta[..., :, None] * jnp.einsum('bhtd,bhid->bhti', k, k) * dec_strict
    rhs = jnp.concatenate([beta[..., None] * v, (beta * eG)[..., None] * k], axis=-1)
    sol = lax.linalg.triangular_solve(A, rhs, left_side=True, lower=True, unit_diagonal=True)
    U = sol[..., :dv] - jnp.einsum('bhtd,bhde->bhte', sol[..., dv:], S0)
    o = (eG[..., None] * jnp.einsum('bhtd,bhde->bhte', q, S0)
         + jnp.einsum('bhti,bhie->bhte', jnp.einsum('bhtd,bhid->bhti', q, k) * dec_incl, U))
    GC = G[..., -1]
    S = (jnp.exp(GC)[..., None, None] * S0
         + jnp.einsum('bhid,bhie->bhde', k * jnp.exp(GC[..., None] - G)[..., None], U))
    return o, S


def _ret_step(S0, q, k, v, g):
    G = jnp.cumsum(g, axis=-1)
    _, dec_incl = _decay_masks(G)
    o = (jnp.exp(G)[..., None] * jnp.einsum('bhtd,bhde->bhte', q, S0)
         + jnp.einsum('bhti,bhie->bhte', jnp.einsum('bhtd,bhid->bhti', q, k) * dec_incl, v))
    GC = G[..., -1]
    S = (jnp.exp(GC)[..., None, None] * S0
         + jnp.einsum('bhid,bhie->bhde', k * jnp.exp(GC[..., None] - G)[..., None], v))
    return o, S


def _run_chunks(step, S0, xs):
    o_lead, S = step(S0, *[a[:, :, :N_META] for a in xs])
    rest = [a[:, :, N_META:] for a in xs]
    T = rest[0].shape[2]
    n = T // CHUNK

    def to_chunks(a):
        return jnp.moveaxis(a.reshape(a.shape[:2] + (n, CHUNK) + a.shape[3:]), 2, 0)

    def body(S_c, c):
        o_c, S_n = step(S_c, *c)
        return S_n, o_c

    S, o_rest = lax.scan(body, S, tuple(to_chunks(a) for a in rest))
    o_rest = jnp.moveaxis(o_rest, 0, 2)
    o_rest = o_rest.reshape(o_rest.shape[:2] + (T,) + o_rest.shape[4:])
    return jnp.concatenate([o_lead, o_rest], axis=2), S


def _lin_op(c1, c2):
    a1, b1 = c1
    a2, b2 = c2
    return a1 * a2, a2 * b1 + b2


def _token_mix(u, pos, st, w_in, dn_conv_w, dn_a_log, dn_dt_bias, dn_norm_w,
               lru_conv_w, lru_conv_b, lru_wa, lru_ba, lru_wx, lru_bx, lru_lambda, w_out):
    f32 = jnp.float32
    B, T, _ = u.shape
    if st is None:
        S_dn0 = jnp.zeros((B, DN_HEADS, DN_DK, DN_DV), f32)
        buf_dn = jnp.zeros((B, CONV_W - 1, 3 * DN_W), u.dtype)
        h0 = jnp.zeros((B, LRU_W), f32)
        buf_lru = jnp.zeros((B, CONV_W - 1, LRU_W), u.dtype)
        S_ret0 = jnp.zeros((B, RET_HEADS, RET_DK, RET_DV), f32)
    else:
        S_dn0, buf_dn, h0, buf_lru, S_ret0 = st
        S_dn0, h0, S_ret0 = S_dn0.astype(f32), h0.astype(f32), S_ret0.astype(f32)
    proj = u @ w_in
    (dn_qkv, dn_a, dn_b, dn_z, lru_x, lru_y,
     ret_q, ret_k, ret_v, ret_g) = jnp.split(proj, _split_offsets(), axis=-1)

    qkv, buf_dn_new = _causal_conv(dn_qkv, buf_dn, dn_conv_w)
    qkv = jax.nn.silu(qkv).astype(f32)
    q, k, v = jnp.split(qkv, 3, axis=-1)
    q = jnp.swapaxes(_l2norm(q.reshape(B, T, DN_HEADS, DN_DK)), 1, 2) * (DN_DK ** -0.5)
    k = jnp.swapaxes(_l2norm(k.reshape(B, T, DN_HEADS, DN_DK)), 1, 2)
    v = jnp.swapaxes(v.reshape(B, T, DN_HEADS, DN_DV), 1, 2)
    g = -jnp.exp(dn_a_log.astype(f32)) * jax.nn.softplus(dn_a.astype(f32) + dn_dt_bias.astype(f32))
    g = jnp.swapaxes(g, 1, 2)
    beta = jnp.swapaxes(jax.nn.sigmoid(dn_b.astype(f32)), 1, 2)
    if st is None:
        o_dn, S_dn = _run_chunks(_delta_step, S_dn0, (q, k, v, g, beta))
    else:
        o_dn, S_dn = _delta_step(S_dn0, q, k, v, g, beta)
    o_dn = jnp.swapaxes(o_dn, 1, 2)
    o_dn = o_dn * lax.rsqrt(jnp.mean(o_dn * o_dn, axis=-1, keepdims=True) + EPS) * dn_norm_w.astype(f32)
    o_dn = (o_dn * jax.nn.silu(dn_z.astype(f32).reshape(B, T, DN_HEADS, DN_DV))).reshape(B, T, DN_W)

    xl, buf_lru_new = _causal_conv(lru_x, buf_lru, lru_conv_w)
    xl = (xl + lru_conv_b).astype(f32)
    xb = xl.reshape(B, T, LRU_BLOCKS, LRU_BD)
    r = jax.nn.sigmoid(jnp.einsum('btnd,nde->btne', xb, lru_wa.astype(f32)).reshape(B, T, LRU_W) + lru_ba.astype(f32))
    i = jax.nn.sigmoid(jnp.einsum('btnd,nde->btne', xb, lru_wx.astype(f32)).reshape(B, T, LRU_W) + lru_bx.astype(f32))
    log_a = -LRU_C * r * jax.nn.softplus(-lru_lambda.astype(f32))
    a = jnp.exp(log_a)
    b = jnp.sqrt(jnp.maximum(-jnp.expm1(2.0 * log_a), 0.0)) * (i * xl)
    b = b.at[:, 0].add(a[:, 0] * h0)
    _, h = lax.associative_scan(_lin_op, (a, b), axis=1)
    h_new = h[:, -1]
    o_lru = h * jax.nn.gelu(lru_y.astype(f32))

    rq = jnp.swapaxes(_rope(ret_q.astype(f32).reshape(B, T, RET_HEADS, RET_DK), pos), 1, 2)
    rk = jnp.swapaxes(_rope(ret_k.astype(f32).reshape(B, T, RET_HEADS, RET_DK), pos), 1, 2) * (RET_DK ** -0.5)
    rv = jnp.swapaxes(ret_v.astype(f32).reshape(B, T, RET_HEADS, RET_DV), 1, 2)
    log_gamma = jnp.log1p(-jnp.exp2(-5.0 - jnp.arange(RET_HEADS, dtype=f32)))
    g_ret = jnp.broadcast_to(log_gamma[None, :, None], (B, RET_HEADS, T))
    if st is None:
        o_ret, S_ret = _run_chunks(_ret_step, S_ret0, (rq, rk, rv, g_ret))
    else:
        o_ret, S_ret = _ret_step(S_ret0, rq, rk, rv, g_ret)
    o_ret = jnp.swapaxes(o_ret, 1, 2)
    mu = jnp.mean(o_ret, axis=-1, keepdims=True)
    var = jnp.mean(jnp.square(o_ret - mu), axis=-1, keepdims=True)
    o_ret = ((o_ret - mu) * lax.rsqrt(var + EPS)).reshape(B, T, RET_W) * jax.nn.silu(ret_g.astype(f32))

    o = jnp.concatenate([o_dn, o_lru, o_ret], axis=-1).astype(u.dtype) @ w_out
    return o, (S_dn, buf_dn_new, h_new, buf_lru_new, S_ret)


def _layer(x, pos, st, n1, f1_in, f1_out, n2, w_in, dn_conv_w, dn_a_log, dn_dt_bias, dn_norm_w,
           lru_conv_w, lru_conv_b, lru_wa, lru_ba, lru_wx, lru_bx, lru_lambda, w_out, n3, f2_in, f2_out):
    x = x + 0.5 * _swiglu(_rmsnorm(x, n1), f1_in, f1_out)
    m, new_st = _token_mix(_rmsnorm(x, n2), pos, st, w_in, dn_conv_w, dn_a_log, dn_dt_bias, dn_norm_w,
                           lru_conv_w, lru_conv_b, lru_wa, lru_ba, lru_wx, lru_bx, lru_lambda, w_out)
    x = x + m
    x = x + 0.5 * _swiglu(_rmsnorm(x, n3), f2_in, f2_out)
    return x, new_st


def setup_inputs(seed: int = 0) -> dict:
    key = jax.random.key(seed)
    ks = jax.random.split(key, 32)
    f32 = jnp.float32
    nrm = lambda k, s, sc: jax.random.normal(k, s, f32) * sc
    gain = lambda k, s: 1.0 + 0.01 * jax.random.normal(k, s, f32)
    dt = jnp.exp(jax.random.uniform(ks[20], (DEPTH, DN_HEADS), f32, np.log(1e-3), np.log(1e-1)))
    a_pow = jax.random.uniform(ks[21], (DEPTH, LRU_W), f32, 0.9, 0.999)
    sig_l = a_pow ** (1.0 / LRU_C)
    return {
        'x_prompt': nrm(ks[0], (BATCH, SEQ, D_MODEL), 1.0),
        'x_sample': nrm(ks[1], (DEC_BATCH, DEC_SEQ, D_MODEL), 1.0),
        'state_dn': nrm(ks[2], (DEPTH, DEC_BATCH, DN_HEADS, DN_DK, DN_DV), DN_DK ** -0.5),
        'state_dn_conv': nrm(ks[3], (DEPTH, DEC_BATCH, CONV_W - 1, 3 * DN_W), 1.0),
        'state_lru': nrm(ks[4], (DEPTH, DEC_BATCH, LRU_W), 1.0),
        'state_lru_conv': nrm(ks[5], (DEPTH, DEC_BATCH, CONV_W - 1, LRU_W), 1.0),
        'state_ret': nrm(ks[6], (DEPTH, DEC_BATCH, RET_HEADS, RET_DK, RET_DV), 1.0),
        'meta_tokens': nrm(ks[7], (N_META, D_MODEL), 1.0),
        'norm_ffn1': gain(ks[8], (DEPTH, D_MODEL)),
        'w_ffn1_in': nrm(ks[9], (DEPTH, D_MODEL, 2 * D_FF), D_MODEL ** -0.5),
        'w_ffn1_out': nrm(ks[10], (DEPTH, D_FF, D_MODEL), D_FF ** -0.5),
        'norm_mix': gain(ks[11], (DEPTH, D_MODEL)),
        'w_in': nrm(ks[12], (DEPTH, D_MODEL, D_IN), D_MODEL ** -0.5),
        'dn_conv_w': nrm(ks[13], (DEPTH, CONV_W, 3 * DN_W), CONV_W ** -0.5),
        'dn_a_log': jnp.log(jax.random.uniform(ks[14], (DEPTH, DN_HEADS), f32, 1.0, 16.0)),
        'dn_dt_bias': dt + jnp.log(-jnp.expm1(-dt)),
        'dn_norm_w': gain(ks[15], (DEPTH, DN_DV)),
        'lru_conv_w': nrm(ks[16], (DEPTH, CONV_W, LRU_W), CONV_W ** -0.5),
        'lru_conv_b': nrm(ks[17], (DEPTH, LRU_W), 0.01),
        'lru_wa': nrm(ks[18], (DEPTH, LRU_BLOCKS, LRU_BD, LRU_BD), LRU_BD ** -0.5),
        'lru_ba': nrm(ks[19], (DEPTH, LRU_W), 0.01),
        'lru_wx': nrm(ks[22], (DEPTH, LRU_BLOCKS, LRU_BD, LRU_BD), LRU_BD ** -0.5),
        'lru_bx': nrm(ks[23], (DEPTH, LRU_W), 0.01),
        'lru_lambda': jnp.log(sig_l) - jnp.log1p(-sig_l),
        'w_out': nrm(ks[24], (DEPTH, D_MIX, D_MODEL), D_MIX ** -0.5),
        'norm_ffn2': gain(ks[25], (DEPTH, D_MODEL)),
        'w_ffn2_in': nrm(ks[26], (DEPTH, D_MODEL, 2 * D_FF), D_MODEL ** -0.5),
        'w_ffn2_out': nrm(ks[27], (DEPTH, D_FF, D_MODEL), D_FF ** -0.5),
        'norm_final': gain(ks[28], (D_MODEL,)),
    }


def reference(x_prompt, x_sample, state_dn, state_dn_conv, state_lru, state_lru_conv, state_ret,
              meta_tokens, norm_ffn1, w_ffn1_in, w_ffn1_out, norm_mix, w_in, dn_conv_w, dn_a_log,
              dn_dt_bias, dn_norm_w, lru_conv_w, lru_conv_b, lru_wa, lru_ba, lru_wx, lru_bx, lru_lambda,
              w_out, norm_ffn2, w_ffn2_in, w_ffn2_out, norm_final):
    B = x_prompt.shape[0]
    meta = jnp.broadcast_to(meta_tokens.astype(x_prompt.dtype)[None], (B, N_META, D_MODEL))
    xp = jnp.concatenate([meta, x_prompt], axis=1)
    xs = x_sample
    pos_p = jnp.arange(xp.shape[1])
    pos_s = PAST_LEN + jnp.arange(xs.shape[1])
    new_p, new_s = [], []
    for l in range(DEPTH):
        w_l = (norm_ffn1[l], w_ffn1_in[l], w_ffn1_out[l], norm_mix[l], w_in[l], dn_conv_w[l], dn_a_log[l],
               dn_dt_bias[l], dn_norm_w[l], lru_conv_w[l], lru_conv_b[l], lru_wa[l], lru_ba[l], lru_wx[l],
               lru_bx[l], lru_lambda[l], w_out[l], norm_ffn2[l], w_ffn2_in[l], w_ffn2_out[l])
        xp, st_p = _layer(xp, pos_p, None, *w_l)
        st_in = (state_dn[l], state_dn_conv[l], state_lru[l], state_lru_conv[l], state_ret[l])
        xs, st_s = _layer(xs, pos_s, st_in, *w_l)
        new_p.append(st_p)
        new_s.append(st_s)
    y_prompt = _rmsnorm(xp, norm_final)[:, N_META:]
    y_sample = _rmsnorm(xs, norm_final)
    dn_p, dn_conv_p, lru_p, lru_conv_p, ret_p = [jnp.stack([s[j] for s in new_p]) for j in range(5)]
    dn_s, dn_conv_s, lru_s, lru_conv_s, ret_s = [jnp.stack([s[j] for s in new_s]) for j in range(5)]
    return (y_prompt, y_sample, dn_p, dn_conv_p, lru_p, lru_conv_p, ret_p,
            dn_s, dn_conv_s, lru_s, lru_conv_s, ret_s)
```

```python
import math
from contextlib import ExitStack
import numpy as np
import concourse.bass as bass
import concourse.mybir as mybir
from concourse.bass_utils import run_bass_kernel_spmd

F32 = mybir.dt.float32
F32R = mybir.dt.float32r
BF16 = mybir.dt.bfloat16
AF = mybir.ActivationFunctionType
ALU = mybir.AluOpType

D = 1024
NT = 2080
NP = 2064
NS = 16
DFF = 2816
DIN = 3592
EPS = 1e-6
TW = 416
TT = [(TW * i, TW) for i in range(5)]
SO = NP - TT[4][0]
CH = [(0, 16)] + [(16 + 128 * i, 128) for i in range(16)]
NCORES = 8
DEBUG_SITES = False


class Op:
    __slots__ = ("eng", "fn", "deps", "signal", "seq", "is_dma", "sem", "cnt", "prev_same_sem")

    def __init__(self, eng, fn, is_dma):
        self.eng = eng
        self.fn = fn
        self.deps = set()
        self.signal = False
        self.seq = None
        self.is_dma = is_dma
        self.sem = None
        self.cnt = None
        self.prev_same_sem = None


class Prog:
    ENGS = ("pe", "act", "dve", "pool", "sp")

    def __init__(self, nc, n_dma_sems=10):
        self.nc = nc
        self.ops = []
        self.last_w = {}
        self.readers = {}
        self.n_dma_sems = n_dma_sems
        self.dma_rr = {e: 0 for e in self.ENGS}
        self.dma_last = {}
        self.dma_cnt = {}
        self.out_dmas = []
        self.fence_deps = []

    def _record(self, o, r, w):
        deps = set(self.fence_deps)
        raw = set()
        for k in r:
            lw = self.last_w.get(k)
            if lw is not None:
                deps.add(lw)
                raw.add(lw)
            if isinstance(k, tuple) and k[0] == "psb":
                for rd in self.readers.get(k, {}).values():
                    if rd.eng != o.eng:
                        deps.add(rd)
        for k in w:
            lw = self.last_w.get(k)
            if lw is not None:
                deps.add(lw)
            for rd in self.readers.get(k, {}).values():
                deps.add(rd)
        for d in deps:
            if d is o:
                continue
            if (not d.is_dma) and (not o.is_dma) and d.eng == o.eng:
                if o.eng == "pe":
                    continue
            o.deps.add(d)
            d.signal = True
        ek = o.sem if o.is_dma else o.eng
        for k in r:
            self.readers.setdefault(k, {})[ek] = o
        for k in w:
            self.last_w[k] = o
            self.readers[k] = {}
        self.ops.append(o)
        return o

    def op(self, eng, fn, r=(), w=()):
        if DEBUG_SITES:
            import traceback
            try:
                fn._site = [(f.lineno, f.name) for f in traceback.extract_stack(limit=5)[:-1]]
            except Exception:
                pass
        return self._record(Op(eng, fn, False), r, w)

    def dma(self, eng, out, in_, r=(), w=(), is_out=False, **kw):
        def fn(e):
            return e.dma_start(out=out, in_=in_, **kw)
        o = Op(eng, fn, True)
        slot = self.dma_rr[eng]
        self.dma_rr[eng] = (slot + 1) % self.n_dma_sems
        key = (eng, slot)
        o.sem = key
        self.dma_cnt[key] = self.dma_cnt.get(key, 0) + 16
        o.cnt = self.dma_cnt[key]
        o.prev_same_sem = self.dma_last.get(key)
        self.dma_last[key] = o
        o.signal = True
        self._record(o, r, w)
        if is_out:
            self.out_dmas.append(o)
        return o

    def fence(self):
        last = {}
        for o in self.ops:
            k = o.sem if o.is_dma else o.eng
            last[k] = o
        self.fence_deps = list(last.values())
        for d in self.fence_deps:
            d.signal = True

    def emit(self):
        nc = self.nc
        with ExitStack() as st:
            esem = {e: st.enter_context(nc.semaphore("s_" + e)) for e in self.ENGS}
            dsem = {}
            for key in self.dma_cnt:
                dsem[key] = st.enter_context(nc.semaphore("d_%s_%d" % key))
            cnt = {e: 0 for e in self.ENGS}
            for o in self.ops:
                if not o.is_dma and o.signal:
                    cnt[o.eng] += 1
                    o.seq = cnt[o.eng]
            self.stats = dict(cnt)
            fin = Op("sp", None, False)
            fin.deps = set(self.out_dmas)
            streams = {e: [o for o in self.ops if o.eng == e] for e in self.ENGS}
            streams["sp"].append(fin)
            block = st.enter_context(nc.Block())

            def run(e_name, eng):
                waited = {}
                for o in streams[e_name]:
                    need = {}
                    deps = list(o.deps)
                    if o.is_dma and o.prev_same_sem is not None:
                        deps.append(o.prev_same_sem)
                    for d in deps:
                        if d.is_dma:
                            s, v, sk = dsem[d.sem], d.cnt, d.sem
                        else:
                            s, v, sk = esem[d.eng], d.seq, d.eng
                        if waited.get(sk, 0) >= v:
                            continue
                        if sk not in need or need[sk][1] < v:
                            need[sk] = (s, v)
                    for sk, (s, v) in need.items():
                        eng.wait_ge(s, v)
                        waited[sk] = v
                    if o.fn is None:
                        continue
                    try:
                        ins = o.fn(eng)
                    except Exception:
                        print("EMIT FAILURE at op recorded from:", getattr(o.fn, "_site", None))
                        raise
                    if o.is_dma:
                        ins.then_inc(dsem[o.sem], 16)
                    elif o.signal:
                        ins.then_inc(esem[o.eng], 1)

            @block.tensor
            def _(eng):
                run("pe", eng)

            @block.scalar
            def _(eng):
                run("act", eng)

            @block.vector
            def _(eng):
                run("dve", eng)

            @block.gpsimd
            def _(eng):
                run("pool", eng)

            @block.sync
            def _(eng):
                run("sp", eng)


class Arena:
    def __init__(self, t, nbytes):
        self.t = t
        self.n = nbytes
        self.off = 0

    def alloc(self, free_shape, dt=F32, parts=128):
        n = 1
        for s in free_shape:
            n *= s
        isz = 4 if dt == F32 else 2
        sz = (n * isz + 31) // 32 * 32
        assert self.off + sz <= self.n, ("arena overflow", self.off, sz, self.n)
        v = self.t[:, self.off // 4:(self.off + sz) // 4]
        if dt != F32:
            v = v.bitcast(dt)
        v = v[:, 0:n]
        if len(free_shape) == 2:
            v = v.rearrange("p (a b) -> p a b", a=free_shape[0])
        elif len(free_shape) == 3:
            v = v.rearrange("p (a b c) -> p a b c", a=free_shape[0], b=free_shape[1])
        self.off += sz
        self.hw = max(getattr(self, 'hw', 0), self.off)
        return v[0:parts] if parts != 128 else v

    def mark(self):
        return self.off

    def release(self, m):
        self.off = m

    def phase(self, name):
        self.peaks = getattr(self, "peaks", [])
        self.peaks.append((name, getattr(self, "hw", 0)))
        self.hw = self.off


def ck(name, a, b):
    a = max(a, 0)
    return [(name, i) for i in range(a // 128, (b - 1) // 128 + 1)]


def run_pool(make_gen, n_items, width):
    free = list(range(width))
    active = []
    nxt = 0
    while nxt < n_items or active:
        while free and nxt < n_items:
            g = free.pop(0)
            active.append((make_gen(nxt, g), g))
            nxt += 1
        still = []
        for gen, g in active:
            try:
                next(gen)
                still.append((gen, g))
            except StopIteration:
                free.append(g)
        active = still


def run_interleaved(gens):
    gens = list(gens)
    while gens:
        nxt = []
        for g in gens:
            try:
                next(g)
                nxt.append(g)
            except StopIteration:
                pass
        gens = nxt


def build(stage=99):
    nc = bass.Bass("TRN2", target_bir_lowering=False)

    def din(name, shape):
        return nc.dram_tensor(name, list(shape), F32, kind="ExternalInput").ap()

    def dout(name, shape):
        return nc.dram_tensor(name, list(shape), F32, kind="ExternalOutput").ap()

    xp = din("xp", [2048, D]); xs = din("xs", [NS, D]); meta = din("meta", [16, D])
    sdn = din("sdn", [2, NS, 4, 128, 128]); sdnc = din("sdnc", [2, NS, 3, 1536])
    slru = din("slru", [2, NS, 256]); slruc = din("slruc", [2, NS, 3, 256])
    sret = din("sret", [2, NS, 4, 64, 64])
    w1i = din("w1i", [2, D, 2 * DFF]); w1o = din("w1o", [2, DFF, D])
    w2i = din("w2i", [2, D, 2 * DFF]); w2o = din("w2o", [2, DFF, D])
    win = din("win", [2, D, DIN]); wout = din("wout", [2, D, D])
    lwa = din("lwa", [2, 4, 64, 64]); lwx = din("lwx", [2, 4, 64, 64])
    pvec = din("pvec", [186, 128]); abv = din("abv", [1, 16])
    c_ident = din("c_ident", [128, 128]); c_ut = din("c_ut", [128, 128]); c_mask = din("c_mask", [128, 128])
    c_retdec = din("c_retdec", [4, 128, 128]); c_reteg = din("c_reteg", [64, 4, 128])
    c_gamc = din("c_gamc", [128, 8])
    c_retkd = din("c_retkd", [128, 8]); c_cos = din("c_cos", [64, NT]); c_sin = din("c_sin", [64, NT])

    yp = dout("yp", [2048, D]); ys = dout("ys", [NS, D])
    o_dn_p = dout("dn_p", [2, 4, 128, 128]); o_dnc_p = dout("dnc_p", [2, 3, 1536])
    o_lru_p = dout("lru_p", [2, 256]); o_lruc_p = dout("lruc_p", [2, 3, 256])
    o_ret_p = dout("ret_p", [2, 4, 64, 64])
    o_dn_s = dout("dn_s", [2, NS, 4, 128, 128]); o_dnc_s = dout("dnc_s", [2, NS, 3, 1536])
    o_lru_s = dout("lru_s", [2, NS, 256]); o_lruc_s = dout("lruc_s", [2, NS, 3, 256])
    o_ret_s = dout("ret_s", [2, NS, 4, 64, 64])

    P = Prog(nc)
    RR_BYTES = 34 * 1024
    ARENA_BYTES = 206 * 1024 - RR_BYTES
    with ExitStack() as st:
        arena_t = st.enter_context(nc.sbuf_tensor("arena", [128, ARENA_BYTES // 4], F32))
        A = Arena(arena_t, ARENA_BYTES)
        rr_t = st.enter_context(nc.sbuf_tensor("rrarena", [128, RR_BYTES // 4], F32))
        AR = Arena(rr_t, RR_BYTES)
        psb = [st.enter_context(nc.psum_tensor("psb%d" % i, [128, 512], F32)) for i in range(8)]
        st_ps = {"b": 0, "q": 0}

        def PB():
            i = st_ps["b"]
            st_ps["b"] = (i + 1) % 4
            return psb[i], [("psb", i)]

        def PQ():
            i = st_ps["q"]
            st_ps["q"] = (i + 1) % 4
            return psb[4 + i][:, 0:128], [("psb", 4 + i)]

        def mm(out, lhsT, rhs, r, w, start=True, stop=True):
            P.op("pe", lambda e: e.matmul(out, lhsT, rhs, start=start, stop=stop), r, w)

        def tr(out, in_, idn, r, w):
            P.op("pe", lambda e: e.transpose(out, in_, idn), r, w)

        def act(out, in_, func, r, w, scale=1.0, bias=0.0):
            P.op("act", lambda e: e.activation(out, in_, func, bias=bias, scale=scale), r, w)

        def tt(out, a, b, op, r, w, eng="dve"):
            P.op(eng, lambda e: e.tensor_tensor(out, a, b, op), r, w)

        def ts(out, a, s1, s2, op0, op1, r, w, eng="dve"):
            if op1 == ALU.pow and s2 == -0.5:
                assert op0 == ALU.add
                P.op("act", lambda e: e.activation(out, a, AF.Ln, bias=epsc[0:out.shape[0], :], scale=1.0), list(r) + ["epsc"], w)
                P.op("act", lambda e: e.activation(out, out, AF.Exp, scale=-0.5), w, w)
                return
            if op1 == ALU.pow and s2 == 0.5:
                assert op0 == ALU.max
                P.op("dve", lambda e: e.tensor_scalar(out, a, 1e-18, None, ALU.max), r, w)
                P.op("act", lambda e: e.activation(out, out, AF.Ln), w, w)
                P.op("act", lambda e: e.activation(out, out, AF.Exp, scale=0.5), w, w)
                return
            if s2 is None:
                P.op(eng, lambda e: e.tensor_scalar(out, a, s1, None, op0), r, w)
            else:
                P.op(eng, lambda e: e.tensor_scalar(out, a, s1, s2, op0, op1), r, w)

        def stt(out, a, s, b, op0, op1, r, w, eng="dve"):
            P.op(eng, lambda e: e.scalar_tensor_tensor(out, a, s, b, op0, op1), r, w)

        def cp(out, in_, r, w, eng="dve"):
            if eng == "act":
                P.op("act", lambda e: e.copy(out, in_), r, w)
            else:
                P.op(eng, lambda e: e.tensor_copy(out, in_), r, w)

        def memset(ap, val, w, eng="dve"):
            P.op(eng, lambda e: e.memset(ap, val), (), w)

        ident = A.alloc([128]); ut = A.alloc([128]); maskT = A.alloc([128])
        ones_f = A.alloc([128]); onesm = A.alloc([128], BF16); ones128b = A.alloc([128], BF16)
        ones1b = A.alloc([128], BF16); o64 = A.alloc([64], parts=64)
        pcol = A.alloc([186]); abb = A.alloc([16]); nA = A.alloc([8])
        retkd = A.alloc([8])
        epsc = A.alloc([1])
        xT = A.alloc([8, NT])
        P.dma("sp", ident, c_ident, w=["ident"])
        P.dma("sp", ut, c_ut, w=["ut"])
        P.dma("sp", maskT, c_mask, w=["maskT"])
        P.dma("sp", retkd, c_retkd, w=["retkd"])
        P.dma("sp", abb, abv.partition_broadcast(128), w=["abb"])
        memset(ones_f, 1.0, ["ones_f"]); memset(onesm, 1.0 / 1024, ["onesm"])
        memset(ones128b, 1.0 / 128, ["ones128b"]); memset(ones1b, 1.0, ["ones1b"])
        memset(o64, 1.0 / 64, ["o64"])
        memset(epsc, EPS, ["epsc"])
        act(nA, abb[:, 0:8], AF.Exp, ["abb"], ["nA"])
        ts(nA, nA, -1.0, None, ALU.mult, None, ["nA"], ["nA"])

        m0 = A.mark()
        stg = A.alloc([2, 1024])
        for (r0, nr) in ((0, 128), (128, 58)):
            P.dma("sp", stg[0:nr, 0, 0:128], pvec[r0:r0 + nr, :], w=[("stg", 0)])
            pq, kq = PQ()
            tr(pq[:, 0:nr], stg[0:nr, 0, 0:128], ident[0:nr, 0:nr], [("stg", 0), "ident"], kq)
            cp(pcol[:, r0:r0 + nr], pq[:, 0:nr], kq, ["pcol"])

        def gain(v, c):
            return pcol[:, v * 8 + c:v * 8 + c + 1]

        srcs = [(xp[i * 128:(i + 1) * 128, :], 128, 16 + 128 * i) for i in range(16)]
        srcs += [(meta[:, :], 16, 0), (xs[:, :], NS, NP)]
        for i, (src, n, col) in enumerate(srcs):
            sl = i % 2
            P.dma("sp", stg[0:n, sl, :], src, w=[("stg", sl)])
            for half in range(2):
                pb, kb = PB()
                for q in range(4):
                    c = half * 4 + q
                    tr(pb[:, q * 128:q * 128 + n], stg[0:n, sl, c * 128:(c + 1) * 128], ident[0:n, 0:n],
                       [("stg", sl), "ident"], kb)
                src_v = pb.rearrange("p (q t) -> p q t", q=4)[:, :, 0:n]
                cp(xT[:, half * 4:half * 4 + 4, col:col + n], src_v, kb, ck("xT", col, col + n),
                   eng=("act" if half else "dve"))
        A.release(m0)
        P.fence()

        def rmsnorm(v, xn):
            m_ = A.mark()
            sq = A.alloc([8, 512], BF16)
            rstd = A.alloc([512])
            for ti, (a, n) in enumerate(TT):
                kx = ck("xT", a, a + n)
                for c in range(8):
                    act(sq[:, c, 0:n], xT[:, c, a:a + n], AF.Square, kx, [("sq", c)])
                pb, kb = PB()
                for c in range(8):
                    mm(pb[:, 0:n], onesm, sq[:, c, 0:n], [("sq", c), "onesm"], kb, start=(c == 0), stop=(c == 7))
                ts(rstd[:, 0:n], pb[:, 0:n], EPS, -0.5, ALU.add, ALU.pow, kb, ["rstd"])
                for c in range(8):
                    stt(xn[:, c, a:a + n], xT[:, c, a:a + n], gain(v, c), rstd[:, 0:n], ALU.mult, ALU.mult,
                        kx + ["rstd", "pcol"], [("xn", ti)])
            A.release(m_)
            P.fence()

        def ffn(l, wi, wo_d, xn):
            m = A.mark()
            hs = A.alloc([6, NT], BF16)
            wg = A.alloc([2, 8, 256], BF16); wu = A.alloc([2, 8, 256], BF16)
            wo = A.alloc([6, 1024], BF16)
            sg = A.alloc([2, 512])
            wiv = wi[l].rearrange("(c p) n -> p c n", p=128)
            slabs = [(0, 6), (6, 12), (12, 18), (18, 22)]
            pair_i = 0
            sgi = 0
            for (j0, j1) in slabs:
                npairs = (j1 - j0) // 2
                for pi in range(npairs):
                    j = j0 + 2 * pi
                    sl = pair_i % 2
                    pair_i += 1
                    P.dma("pool", wg[:, sl], wiv[:, :, j * 128:j * 128 + 256], w=[("wg", sl)])
                    P.dma("pool", wu[:, sl], wiv[:, :, DFF + j * 128:DFF + j * 128 + 256], w=[("wu", sl)])
                    if pi == 0:
                        for jl in range(j1 - j0):
                            P.dma("pool", wo[:, jl, :], wo_d[l, (j0 + jl) * 128:(j0 + jl + 1) * 128, :], w=[("wo", jl)])
                    for jj in range(2):
                        jl = 2 * pi + jj
                        for ti, (a, n) in enumerate(TT):
                            pg, kg = PB()
                            for c in range(8):
                                mm(pg[:, 0:n], wg[:, sl, c, jj * 128:(jj + 1) * 128], xn[:, c, a:a + n],
                                   [("wg", sl), ("xn", ti)], kg, start=(c == 0), stop=(c == 7))
                            pu, ku = PB()
                            for c in range(8):
                                mm(pu[:, 0:n], wu[:, sl, c, jj * 128:(jj + 1) * 128], xn[:, c, a:a + n],
                                   [("wu", sl), ("xn", ti)], ku, start=(c == 0), stop=(c == 7))
                            s = sgi % 2
                            sgi += 1
                            act(sg[:, s, 0:n], pg[:, 0:n], AF.Silu, kg, [("sg", s)])
                            tt(hs[:, jl, a:a + n], sg[:, s, 0:n], pu[:, 0:n], ALU.mult, ku + [("sg", s)], [("hs", jl, ti)])
                nj = j1 - j0
                for ti, (a, n) in enumerate(TT):
                    for dc in range(8):
                        pb, kb = PB()
                        for jl in range(nj):
                            mm(pb[:, 0:n], wo[:, jl, dc * 128:(dc + 1) * 128], hs[:, jl, a:a + n],
                               [("wo", jl), ("hs", jl, ti)], kb, start=(jl == 0), stop=(jl == nj - 1))
                        kx = ck("xT", a, a + n)
                        stt(xT[:, dc, a:a + n], pb[:, 0:n], 0.5, xT[:, dc, a:a + n], ALU.mult, ALU.add, kb + kx, kx)
            A.phase("ffn")
            A.release(m)
            P.fence()

        def store_T(src, n_p, n_f, dst, r, tmp, ktmp):
            pq, kq = PQ()
            tr(pq[0:n_f, 0:n_p], src, ident[0:n_p, 0:n_p], list(r) + ["ident"], kq)
            cp(tmp[0:n_f, 0:n_p], pq[0:n_f, 0:n_p], kq, ktmp, eng="act")
            P.dma("sp", dst, tmp[0:n_f, 0:n_p], r=ktmp, is_out=True)

        def mixer(l, xn):
            m = A.mark()
            oT = A.alloc([4, NT], BF16)
            wring = A.alloc([2, 8, 128], BF16)
            pre = A.alloc([2, 515])
            tmpc = A.alloc([2, 512])
            sqb = A.alloc([2, 512], BF16)
            rnb = A.alloc([1, 512])
            stT = A.alloc([2, 128])
            wiv = win[l].rearrange("(c p) n -> p c n", p=128)
            cnt = {"w": 0, "pre": 0, "tmp": 0, "st": 0}

            def load_w(col0, ncols):
                sl = cnt["w"] % 2
                cnt["w"] += 1
                P.dma("pool", wring[:, sl, :, 0:ncols], wiv[:, :, col0:col0 + ncols], w=[("wr", sl)])
                return sl

            def proj_tile(sl, ncols, ti):
                a, n = TT[ti]
                pb, kb = PB()
                for c in range(8):
                    mm(pb[0:ncols, 0:n], wring[:, sl, c, 0:ncols], xn[:, c, a:a + n], [("wr", sl), ("xn", ti)], kb,
                       start=(c == 0), stop=(c == 7))
                return pb, kb

            def stsl():
                s = cnt["st"] % 2
                cnt["st"] += 1
                return stT[:, s, :], [("stT", s)]

            def apply_wout(row0s, kparts):
                npc = len(row0s)
                mw_ = A.mark()
                womix = A.alloc([4, 1024], BF16)
                for i, r0 in enumerate(row0s):
                    P.dma("pool", womix[0:kparts, i, :], wout[l, r0:r0 + kparts, :], w=[("womix", i)])
                for ti, (a, n) in enumerate(TT):
                    for dc in range(8):
                        pb, kb = PB()
                        for i in range(npc):
                            mm(pb[:, 0:n], womix[0:kparts, i, dc * 128:(dc + 1) * 128], oT[0:kparts, i, a:a + n],
                               [("womix", i)] + ck(("oT", i), a, a + n), kb, start=(i == 0), stop=(i == npc - 1))
                        kx = ck("xT", a, a + n)
                        stt(xT[:, dc, a:a + n], pb[:, 0:n], 1.0, xT[:, dc, a:a + n], ALU.mult, ALU.add, kb + kx, kx)
                A.release(mw_)

            def conv_chunk(col0, wrow, buf3, dstf, nparts=128, bias=None, post=None):
                sl = load_w(col0, nparts)
                last_ps = None
                pbs = {0: proj_tile(sl, nparts, 0)}
                pres = {}

                def evac(ti):
                    a, n = TT[ti]
                    pb, kb = pbs.pop(ti)
                    ps_ = cnt["pre"] % 2
                    cnt["pre"] += 1
                    kp = [("pre", ps_)]
                    if ti == 0:
                        memset(pre[0:nparts, ps_, 0:3], 0.0, kp)
                    else:
                        cp(pre[0:nparts, ps_, 0:3], pre[0:nparts, 1 - ps_, TW:TW + 3], [("pre", 1 - ps_)], kp)
                    cp(pre[0:nparts, ps_, 3:3 + n], pb[0:nparts, 0:n], kb, kp, eng="act")
                    pres[ti] = ps_

                if len(TT) > 1:
                    pbs[1] = proj_tile(sl, nparts, 1)
                evac(0)
                for ti, (a, n) in enumerate(TT):
                    if ti + 2 < len(TT):
                        pbs[ti + 2] = proj_tile(sl, nparts, ti + 2)
                    if ti + 1 < len(TT):
                        evac(ti + 1)
                    ps_ = pres.pop(ti)
                    kp = [("pre", ps_)]
                    npr = n if ti < 4 else SO
                    tsl = cnt["tmp"] % 2
                    cnt["tmp"] += 1
                    kt = [("tmpc", tsl)]
                    t_ = tmpc[0:nparts, tsl, :]
                    ts(t_[:, 0:npr], pre[0:nparts, ps_, 3:3 + npr], wrow(3)[0:nparts], None, ALU.mult, None, kp + ["pcol"], kt)
                    for j in (2, 1, 0):
                        stt(t_[:, 0:npr], pre[0:nparts, ps_, j:j + npr], wrow(j)[0:nparts], t_[:, 0:npr], ALU.mult, ALU.add,
                            kp + kt + ["pcol"], kt)
                    if ti == 4:
                        ts(t_[:, SO:SO + 16], pre[0:nparts, ps_, 3 + SO:3 + SO + 16], wrow(3)[0:nparts], None, ALU.mult, None, kp + ["pcol"], kt)
                        for j in (2, 1, 0):
                            stt(t_[:, SO:SO + 16], buf3[0:nparts, j, :], wrow(j)[0:nparts], t_[:, SO:SO + 16], ALU.mult, ALU.add,
                                kt + ["buf3", "pcol"], kt)
                    if bias is not None:
                        ts(t_[:, 0:n], t_[:, 0:n], bias[0:nparts], None, ALU.add, None, kt + ["pcol"], kt)
                    dstf(ti, a, n, t_[:, 0:n], kt)
                    last_ps = ps_
                if post is not None:
                    for ti, (a, n) in enumerate(TT):
                        post(ti, a, n)
                return last_ps

            def conv_state_out(ps_, nparts, dst_p, dst_s):
                tmp, kt = stsl()
                store_T(pre[0:nparts, ps_, SO:3 + SO], nparts, 3, dst_p, [("pre", ps_)], tmp, kt)
                tmp, kt = stsl()
                store_T(pre[0:nparts, ps_, 3 + SO:3 + SO + 16], nparts, 16, dst_s, [("pre", ps_)], tmp, kt)

            cst48 = A.alloc([128])

            def load_bufT(src48, ncol0, nparts, buf, kbuf):
                P.dma("sp", cst48[0:48, 0:nparts], src48.rearrange("s j n -> (s j) n")[:, ncol0:ncol0 + nparts], w=["cst48"])
                pq, kq = PQ()
                tr(pq[0:nparts, 0:48], cst48[0:48, 0:nparts], ident[0:48, 0:48], ["cst48", "ident"], kq)
                cp(buf[0:nparts].rearrange("p j s -> p s j"), pq[0:nparts, 0:48].rearrange("p (s j) -> p s j", j=3), kq, kbuf)

            mdn = A.mark()
            gtm = A.alloc([18, 4]); btm = A.alloc([18, 4]); Gc = A.alloc([18, 4])
            wab = A.alloc([8, 8], BF16)
            P.dma("sp", o_dnc_s[l, :, 0:2, :], sdnc[l, :, 1:3, :], is_out=True)
            P.dma("pool", wab, wiv[:, :, 1536:1544], w=["wab"])
            blocks = CH + [(NP, NS)]
            memset(gtm, 0.0, ["gtm"]); memset(btm, 0.0, ["btm"])
            for bi, (t0, C) in enumerate(blocks):
                pq, kq = PQ()
                for c in range(8):
                    mm(pq[0:C, 0:8], xn[:, c, t0:t0 + C], wab[:, c, :], ["wab"] + [("xn", i) for i in range(5)], kq,
                       start=(c == 0), stop=(c == 7))
                tt(gtm[0:C, bi, :], pq[0:C, 0:4], abb[0:C, 8 + 4 * l:12 + 4 * l], ALU.add, kq + ["abb"], ["gtm"])
                cp(btm[0:C, bi, :], pq[0:C, 4:8], kq, ["btm"])
            act(gtm, gtm, AF.Exp, ["gtm"], ["gtm"])
            act(gtm, gtm, AF.Ln, ["gtm"], ["gtm"], bias=1.0)
            tt(gtm, gtm, nA[:, 4 * l:4 * l + 4].unsqueeze(1).to_broadcast([128, 18, 4]), ALU.mult, ["gtm", "nA"], ["gtm"])
            act(btm, btm, AF.Sigmoid, ["btm"], ["btm"])
            for bi, (t0, C) in enumerate(blocks):
                pq2, kq2 = PQ()
                mm(pq2[0:C, 0:4], ut[0:C, 0:C], gtm[0:C, bi, :], ["ut", "gtm"], kq2)
                cp(Gc[0:C, bi, :], pq2[0:C, 0:4], kq2, ["Gc"])

            for h in range(4):
                mh = A.mark()
                mrr = AR.mark()
                qT = AR.alloc([NT]); kT = AR.alloc([NT]); vT = A.alloc([NT])
                buf3 = A.alloc([3, 16])
                S = A.alloc([128])
                memset(S, 0.0, ["S"])

                def mk_dst(dst, name, l2scale):
                    def f(ti, a, n, src, kt):
                        kd_ = ck(name, a, a + n)
                        act((dst if l2scale is not None else dst)[:, a:a + n], src, AF.Silu, kt, kd_)
                    return f

                def mk_post(dst, name, l2scale):
                    held = {}

                    def stage1(ti):
                        a, n = TT[ti]
                        kd_ = ck(name, a, a + n)
                        s2 = ti % 2
                        act(sqb[:, s2, 0:n], dst[:, a:a + n], AF.Square, kd_, [("sqb", s2)])
                        pb, kb = PB()
                        mm(pb[:, 0:n], ones1b, sqb[:, s2, 0:n], [("sqb", s2), "ones1b"], kb)
                        held[ti] = (pb, kb)

                    def f(ti, a, n):
                        kd_ = ck(name, a, a + n)
                        if ti == 0:
                            stage1(0)
                        if ti + 1 < len(TT):
                            stage1(ti + 1)
                        pb, kb = held.pop(ti)
                        ts(rnb[:, 0, 0:n], pb[:, 0:n], EPS, -0.5, ALU.add, ALU.pow, kb, [("rnb", 0)])
                        stt(dst[:, a:a + n], dst[:, a:a + n], l2scale, rnb[:, 0, 0:n], ALU.mult, ALU.mult,
                            kd_ + [("rnb", 0)], kd_)
                    return f

                for (which, dst, name, sc) in ((0, qT, "qT", 128.0 ** -0.5), (1, kT, "kT", 1.0), (2, vT, "vT", None)):
                    chn = which * 4 + h
                    load_bufT(sdnc[l], chn * 128, 128, buf3, ["buf3"])
                    wrow = (lambda j, chn=chn: pcol[:, 56 + (l * 4 + j) * 12 + chn:56 + (l * 4 + j) * 12 + chn + 1])
                    ps_ = conv_chunk(chn * 128, wrow, buf3, mk_dst(dst, name, sc),
                                     post=(mk_post(dst, name, sc) if sc is not None else None))
                    conv_state_out(ps_, 128, o_dnc_p[l, :, chn * 128:(chn + 1) * 128], o_dnc_s[l, :, 2, chn * 128:(chn + 1) * 128])

                nwcol = pcol[:, 184 + l:185 + l]

                def dn_post(o_src, ko, t0, C):
                    osb = A_t["osb"][:, 0:C]
                    cp(osb, o_src, ko, ["osb"], eng="act")
                    sq_ = A_t["osq"][:, 0:C]
                    act(sq_, o_src, AF.Square, ko, ["osq"])
                    yield
                    pq, kq = psb[3][:, 0:128], [("psb", 3)]
                    mm(pq[:, 0:C], ones128b, sq_, ["osq", "ones128b"], kq)
                    yield
                    rn_ = A_t["orn"][:, 0:C]
                    ts(rn_, pq[:, 0:C], EPS, -0.5, ALU.add, ALU.pow, kq, ["orn"])
                    yield
                    stt(osb, osb, nwcol, rn_, ALU.mult, ALU.mult, ["osb", "orn", "pcol"], ["osb"])
                    yield
                    cp(oT[:, h, t0:t0 + C], osb, ["osb"], ck(("oT", h), t0, t0 + C))

                A_t = {"osb": A.alloc([128]), "osq": A.alloc([128], BF16), "orn": A.alloc([128])}
                NG = 3
                sets = []
                names = ("gUT", "Gb", "E", "dec", "eGb", "qp", "cols", "kg", "kd", "vtm", "N", "MT", "NT", "R",
                         "Pa", "PTa", "Pb", "PTb")
                RRN = ("kg", "kd", "vtm", "N", "MT", "NT", "R", "Pa", "PTa", "Pb", "PTb")
                S0s = A.alloc([16, 128])
                P.dma("sp", S0s, sdn[l, :, h].rearrange("s d e -> d s e"), w=["S0s"])
                sets.append({k: (AR.alloc([128]) if k in RRN else A.alloc([128])) for k in names})
                msets = A.mark()
                for g in range(1, NG):
                    sets.append({k: (AR.alloc([128]) if k in RRN else A.alloc([128])) for k in names})
                for T_ in sets:
                    T_["nW"], T_["U"] = T_["gUT"], T_["NT"]
                ALIAS = {"nW": "gUT", "U": "NT"}

                def prep(ci, g):
                    t0, C = CH[ci]
                    T_ = sets[g]
                    K = lambda n_: [("set", g, ALIAS.get(n_, n_))]
                    RR = lambda n_: T_[n_]
                    bank = psb[4 + g]
                    kbk = [("psb", 4 + g)]
                    kk = ck("kT", t0, t0 + C); kqk = ck("qT", t0, t0 + C); kv = ck("vT", t0, t0 + C)
                    gcol = gtm[0:C, ci, h:h + 1]; bcol = btm[0:C, ci, h:h + 1]; Gcol = Gc[0:C, ci, h:h + 1]
                    kTr = kT; qTr = qT
                    ts(T_["gUT"][0:C, 0:C], ut[0:C, 0:C], gcol, None, ALU.mult, None, ["ut", "gtm"], K("gUT"))
                    mm(bank[:, 0:C], ones_f[0:C, 0:128], T_["gUT"][0:C, 0:C], K("gUT") + ["ones_f"], kbk)
                    yield
                    cp(T_["Gb"][:, 0:C], bank[:, 0:C], kbk, K("Gb"))
                    stt(T_["E"][0:C, 0:C], bank[0:C, 0:C], Gcol, maskT[0:C, 0:C], ALU.subtract, ALU.add,
                        kbk + ["Gc", "maskT"], K("E"))
                    yield
                    act(T_["dec"][0:C, 0:C], T_["E"][0:C, 0:C], AF.Exp, K("E"), K("dec"))
                    act(T_["eGb"][:, 0:C], T_["Gb"][:, 0:C], AF.Exp, K("Gb"), K("eGb"))
                    cols = T_["cols"]
                    act(cols[0:C, 0:1], Gcol, AF.Exp, ["Gc"], K("cols"))
                    act(cols[0:C, 1:2], Gcol, AF.Exp, ["Gc"] + K("Gb"), K("cols"), scale=-1.0, bias=T_["Gb"][0:C, C - 1:C])
                    tr(bank[0:C, 0:128], kT[:, t0:t0 + C], ident, kk + ["ident"], kbk)
                    tr(bank[0:C, 128:256], vT[:, t0:t0 + C], ident, kv + ["ident"], kbk)
                    mm(bank[0:C, 256:256 + C], kTr[:, t0:t0 + C], kTr[:, t0:t0 + C], kk, kbk)
                    mm(bank[0:C, 384:384 + C], kTr[:, t0:t0 + C], qTr[:, t0:t0 + C], kk + kqk, kbk)
                    yield
                    ts(RR("kg")[0:C, :], bank[0:C, 0:128], cols[0:C, 0:1], None, ALU.mult, None, kbk + K("cols"), K("kg"))
                    ts(RR("kd")[0:C, :], bank[0:C, 0:128], cols[0:C, 1:2], None, ALU.mult, None, kbk + K("cols"), K("kd"))
                    cp(RR("vtm")[0:C, :], bank[0:C, 128:256], kbk, K("vtm"))
                    stt(RR("N")[0:C, 0:C], bank[0:C, 256:256 + C], bcol, T_["dec"][0:C, 0:C], ALU.mult, ALU.mult,
                        kbk + ["btm"] + K("dec"), K("N"))
                    tt(T_["E"][0:C, 0:C], T_["dec"][0:C, 0:C], ident[0:C, 0:C], ALU.add, K("dec") + ["ident"], K("E"), eng="pool")
                    tt(T_["qp"][:, 0:C], qT[:, t0:t0 + C], T_["eGb"][:, 0:C], ALU.mult, kqk + K("eGb"), K("qp"), eng="pool")
                    yield
                    tt(RR("MT")[0:C, 0:C], bank[0:C, 384:384 + C], T_["E"][0:C, 0:C], ALU.mult, kbk + K("E"), K("MT"))
                    tt(RR("R")[0:C, 0:C], ident[0:C, 0:C], T_["N"][0:C, 0:C], ALU.subtract, K("N") + ["ident"], K("R"))
                    yield
                    tr(bank[0:C, 0:C], T_["N"][0:C, 0:C], ident[0:C, 0:C], K("N") + ["ident"], kbk)
                    yield
                    cp(RR("NT")[0:C, 0:C], bank[0:C, 0:C], kbk, K("NT"))
                    yield
                    J = 7 if C == 128 else 4
                    Pc, PTc, kPc, kPTc = "N", "NT", K("N"), K("NT")
                    for j in range(1, J):
                        Pn, PTn = ("Pa", "PTa") if j % 2 else ("Pb", "PTb")
                        kPn, kPTn = K(Pn), K(PTn)
                        mm(bank[0:C, 0:C], RR(Pc)[0:C, 0:C], RR(PTc)[0:C, 0:C], kPc + kPTc, kbk)
                        if j < J - 1:
                            mm(bank[0:C, 128:128 + C], RR(PTc)[0:C, 0:C], RR(Pc)[0:C, 0:C], kPc + kPTc, kbk)
                        if j >= 2:
                            mm(bank[0:C, 256:256 + C], RR(PTc)[0:C, 0:C], RR("R")[0:C, 0:C], kPTc + K("R"), kbk)
                        yield
                        cp(RR(PTn)[0:C, 0:C], bank[0:C, 0:C], kbk, kPTn)
                        if j < J - 1:
                            cp(RR(Pn)[0:C, 0:C], bank[0:C, 128:128 + C], kbk, kPn)
                        if j >= 2:
                            tt(RR("R")[0:C, 0:C], T_["R"][0:C, 0:C], bank[0:C, 256:256 + C], ALU.add, K("R") + kbk, K("R"))
                        Pc, PTc, kPc, kPTc = Pn, PTn, kPn, kPTn
                        yield
                    mm(bank[0:C, 0:C], RR(PTc)[0:C, 0:C], RR("R")[0:C, 0:C], kPTc + K("R"), kbk)
                    yield
                    tt(RR("R")[0:C, 0:C], T_["R"][0:C, 0:C], bank[0:C, 0:C], ALU.add, K("R") + kbk, K("R"))
                    yield
                    mm(bank[:, 0:C], RR("kg")[0:C, :], RR("R")[0:C, 0:C], K("kg") + K("R"), kbk)
                    yield
                    ts(T_["nW"][:, 0:C], bank[:, 0:C], -1.0, None, ALU.mult, None, kbk, K("nW"))

                turn = {"i": 0}

                def seq(ci, g):
                    t0, C = CH[ci]
                    T_ = sets[g]
                    K = lambda n_: [("set", g, ALIAS.get(n_, n_))]
                    RR = lambda n_: T_[n_]
                    while turn["i"] != ci:
                        yield
                    pu, kpu = psb[0][:, 0:128], [("psb", 0)]
                    po, kpo = psb[1][:, 0:128], [("psb", 1)]
                    psn, kps = psb[2][:, 0:128], [("psb", 2)]
                    mm(pu[0:C, :], RR("R")[0:C, 0:C], RR("vtm")[0:C, :], K("R") + K("vtm"), kpu, start=True, stop=False)
                    mm(pu[0:C, :], T_["nW"][:, 0:C], S, K("nW") + ["S"], kpu, start=False, stop=True)
                    mm(po[:, 0:C], S, T_["qp"][:, 0:C], ["S"] + K("qp"), kpo, start=True, stop=False)
                    yield
                    ts(RR("U")[0:C, :], pu[0:C, :], btm[0:C, ci, h:h + 1], None, ALU.mult, None, kpu + ["btm"], K("U"))
                    yield
                    mm(po[:, 0:C], RR("U")[0:C, :], RR("MT")[0:C, 0:C], K("U") + K("MT"), kpo, start=False, stop=True)
                    mm(psn, RR("kd")[0:C, :], RR("U")[0:C, :], K("kd") + K("U"), kps)
                    yield
                    stt(S, S, T_["eGb"][:, C - 1:C], psn, ALU.mult, ALU.add, ["S"] + K("eGb") + kps, ["S"])
                    yield from dn_post(po[:, 0:C], kpo, t0, C)
                    turn["i"] = ci + 1

                def chunk_gen(ci, g):
                    yield from prep(ci, g)
                    yield from seq(ci, g)

                run_pool(chunk_gen, 17, NG)
                P.dma("sp", o_dn_p[l, h], S, r=["S"], is_out=True)

                T0 = sets[0]
                K0 = lambda n_: [("set", 0, ALIAS.get(n_, n_))]
                A.release(msets)
                P.fence()
                T0 = dict(T0)
                for nm_ in ("U", "MT", "N", "kg", "kd"):
                    T0[nm_] = A.alloc([128])
                K0 = lambda n_: [("sset", n_)] if n_ in ("U", "MT", "N", "kg", "kd") else [("set", 0, ALIAS.get(n_, n_))]
                sc_ = slice(NP, NT)
                bi = 17
                ksq = ck("qT", NP, NT); ksk = ck("kT", NP, NT); ksv = ck("vT", NP, NT)

                def rowbc(dst, val_col, kval, kdst):
                    ts(T0["gUT"][0:16, 0:16], ident[0:16, 0:16], val_col, None, ALU.mult, None, ["ident"] + kval, K0("gUT"))
                    pq, kq = PQ()
                    mm(pq[:, 0:16], ones_f[0:16, 0:128], T0["gUT"][0:16, 0:16], K0("gUT") + ["ones_f"], kq)
                    cp(dst, pq[:, 0:16], kq, kdst, eng="act")

                act(T0["cols"][0:16, 0:1], gtm[0:16, bi, h:h + 1], AF.Exp, ["gtm"], K0("cols"))
                eGbc = T0["eGb"][:, 0:16]; bbc = T0["Gb"][:, 0:16]
                rowbc(eGbc, T0["cols"][0:16, 0:1], K0("cols"), K0("eGb"))
                rowbc(bbc, btm[0:16, bi, h:h + 1], ["btm"], K0("Gb"))
                kq_ = T0["qp"][:, 0:32].rearrange("p (s two) -> p s two", two=2)
                cp(kq_[:, :, 0], kT[:, sc_], ksk, K0("qp"))
                cp(kq_[:, :, 1], qT[:, sc_], ksq, K0("qp"))
                pks, kpks = PQ()
                for s in range(16):
                    mm(pks[:, 2 * s:2 * s + 2], S0s[:, s, :], kq_[:, s, :], ["S0s"] + K0("qp"), kpks)
                pksv = pks[:, 0:32].rearrange("p (s two) -> p s two", two=2)
                tt(T0["E"][:, 0:16], qT[:, sc_], kT[:, sc_], ALU.mult, ksq + ksk, K0("E"))
                pqk, kpqk = PQ()
                mm(pqk[:, 0:16], ones_f, T0["E"][:, 0:16], K0("E") + ["ones_f"], kpqk)
                t1 = T0["dec"][:, 0:16]
                tt(t1, pksv[:, :, 0], eGbc, ALU.mult, kpks + K0("eGb"), K0("dec"))
                tt(t1, vT[:, sc_], t1, ALU.subtract, ksv + K0("dec"), K0("dec"))
                UTs = T0["U"][:, 0:16]
                tt(UTs, t1, bbc, ALU.mult, K0("dec") + K0("Gb"), K0("U"))
                t3 = T0["MT"][:, 0:16]
                tt(t3, pksv[:, :, 1], eGbc, ALU.mult, kpks + K0("eGb"), K0("MT"))
                t4 = T0["N"][:, 0:16]
                tt(t4, pqk[:, 0:16], UTs, ALU.mult, kpqk + K0("U"), K0("N"))
                tt(t3, t3, t4, ALU.add, K0("MT") + K0("N"), K0("MT"))
                for _ in dn_post(t3, K0("MT"), NP, NS):
                    pass
                pk, kpk = PQ()
                tr(pk[0:16, :], kT[:, sc_], ident, ksk + ["ident"], kpk)
                cp(T0["kg"][0:16, :], pk[0:16, :], kpk, K0("kg"), eng="act")
                pU, kpU = PQ()
                tr(pU[0:16, :], UTs, ident, K0("U") + ["ident"], kpU)
                cp(T0["kd"][0:16, :], pU[0:16, :], kpU, K0("kd"), eng="act")
                kmask = A.alloc([8, 128])
                Sn = A.alloc([8, 128])
                for hf in range(2):
                    tt(kmask[0:16], T0["kg"][0:16, :].unsqueeze(1).to_broadcast([16, 8, 128]),
                       ident[0:16, hf * 8:hf * 8 + 8].unsqueeze(2).to_broadcast([16, 8, 128]), ALU.mult,
                       K0("kg") + ["ident"], ["kmask"])
                    for sg_ in range(2):
                        pb, kb = PB()
                        for s4 in range(4):
                            s8 = sg_ * 4 + s4
                            mm(pb[:, s4 * 128:(s4 + 1) * 128], kmask[0:16, s8, :], T0["kd"][0:16, :], ["kmask"] + K0("kd"), kb)
                        for s4 in range(4):
                            s8 = sg_ * 4 + s4
                            s = hf * 8 + s8
                            stt(Sn[:, s8, :], S0s[:, s, :], eGbc[:, s:s + 1], pb[:, s4 * 128:(s4 + 1) * 128], ALU.mult, ALU.add,
                                ["S0s"] + K0("eGb") + kb, [("Sn", sg_)])
                    P.dma("sp", o_dn_s[l, hf * 8:hf * 8 + 8, h].rearrange("s d e -> d s e"), Sn, r=[("Sn", 0), ("Sn", 1)], is_out=True)
                sl = load_w(1544 + h * 128, 128)
                for ti, (a, n) in enumerate(TT):
                    pb, kb = proj_tile(sl, 128, ti)
                    s2 = ti % 2
                    act(tmpc[:, s2, 0:n], pb[:, 0:n], AF.Silu, kb, [("tmpc", s2)])
                    ko = ck(("oT", h), a, a + n)
                    tt(oT[:, h, a:a + n], oT[:, h, a:a + n], tmpc[:, s2, 0:n], ALU.mult, ko + [("tmpc", s2)], ko)
                A.release(mh)
                AR.release(mrr)
                P.fence()
            apply_wout([0, 128, 256, 384], 128)
            A.phase("dn")
            A.release(mdn)
            P.fence()
            if stage < 3:
                A.release(m)
                return

            mlru = A.mark()
            P.dma("sp", o_lruc_s[l, :, 0:2, :], slruc[l, :, 1:3, :], is_out=True)
            hst = A.alloc([256])
            P.dma("sp", hst[0:16, :], slru[l], w=["hst"])
            bda = A.alloc([2, 128]); bdx = A.alloc([2, 128])
            memset(bda, 0.0, ["bda"]); memset(bdx, 0.0, ["bdx"])
            for c in range(2):
                for b2 in range(2):
                    nb = c * 2 + b2
                    P.dma("sp", bda[b2 * 64:(b2 + 1) * 64, c, b2 * 64:(b2 + 1) * 64], lwa[l, nb], r=["bda"], w=["bda"])
                    P.dma("sp", bdx[b2 * 64:(b2 + 1) * 64, c, b2 * 64:(b2 + 1) * 64], lwx[l, nb], r=["bdx"], w=["bdx"])
            cA = A.alloc([2])
            act(cA, pcol[:, 180 + 2 * l:182 + 2 * l], AF.Exp, ["pcol"], ["cA"], scale=-1.0)
            act(cA, cA, AF.Ln, ["cA"], ["cA"], bias=1.0)
            ts(cA, cA, -8.0, None, ALU.mult, None, ["cA"], ["cA"])
            for c in range(2):
                mc = A.mark()
                xl = A.alloc([NT]); av = A.alloc([NT]); bv = A.alloc([NT]); hv = xl
                buf3 = A.alloc([3, 16]); h0T = A.alloc([16])
                gt = A.alloc([2, 512])
                load_bufT(slruc[l], c * 128, 128, buf3, ["buf3"])
                pq, kq = PQ()
                tr(pq[:, 0:16], hst[0:16, c * 128:(c + 1) * 128], ident[0:16, 0:16], ["hst", "ident"], kq)
                cp(h0T, pq[:, 0:16], kq, ["h0T"])

                def lru_dst(ti, a, n, src, kt):
                    kx = ck("xl", a, a + n)
                    cp(xl[:, a:a + n], src, kt, kx, eng="act")
                    pr, kr = PB()
                    mm(pr[:, 0:n], bda[:, c, :], xl[:, a:a + n], ["bda"] + kx, kr)
                    pi_, ki = PB()
                    mm(pi_[:, 0:n], bdx[:, c, :], xl[:, a:a + n], ["bdx"] + kx, ki)
                    g0 = gt[:, 0, 0:n]; g1 = gt[:, 1, 0:n]
                    act(g0, pr[:, 0:n], AF.Sigmoid, kr + ["pcol"], [("gt", 0)], bias=pcol[:, 172 + 2 * l + c:173 + 2 * l + c])
                    act(g1, pi_[:, 0:n], AF.Sigmoid, ki + ["pcol"], [("gt", 1)], bias=pcol[:, 176 + 2 * l + c:177 + 2 * l + c])
                    ka = ck("av", a, a + n); kb_ = ck("bv", a, a + n)
                    act(av[:, a:a + n], g0, AF.Exp, [("gt", 0), "cA"], ka, scale=cA[:, c:c + 1])
                    tt(g0, av[:, a:a + n], av[:, a:a + n], ALU.mult, ka, [("gt", 0)])
                    ts(g0, g0, -1.0, 1.0, ALU.mult, ALU.add, [("gt", 0)], [("gt", 0)])
                    ts(g0, g0, 0.0, 0.5, ALU.max, ALU.pow, [("gt", 0)], [("gt", 0)])
                    tt(g1, g1, xl[:, a:a + n], ALU.mult, [("gt", 1)] + kx, [("gt", 1)])
                    tt(bv[:, a:a + n], g0, g1, ALU.mult, [("gt", 0), ("gt", 1)], kb_)

                wrow = (lambda j, c=c: pcol[:, 152 + (l * 4 + j) * 2 + c:152 + (l * 4 + j) * 2 + c + 1])
                ps_ = conv_chunk(2056 + c * 128, wrow, buf3, lru_dst, bias=pcol[:, 168 + 2 * l + c:169 + 2 * l + c])
                conv_state_out(ps_, 128, o_lruc_p[l, :, c * 128:(c + 1) * 128], o_lruc_s[l, :, 2, c * 128:(c + 1) * 128])
                kall_a = ck("av", 0, NT); kall_b = ck("bv", 0, NT)
                P.op("dve", lambda e, av=av, bv=bv, hv=hv: e.tensor_tensor_scan(hv[:, 0:NP], av[:, 0:NP], bv[:, 0:NP], 0.0,
                                                                                 ALU.mult, ALU.add), kall_a + kall_b + ck("xl", 0, NT), ck("xl", 0, NT) + ["hv"])
                tt(hv[:, NP:NT], av[:, NP:NT], h0T, ALU.mult, kall_a + ["h0T"] + ck("xl", NP, NT), ["hvs"] + ck("xl", NP, NT))
                tt(hv[:, NP:NT], hv[:, NP:NT], bv[:, NP:NT], ALU.add, kall_b + ["hvs"], ["hvs"])
                sl = load_w(2312 + c * 128, 128)
                for ti, (a, n) in enumerate(TT):
                    pb, kb = proj_tile(sl, 128, ti)
                    y_ = gt[:, 0, 0:n]; u_ = gt[:, 1, 0:n]
                    cp(y_, pb[:, 0:n], kb, [("gt", 0)], eng="act")
                    tt(u_, y_, y_, ALU.mult, [("gt", 0)], [("gt", 1)])
                    ts(u_, u_, 0.044715, 1.0, ALU.mult, ALU.add, [("gt", 1)], [("gt", 1)])
                    tt(u_, u_, y_, ALU.mult, [("gt", 0), ("gt", 1)], [("gt", 1)])
                    act(u_, u_, AF.Sigmoid, [("gt", 1)], [("gt", 1)], scale=1.5957691216057308)
                    tt(u_, u_, y_, ALU.mult, [("gt", 0), ("gt", 1)], [("gt", 1)])
                    tt(oT[:, c, a:a + n], hv[:, a:a + n], u_, ALU.mult, ["hv", "hvs", ("gt", 1)], ck(("oT", c), a, a + n))
                tmp, kt = stsl()
                store_T(hv[:, NP - 1:NP], 128, 1, o_lru_p[l:l + 1, c * 128:(c + 1) * 128], ["hv"], tmp, kt)
                tmp, kt = stsl()
                store_T(hv[:, NP:NT], 128, 16, o_lru_s[l, :, c * 128:(c + 1) * 128], ["hvs"], tmp, kt)
                A.release(mc)
                P.fence()
            apply_wout([512, 640], 128)
            A.phase("lru")
            A.release(mlru)
            P.fence()
            if stage < 4:
                A.release(m)
                return

            mret = A.mark()
            cosr = A.alloc([2, 512]); sinr = A.alloc([2, 512])
            rcnt = {"i": 0}
            o64bd = A.alloc([128])
            memset(o64bd, 0.0, ["o64bd"])
            memset(o64bd[0:64, 0:64], 1.0 / 64, ["o64bd"])
            memset(o64bd[64:128, 64:128], 1.0 / 64, ["o64bd"])
            gamc = A.alloc([8])
            P.dma("sp", gamc, c_gamc, w=["gamc"])
            for hp in range(2):
                mh = A.mark()
                mrr = AR.mark()
                rq = AR.alloc([NT]); rk = AR.alloc([NT]); rv = A.alloc([NT])
                decT = A.alloc([2, 128])
                reteg = A.alloc([128])
                for hh in range(2):
                    P.dma("sp", decT[:, hh, :], c_retdec[2 * hp + hh], w=["decT"])
                    P.dma("sp", reteg[hh * 64:(hh + 1) * 64, :], c_reteg[:, 2 * hp + hh, :], w=["reteg"])
                Sr = A.alloc([128])
                memset(Sr, 0.0, ["Sr"])
                tq_ = A.alloc([2, 512])
                gC = lambda j: gamc[:, 4 * hp + j:4 * hp + j + 1]
                for (which, dst, name) in ((0, rq, "rq"), (1, rk, "rk")):
                    c0 = 2568 + which * 256 + hp * 128
                    sl = cnt["w"] % 2
                    cnt["w"] += 1
                    sl2 = cnt["w"] % 2
                    cnt["w"] += 1
                    P.dma("pool", wring[:, sl, :, 0:128], wiv[:, :, c0:c0 + 128], w=[("wr", sl)])
                    for hh in range(2):
                        P.dma("pool", wring[:, sl2, :, hh * 64:hh * 64 + 32], wiv[:, :, c0 + hh * 64 + 32:c0 + hh * 64 + 64],
                              r=[("wr", sl2)], w=[("wr", sl2)])
                        P.dma("pool", wring[:, sl2, :, hh * 64 + 32:hh * 64 + 64], wiv[:, :, c0 + hh * 64:c0 + hh * 64 + 32],
                              r=[("wr", sl2)], w=[("wr", sl2)])
                    for ti, (a, n) in enumerate(TT):
                        pb, kb = proj_tile(sl, 128, ti)
                        pb2, kb2 = proj_tile(sl2, 128, ti)
                        s2 = ti % 2
                        rs = rcnt["i"] % 2
                        rcnt["i"] += 1
                        for hh in range(2):
                            P.dma("sp", cosr[hh * 64:(hh + 1) * 64, rs, 0:n], c_cos[:, a:a + n], w=[("cosr", rs)])
                            P.dma("sp", sinr[hh * 64:(hh + 1) * 64, rs, 0:n], c_sin[:, a:a + n], w=[("sinr", rs)])
                        tt(tq_[:, s2, 0:n], pb[:, 0:n], cosr[:, rs, 0:n], ALU.mult, kb + [("cosr", rs)], [("tq", s2)])
                        tt(dst[:, a:a + n], pb2[:, 0:n], sinr[:, rs, 0:n], ALU.mult, kb2 + [("sinr", rs)], ck(name, a, a + n))
                        tt(dst[:, a:a + n], dst[:, a:a + n], tq_[:, s2, 0:n], ALU.add, ck(name, a, a + n) + [("tq", s2)],
                           ck(name, a, a + n))
                sl = load_w(3080 + hp * 128, 128)
                for ti, (a, n) in enumerate(TT):
                    pb, kb = proj_tile(sl, 128, ti)
                    cp(rv[:, a:a + n], pb[:, 0:n], kb, ck("rv", a, a + n), eng="act")

                RNG = 3
                rsets = [{k: A.alloc([128]) for k in ("qp", "ktm", "vtm", "MT0", "MT1", "o", "cen")} for _ in range(RNG)]
                for gi_, T_ in enumerate(rsets):
                    T_["sq"] = T_["o"]
                    T_["rn"] = T_["o"]
                    T_["qbd"] = AR.alloc([2, 128])
                    memset(T_["qbd"], 0.0, [("rset", gi_, "qbd")])
                RAL = {"sq": "o", "rn": "o"}

                def ret_post(o_src, ko, t0, C, T_, K, bank, kbk):
                    cp(T_["o"][:, 0:C], o_src, ko, K("o"))
                    yield
                    mm(bank[:, 0:C], o64bd, T_["o"][:, 0:C], K("o") + ["o64bd"], kbk)
                    yield
                    tt(T_["cen"][:, 0:C], T_["o"][:, 0:C], bank[:, 0:C], ALU.subtract, K("o") + kbk, K("cen"))
                    tt(T_["sq"][:, 0:C], T_["cen"][:, 0:C], T_["cen"][:, 0:C], ALU.mult, K("cen"), K("sq"))
                    yield
                    mm(bank[:, 0:C], o64bd, T_["sq"][:, 0:C], K("sq") + ["o64bd"], kbk)
                    yield
                    ts(T_["rn"][:, 0:C], bank[:, 0:C], EPS, -0.5, ALU.add, ALU.pow, kbk, K("rn"))
                    yield
                    tt(oT[:, hp, t0:t0 + C], T_["cen"][:, 0:C], T_["rn"][:, 0:C], ALU.mult, K("cen") + K("rn"),
                       ck(("oT", hp), t0, t0 + C))

                rturn = {"i": 0}

                def ret_chunk(ci, g):
                    t0, C = CH[ci]
                    T_ = rsets[g]
                    K = lambda n_, g=g: [("rset", g, RAL.get(n_, n_))]
                    bank = psb[4 + g]
                    kbk = [("psb", 4 + g)]
                    kq_ = ck("rq", t0, t0 + C); kk_ = ck("rk", t0, t0 + C); kv_ = ck("rv", t0, t0 + C)
                    tt(T_["qp"][:, 0:C], rq[:, t0:t0 + C], reteg[:, 0:C], ALU.mult, kq_ + ["reteg"], K("qp"), eng="pool")
                    tr(bank[0:C, 0:128], rk[:, t0:t0 + C], ident, kk_ + ["ident"], kbk)
                    tr(bank[0:C, 128:256], rv[:, t0:t0 + C], ident, kv_ + ["ident"], kbk)
                    for hh in range(2):
                        b0 = hh * 64
                        cp(T_["qbd"][b0:b0 + 64, hh, 0:C], rq[b0:b0 + 64, t0:t0 + C], kq_, K("qbd"), eng="act")
                    mm(bank[0:C, 256:512].rearrange("p (h c) -> p h c", h=2)[:, :, 0:C], rk[:, t0:t0 + C], T_["qbd"][:, :, 0:C],
                       kk_ + K("qbd"), kbk)
                    yield
                    jc = 0 if C == 128 else 1
                    for hh in range(2):
                        h = 2 * hp + hh
                        kcol = retkd[0:C, 2 * h + jc:2 * h + jc + 1]
                        ts(T_["ktm"][0:C, hh * 64:(hh + 1) * 64], bank[0:C, hh * 64:(hh + 1) * 64], kcol, None, ALU.mult, None,
                           kbk + ["retkd"], K("ktm"))
                    cp(T_["vtm"][0:C, :], bank[0:C, 128:256], kbk, K("vtm"))
                    for hh in range(2):
                        tt(T_["MT%d" % hh][0:C, 0:C], bank[0:C, 256 + hh * 128:256 + hh * 128 + C], decT[0:C, hh, 0:C], ALU.mult,
                           kbk + ["decT"], K("MT%d" % hh))
                    yield
                    while rturn["i"] != ci:
                        yield
                    mm(bank[:, 0:C], Sr, T_["qp"][:, 0:C], ["Sr"] + K("qp"), kbk, start=True, stop=False)
                    for hh in range(2):
                        b0 = hh * 64
                        mm(bank[b0:b0 + 64, 0:C], T_["vtm"][0:C, b0:b0 + 64], T_["MT%d" % hh][0:C, 0:C],
                           K("vtm") + K("MT%d" % hh), kbk, start=False, stop=True)
                    for hh in range(2):
                        b0 = hh * 64
                        mm(bank[b0:b0 + 64, 256 + b0:320 + b0], T_["ktm"][0:C, b0:b0 + 64], T_["vtm"][0:C, b0:b0 + 64],
                           K("ktm") + K("vtm"), kbk)
                    yield
                    for hh in range(2):
                        b0 = hh * 64
                        stt(Sr[b0:b0 + 64, b0:b0 + 64], Sr[b0:b0 + 64, b0:b0 + 64], gC(jc)[b0:b0 + 64], bank[b0:b0 + 64, 256 + b0:320 + b0],
                            ALU.mult, ALU.add, ["Sr", "gamc"] + kbk, ["Sr"])
                    rturn["i"] = ci + 1
                    yield from ret_post(bank[:, 0:C], kbk, t0, C, T_, K, bank, kbk)

                run_pool(ret_chunk, 17, RNG)
                for hh in range(2):
                    P.dma("sp", o_ret_p[l, 2 * hp + hh], Sr[hh * 64:(hh + 1) * 64, hh * 64:(hh + 1) * 64], r=["Sr"], is_out=True)

                T_ = rsets[0]
                K = lambda n_: [("rset", 0, RAL.get(n_, n_))]
                sc_ = slice(NP, NT)
                ksq = ck("rq", NP, NT); ksk = ck("rk", NP, NT); ksv = ck("rv", NP, NT)
                S0r = cosr.rearrange("p a (b c) -> p (a b) c", c=64)
                Snr = sinr.rearrange("p a (b c) -> p (a b) c", c=64)
                KS0 = [("cosr", 0), ("cosr", 1)]
                KSN = [("sinr", 0), ("sinr", 1)]
                for hh in range(2):
                    P.dma("sp", S0r[hh * 64:(hh + 1) * 64], sret[l, :, 2 * hp + hh].rearrange("s d e -> d s e"), w=KS0)
                pqs2 = [PQ(), PQ()]
                for hh in range(2):
                    b0 = hh * 64
                    pqs, kpqs = pqs2[hh]
                    for s_ in range(16):
                        mm(pqs[b0:b0 + 64, s_:s_ + 1], S0r[b0:b0 + 64, s_, :], rq[b0:b0 + 64, NP + s_:NP + s_ + 1], KS0 + ksq, kpqs)
                tt(T_["sq"][:, 0:16], rq[:, sc_], rk[:, sc_], ALU.mult, ksq + ksk, K("sq"))
                pqk, kpqk = PQ()
                mm(pqk[:, 0:16], o64bd, T_["sq"][:, 0:16], K("sq") + ["o64bd"], kpqk)
                stt(T_["cen"][:, 0:16], pqk[:, 0:16], 8.0, rv[:, sc_], ALU.mult, ALU.mult, kpqk + ksv, K("cen"))
                for hh in range(2):
                    b0 = hh * 64
                    pqs, kpqs = pqs2[hh]
                    stt(T_["o"][b0:b0 + 64, 0:16], pqs[b0:b0 + 64, 0:16], gC(2)[b0:b0 + 64], T_["cen"][b0:b0 + 64, 0:16], ALU.mult, ALU.add,
                        kpqs + K("cen") + ["gamc"], K("o"))
                T1 = rsets[1]
                K1 = lambda n_: [("rset", 1, RAL.get(n_, n_))]
                for _ in ret_post(T_["o"][:, 0:16], K("o"), NP, NS, T1, K1, psb[7], [("psb", 7)]):
                    pass
                pk, kpk = PQ()
                tr(pk[0:16, 0:128], rk[:, sc_], ident, ksk + ["ident"], kpk)
                ts(T_["ktm"][0:16, :], pk[0:16, 0:128], 0.125, None, ALU.mult, None, kpk, K("ktm"))
                pv, kpv = PQ()
                tr(pv[0:16, 0:128], rv[:, sc_], ident, ksv + ["ident"], kpv)
                cp(T_["vtm"][0:16, :], pv[0:16, 0:128], kpv, K("vtm"), eng="act")
                kmask = AR.alloc([8, 128])
                for sg_ in range(2):
                    tt(kmask[0:16], T_["ktm"][0:16, :].unsqueeze(1).to_broadcast([16, 8, 128]),
                       ident[0:16, sg_ * 8:sg_ * 8 + 8].unsqueeze(2).to_broadcast([16, 8, 128]), ALU.mult,
                       K("ktm") + ["ident"], ["kmask"])
                    pb, kb = PB()
                    for hh in range(2):
                        b0 = hh * 64
                        for s8 in range(8):
                            mm(pb[b0:b0 + 64, s8 * 64:(s8 + 1) * 64], kmask[0:16, s8, b0:b0 + 64], T_["vtm"][0:16, b0:b0 + 64],
                               ["kmask"] + K("vtm"), kb)
                    stt(Snr[:, sg_ * 8:(sg_ + 1) * 8, :], S0r[:, sg_ * 8:(sg_ + 1) * 8, :], gC(2),
                        pb.rearrange("p (s e) -> p s e", e=64), ALU.mult, ALU.add, KS0 + kb + ["gamc"], KSN)
                for hh in range(2):
                    P.dma("sp", o_ret_s[l, :, 2 * hp + hh].rearrange("s d e -> d s e"), Snr[hh * 64:(hh + 1) * 64], r=KSN, is_out=True)
                sl = load_w(3336 + hp * 128, 128)
                for ti, (a, n) in enumerate(TT):
                    pb, kb = proj_tile(sl, 128, ti)
                    s2 = ti % 2
                    act(tq_[:, s2, 0:n], pb[:, 0:n], AF.Silu, kb, [("tq", s2)])
                    ko = ck(("oT", hp), a, a + n)
                    tt(oT[:, hp, a:a + n], oT[:, hp, a:a + n], tq_[:, s2, 0:n], ALU.mult, ko + [("tq", s2)], ko)
                A.release(mh)
                AR.release(mrr)
                P.fence()
            apply_wout([768, 896], 128)
            A.phase("ret")
            A.release(mret)
            A.release(m)
            P.fence()

        xn = A.alloc([8, NT], BF16)
        for l in range(2):
            if stage >= 1:
                rmsnorm(0 + l, xn)
                ffn(l, w1i, w1o, xn)
            if stage >= 2:
                rmsnorm(2 + l, xn)
                mixer(l, xn)
            if stage >= 5:
                rmsnorm(4 + l, xn)
                ffn(l, w2i, w2o, xn)
            if stage < 6:
                break

        P.fence()
        yt = A.alloc([8, 128]); ostg = A.alloc([2, 1024])
        sq = A.alloc([8, 512], BF16)
        rstd = A.alloc([512])
        oblocks = [(16 + 128 * i, 128, yp[i * 128:(i + 1) * 128, :]) for i in range(16)] + [(NP, NS, ys[:, :])]
        for bi, (a, n, dst) in enumerate(oblocks):
            kx = ck("xT", a, a + n)
            for c in range(8):
                act(sq[:, c, 0:n], xT[:, c, a:a + n], AF.Square, kx, [("sq", c)])
            pq, kq = PQ()
            for c in range(8):
                mm(pq[:, 0:n], onesm, sq[:, c, 0:n], [("sq", c), "onesm"], kq, start=(c == 0), stop=(c == 7))
            ts(rstd[:, 0:n], pq[:, 0:n], EPS, -0.5, ALU.add, ALU.pow, kq, ["rstd"])
            for c in range(8):
                stt(yt[:, c, 0:n], xT[:, c, a:a + n], gain(6, c), rstd[:, 0:n], ALU.mult, ALU.mult, kx + ["rstd", "pcol"], ["yt"])
            sl = bi % 2
            for half in range(2):
                pb, kb = PB()
                for q in range(4):
                    c = half * 4 + q
                    tr(pb[0:n, q * 128:(q + 1) * 128], yt[:, c, 0:n], ident, ["yt", "ident"], kb)
                cp(ostg[0:n, sl, half * 512:(half + 1) * 512], pb[0:n, :], kb, [("ostg", sl, half)], eng=("act" if half else "dve"))
            P.dma("sp", dst, ostg[0:n, sl, :], r=[("ostg", sl, 0), ("ostg", sl, 1)], is_out=True)
        A.phase('final')
        P.arena_hw = A.peaks + [('rr', AR.hw)]
        P.emit()
    return nc, P


def _consts():
    i = np.arange(128)
    c = {}
    c["c_ident"] = np.eye(128, dtype=np.float32)
    c["c_ut"] = (i[:, None] <= i[None, :]).astype(np.float32)
    c["c_mask"] = np.where(i[None, :] > i[:, None], 0.0, -1e30).astype(np.float32)
    gam = (1.0 - 2.0 ** (-5.0 - np.arange(4))).astype(np.float64)
    dec = np.zeros((4, 128, 128), np.float64)
    diff = (i[None, :] - i[:, None]).astype(np.float64)
    for h in range(4):
        dec[h] = np.where(diff >= 0, 0.125 * gam[h] ** np.maximum(diff, 0), 0.0)
    c["c_retdec"] = dec.astype(np.float32)
    eg = np.zeros((64, 4, 128), np.float64)
    for h in range(4):
        eg[:, h, :] = gam[h] ** (i[None, :] + 1.0)
    c["c_reteg"] = eg.astype(np.float32)
    kd = np.zeros((128, 8), np.float64)
    for h in range(4):
        kd[:, 2 * h] = 0.125 * gam[h] ** (127.0 - i)
        kd[:16, 2 * h + 1] = 0.125 * gam[h] ** (15.0 - i[:16])
    c["c_retkd"] = kd.astype(np.float32)
    gc = np.zeros((128, 8), np.float64)
    for hp in range(2):
        for hh in range(2):
            g_ = gam[2 * hp + hh]
            gc[hh * 64:(hh + 1) * 64, 4 * hp + 0] = g_ ** 128
            gc[hh * 64:(hh + 1) * 64, 4 * hp + 1] = g_ ** 16
            gc[hh * 64:(hh + 1) * 64, 4 * hp + 2] = g_
    c["c_gamc"] = gc.astype(np.float32)
    pos = np.concatenate([np.arange(NP), np.full(NS, 16384)]).astype(np.float32)
    inv = (10000.0 ** (-np.arange(32, dtype=np.float32) / 32)).astype(np.float32)
    ang = (pos[None, :] * inv[:, None]).astype(np.float32).astype(np.float64)
    cos = np.cos(ang); sin = np.sin(ang)
    c["c_cos"] = np.concatenate([cos, cos], 0).astype(np.float32)
    c["c_sin"] = np.concatenate([-sin, sin], 0).astype(np.float32)
    return c


_CACHE = {}


def _get_nc(stage):
    if stage not in _CACHE:
        _CACHE[stage] = build(stage)[0]
    return _CACHE[stage]


def kernel(x_prompt, x_sample, state_dn, state_dn_conv, state_lru, state_lru_conv, state_ret,
           meta_tokens, norm_ffn1, w_ffn1_in, w_ffn1_out, norm_mix, w_in, dn_conv_w, dn_a_log,
           dn_dt_bias, dn_norm_w, lru_conv_w, lru_conv_b, lru_wa, lru_ba, lru_wx, lru_bx, lru_lambda,
           w_out, norm_ffn2, w_ffn2_in, w_ffn2_out, norm_final, _stage=99):
    f = lambda a: np.ascontiguousarray(np.asarray(a, dtype=np.float32))
    nc = _get_nc(_stage)
    pvec = np.concatenate([
        f(norm_ffn1).reshape(16, 128), f(norm_mix).reshape(16, 128), f(norm_ffn2).reshape(16, 128),
        f(norm_final).reshape(8, 128),
        f(dn_conv_w).reshape(96, 128), f(lru_conv_w).reshape(16, 128), f(lru_conv_b).reshape(4, 128),
        f(lru_ba).reshape(4, 128), f(lru_bx).reshape(4, 128), f(lru_lambda).reshape(4, 128),
        f(dn_norm_w).reshape(2, 128)], axis=0)
    abv = np.concatenate([f(dn_a_log).reshape(-1), f(dn_dt_bias).reshape(-1)]).reshape(1, 16)
    shared = {
        "meta": f(meta_tokens), "w1i": f(w_ffn1_in), "w1o": f(w_ffn1_out), "w2i": f(w_ffn2_in), "w2o": f(w_ffn2_out),
        "win": f(w_in), "wout": f(w_out), "lwa": f(lru_wa), "lwx": f(lru_wx), "pvec": f(pvec), "abv": f(abv),
    }
    shared.update(_consts())
    xp = f(x_prompt); xs = f(x_sample)
    sdn = f(state_dn); sdnc = f(state_dn_conv); slru = f(state_lru); slruc = f(state_lru_conv); sret = f(state_ret)
    in_maps = []
    for c in range(NCORES):
        s = slice(c * NS, (c + 1) * NS)
        m = dict(shared)
        m.update({"xp": xp[c], "xs": f(xs[s, 0, :]), "sdn": f(sdn[:, s]), "sdnc": f(sdnc[:, s]), "slru": f(slru[:, s]),
                  "slruc": f(slruc[:, s]), "sret": f(sret[:, s])})
        in_maps.append(m)
    res = run_bass_kernel_spmd(nc, in_maps, core_ids=list(range(NCORES)))
    R = res.results
    y_prompt = np.stack([R[c]["yp"] for c in range(NCORES)], 0)
    y_sample = np.concatenate([R[c]["ys"] for c in range(NCORES)], 0)[:, None, :]
    stp = lambda k: np.stack([R[c][k] for c in range(NCORES)], 1)
    cat = lambda k: np.concatenate([R[c][k] for c in range(NCORES)], 1)
    return (y_prompt, y_sample, stp("dn_p"), stp("dnc_p"), stp("lru_p"), stp("lruc_p"), stp("ret_p"),
            cat("dn_s"), cat("dnc_s"), cat("lru_s"), cat("lruc_s"), cat("ret_s"))
```

```python
import math
from contextlib import ExitStack
import numpy as np
import concourse.bass as bass
import concourse.mybir as mybir
from concourse.bass_utils import run_bass_kernel_spmd

F32 = mybir.dt.float32
F32R = mybir.dt.float32r
BF16 = mybir.dt.bfloat16
AF = mybir.ActivationFunctionType
ALU = mybir.AluOpType

D = 1024
NT = 2080
NP = 2064
NS = 16
DFF = 2816
DIN = 3592
EPS = 1e-6
TW = 416
TT = [(TW * i, TW) for i in range(5)]
SO = NP - TT[4][0]
CH = [(0, 16)] + [(16 + 128 * i, 128) for i in range(16)]
NCORES = 8
DEBUG_SITES = False


class Op:
    __slots__ = ("eng", "fn", "deps", "signal", "seq", "is_dma", "sem", "cnt", "prev_same_sem")

    def __init__(self, eng, fn, is_dma):
        self.eng = eng
        self.fn = fn
        self.deps = set()
        self.signal = False
        self.seq = None
        self.is_dma = is_dma
        self.sem = None
        self.cnt = None
        self.prev_same_sem = None


class Prog:
    ENGS = ("pe", "act", "dve", "pool", "sp")

    def __init__(self, nc, n_dma_sems=10):
        self.nc = nc
        self.ops = []
        self.last_w = {}
        self.readers = {}
        self.n_dma_sems = n_dma_sems
        self.dma_rr = {e: 0 for e in self.ENGS}
        self.dma_last = {}
        self.dma_cnt = {}
        self.out_dmas = []
        self.fence_deps = []

    def _record(self, o, r, w):
        deps = set(self.fence_deps)
        raw = set()
        for k in r:
            lw = self.last_w.get(k)
            if lw is not None:
                deps.add(lw)
                raw.add(lw)
            if isinstance(k, tuple) and k[0] == "psb":
                for rd in self.readers.get(k, {}).values():
                    if rd.eng != o.eng:
                        deps.add(rd)
        for k in w:
            lw = self.last_w.get(k)
            if lw is not None:
                deps.add(lw)
            for rd in self.readers.get(k, {}).values():
                deps.add(rd)
        for d in deps:
            if d is o:
                continue
            if (not d.is_dma) and (not o.is_dma) and d.eng == o.eng:
                if o.eng == "pe":
                    continue
            o.deps.add(d)
            d.signal = True
        ek = o.sem if o.is_dma else o.eng
        for k in r:
            self.readers.setdefault(k, {})[ek] = o
        for k in w:
            self.last_w[k] = o
            self.readers[k] = {}
        self.ops.append(o)
        return o

    def op(self, eng, fn, r=(), w=()):
        if DEBUG_SITES:
            import traceback
            try:
                fn._site = [(f.lineno, f.name) for f in traceback.extract_stack(limit=5)[:-1]]
            except Exception:
                pass
        return self._record(Op(eng, fn, False), r, w)

    def dma(self, eng, out, in_, r=(), w=(), is_out=False, **kw):
        def fn(e):
            return e.dma_start(out=out, in_=in_, **kw)
        o = Op(eng, fn, True)
        slot = self.dma_rr[eng]
        self.dma_rr[eng] = (slot + 1) % self.n_dma_sems
        key = (eng, slot)
        o.sem = key
        self.dma_cnt[key] = self.dma_cnt.get(key, 0) + 16
        o.cnt = self.dma_cnt[key]
        o.prev_same_sem = self.dma_last.get(key)
        self.dma_last[key] = o
        o.signal = True
        self._record(o, r, w)
        if is_out:
            self.out_dmas.append(o)
        return o

    def fence(self):
        last = {}
        for o in self.ops:
            k = o.sem if o.is_dma else o.eng
            last[k] = o
        self.fence_deps = list(last.values())
        for d in self.fence_deps:
            d.signal = True

    def emit(self):
        nc = self.nc
        with ExitStack() as st:
            esem = {e: st.enter_context(nc.semaphore("s_" + e)) for e in self.ENGS}
            dsem = {}
            for key in self.dma_cnt:
                dsem[key] = st.enter_context(nc.semaphore("d_%s_%d" % key))
            cnt = {e: 0 for e in self.ENGS}
            for o in self.ops:
                if not o.is_dma and o.signal:
                    cnt[o.eng] += 1
                    o.seq = cnt[o.eng]
            self.stats = dict(cnt)
            fin = Op("sp", None, False)
            fin.deps = set(self.out_dmas)
            streams = {e: [o for o in self.ops if o.eng == e] for e in self.ENGS}
            streams["sp"].append(fin)
            block = st.enter_context(nc.Block())

            def run(e_name, eng):
                waited = {}
                for o in streams[e_name]:
                    need = {}
                    deps = list(o.deps)
                    if o.is_dma and o.prev_same_sem is not None:
                        deps.append(o.prev_same_sem)
                    for d in deps:
                        if d.is_dma:
                            s, v, sk = dsem[d.sem], d.cnt, d.sem
                        else:
                            s, v, sk = esem[d.eng], d.seq, d.eng
                        if waited.get(sk, 0) >= v:
                            continue
                        if sk not in need or need[sk][1] < v:
                            need[sk] = (s, v)
                    for sk, (s, v) in need.items():
                        eng.wait_ge(s, v)
                        waited[sk] = v
                    if o.fn is None:
                        continue
                    try:
                        ins = o.fn(eng)
                    except Exception:
                        print("EMIT FAILURE at op recorded from:", getattr(o.fn, "_site", None))
                        raise
                    if o.is_dma:
                        ins.then_inc(dsem[o.sem], 16)
                    elif o.signal:
                        ins.then_inc(esem[o.eng], 1)

            @block.tensor
            def _(eng):
                run("pe", eng)

            @block.scalar
            def _(eng):
                run("act", eng)

            @block.vector
            def _(eng):
                run("dve", eng)

            @block.gpsimd
            def _(eng):
                run("pool", eng)

            @block.sync
            def _(eng):
                run("sp", eng)


class Arena:
    def __init__(self, t, nbytes):
        self.t = t
        self.n = nbytes
        self.off = 0

    def alloc(self, free_shape, dt=F32, parts=128):
        n = 1
        for s in free_shape:
            n *= s
        isz = 4 if dt == F32 else 2
        sz = (n * isz + 31) // 32 * 32
        assert self.off + sz <= self.n, ("arena overflow", self.off, sz, self.n)
        v = self.t[:, self.off // 4:(self.off + sz) // 4]
        if dt != F32:
            v = v.bitcast(dt)
        v = v[:, 0:n]
        if len(free_shape) == 2:
            v = v.rearrange("p (a b) -> p a b", a=free_shape[0])
        elif len(free_shape) == 3:
            v = v.rearrange("p (a b c) -> p a b c", a=free_shape[0], b=free_shape[1])
        self.off += sz
        self.hw = max(getattr(self, 'hw', 0), self.off)
        return v[0:parts] if parts != 128 else v

    def mark(self):
        return self.off

    def release(self, m):
        self.off = m

    def phase(self, name):
        self.peaks = getattr(self, "peaks", [])
        self.peaks.append((name, getattr(self, "hw", 0)))
        self.hw = self.off


def ck(name, a, b):
    a = max(a, 0)
    return [(name, i) for i in range(a // 128, (b - 1) // 128 + 1)]


def run_pool(make_gen, n_items, width):
    free = list(range(width))
    active = []
    nxt = 0
    while nxt < n_items or active:
        while free and nxt < n_items:
            g = free.pop(0)
            active.append((make_gen(nxt, g), g))
            nxt += 1
        still = []
        for gen, g in active:
            try:
                next(gen)
                still.append((gen, g))
            except StopIteration:
                free.append(g)
        active = still


def run_interleaved(gens):
    gens = list(gens)
    while gens:
        nxt = []
        for g in gens:
            try:
                next(g)
                nxt.append(g)
            except StopIteration:
                pass
        gens = nxt


def build(stage=99):
    nc = bass.Bass("TRN2", target_bir_lowering=False)

    def din(name, shape):
        return nc.dram_tensor(name, list(shape), F32, kind="ExternalInput").ap()

    def dout(name, shape):
        return nc.dram_tensor(name, list(shape), F32, kind="ExternalOutput").ap()

    xp = din("xp", [2048, D]); xs = din("xs", [NS, D]); meta = din("meta", [16, D])
    sdn = din("sdn", [2, NS, 4, 128, 128]); sdnc = din("sdnc", [2, NS, 3, 1536])
    slru = din("slru", [2, NS, 256]); slruc = din("slruc", [2, NS, 3, 256])
    sret = din("sret", [2, NS, 4, 64, 64])
    w1i = din("w1i", [2, D, 2 * DFF]); w1o = din("w1o", [2, DFF, D])
    w2i = din("w2i", [2, D, 2 * DFF]); w2o = din("w2o", [2, DFF, D])
    win = din("win", [2, D, DIN]); wout = din("wout", [2, D, D])
    lwa = din("lwa", [2, 4, 64, 64]); lwx = din("lwx", [2, 4, 64, 64])
    pvec = din("pvec", [186, 128]); abv = din("abv", [1, 16])
    c_ident = din("c_ident", [128, 128]); c_ut = din("c_ut", [128, 128]); c_mask = din("c_mask", [128, 128])
    c_retdec = din("c_retdec", [4, 128, 128]); c_reteg = din("c_reteg", [64, 4, 128])
    c_gamc = din("c_gamc", [128, 8])
    c_retkd = din("c_retkd", [128, 8]); c_cos = din("c_cos", [64, NT]); c_sin = din("c_sin", [64, NT])

    yp = dout("yp", [2048, D]); ys = dout("ys", [NS, D])
    o_dn_p = dout("dn_p", [2, 4, 128, 128]); o_dnc_p = dout("dnc_p", [2, 3, 1536])
    o_lru_p = dout("lru_p", [2, 256]); o_lruc_p = dout("lruc_p", [2, 3, 256])
    o_ret_p = dout("ret_p", [2, 4, 64, 64])
    o_dn_s = dout("dn_s", [2, NS, 4, 128, 128]); o_dnc_s = dout("dnc_s", [2, NS, 3, 1536])
    o_lru_s = dout("lru_s", [2, NS, 256]); o_lruc_s = dout("lruc_s", [2, NS, 3, 256])
    o_ret_s = dout("ret_s", [2, NS, 4, 64, 64])

    P = Prog(nc)
    RR_BYTES = 34 * 1024
    ARENA_BYTES = 206 * 1024 - RR_BYTES
    with ExitStack() as st:
        arena_t = st.enter_context(nc.sbuf_tensor("arena", [128, ARENA_BYTES // 4], F32))
        A = Arena(arena_t, ARENA_BYTES)
        rr_t = st.enter_context(nc.sbuf_tensor("rrarena", [128, RR_BYTES // 4], F32))
        AR = Arena(rr_t, RR_BYTES)
        psb = [st.enter_context(nc.psum_tensor("psb%d" % i, [128, 512], F32)) for i in range(8)]
        st_ps = {"b": 0, "q": 0}

        def PB():
            i = st_ps["b"]
            st_ps["b"] = (i + 1) % 4
            return psb[i], [("psb", i)]

        def PQ():
            i = st_ps["q"]
            st_ps["q"] = (i + 1) % 4
            return psb[4 + i][:, 0:128], [("psb", 4 + i)]

        def mm(out, lhsT, rhs, r, w, start=True, stop=True):
            P.op("pe", lambda e: e.matmul(out, lhsT, rhs, start=start, stop=stop), r, w)

        def tr(out, in_, idn, r, w):
            P.op("pe", lambda e: e.transpose(out, in_, idn), r, w)

        def act(out, in_, func, r, w, scale=1.0, bias=0.0):
            P.op("act", lambda e: e.activation(out, in_, func, bias=bias, scale=scale), r, w)

        def tt(out, a, b, op, r, w, eng="dve"):
            P.op(eng, lambda e: e.tensor_tensor(out, a, b, op), r, w)

        def ts(out, a, s1, s2, op0, op1, r, w, eng="dve"):
            if op1 == ALU.pow and s2 == -0.5:
                assert op0 == ALU.add
                P.op("act", lambda e: e.activation(out, a, AF.Ln, bias=epsc[0:out.shape[0], :], scale=1.0), list(r) + ["epsc"], w)
                P.op("act", lambda e: e.activation(out, out, AF.Exp, scale=-0.5), w, w)
                return
            if op1 == ALU.pow and s2 == 0.5:
                assert op0 == ALU.max
                P.op("dve", lambda e: e.tensor_scalar(out, a, 1e-18, None, ALU.max), r, w)
                P.op("act", lambda e: e.activation(out, out, AF.Ln), w, w)
                P.op("act", lambda e: e.activation(out, out, AF.Exp, scale=0.5), w, w)
                return
            if s2 is None:
                P.op(eng, lambda e: e.tensor_scalar(out, a, s1, None, op0), r, w)
            else:
                P.op(eng, lambda e: e.tensor_scalar(out, a, s1, s2, op0, op1), r, w)

        def stt(out, a, s, b, op0, op1, r, w, eng="dve"):
            P.op(eng, lambda e: e.scalar_tensor_tensor(out, a, s, b, op0, op1), r, w)

        def cp(out, in_, r, w, eng="dve"):
            if eng == "act":
                P.op("act", lambda e: e.copy(out, in_), r, w)
            else:
                P.op(eng, lambda e: e.tensor_copy(out, in_), r, w)

        def memset(ap, val, w, eng="dve"):
            P.op(eng, lambda e: e.memset(ap, val), (), w)

        ident = A.alloc([128]); ut = A.alloc([128]); maskT = A.alloc([128])
        ones_f = A.alloc([128]); onesm = A.alloc([128], BF16); ones128b = A.alloc([128], BF16)
        ones1b = A.alloc([128], BF16); o64 = A.alloc([64], parts=64)
        pcol = A.alloc([186]); abb = A.alloc([16]); nA = A.alloc([8])
        retkd = A.alloc([8])
        epsc = A.alloc([1])
        xT = A.alloc([8, NT])
        P.dma("sp", ident, c_ident, w=["ident"])
        P.dma("sp", ut, c_ut, w=["ut"])
        P.dma("sp", maskT, c_mask, w=["maskT"])
        P.dma("sp", retkd, c_retkd, w=["retkd"])
        P.dma("sp", abb, abv.partition_broadcast(128), w=["abb"])
        memset(ones_f, 1.0, ["ones_f"]); memset(onesm, 1.0 / 1024, ["onesm"])
        memset(ones128b, 1.0 / 128, ["ones128b"]); memset(ones1b, 1.0, ["ones1b"])
        memset(o64, 1.0 / 64, ["o64"])
        memset(epsc, EPS, ["epsc"])
        act(nA, abb[:, 0:8], AF.Exp, ["abb"], ["nA"])
        ts(nA, nA, -1.0, None, ALU.mult, None, ["nA"], ["nA"])

        m0 = A.mark()
        stg = A.alloc([2, 1024])
        for (r0, nr) in ((0, 128), (128, 58)):
            P.dma("sp", stg[0:nr, 0, 0:128], pvec[r0:r0 + nr, :], w=[("stg", 0)])
            pq, kq = PQ()
            tr(pq[:, 0:nr], stg[0:nr, 0, 0:128], ident[0:nr, 0:nr], [("stg", 0), "ident"], kq)
            cp(pcol[:, r0:r0 + nr], pq[:, 0:nr], kq, ["pcol"])

        def gain(v, c):
            return pcol[:, v * 8 + c:v * 8 + c + 1]

        srcs = [(xp[i * 128:(i + 1) * 128, :], 128, 16 + 128 * i) for i in range(16)]
        srcs += [(meta[:, :], 16, 0), (xs[:, :], NS, NP)]
        for i, (src, n, col) in enumerate(srcs):
            sl = i % 2
            P.dma("sp", stg[0:n, sl, :], src, w=[("stg", sl)])
            for half in range(2):
                pb, kb = PB()
                for q in range(4):
                    c = half * 4 + q
                    tr(pb[:, q * 128:q * 128 + n], stg[0:n, sl, c * 128:(c + 1) * 128], ident[0:n, 0:n],
                       [("stg", sl), "ident"], kb)
                src_v = pb.rearrange("p (q t) -> p q t", q=4)[:, :, 0:n]
                cp(xT[:, half * 4:half * 4 + 4, col:col + n], src_v, kb, ck("xT", col, col + n),
                   eng=("act" if half else "dve"))
        A.release(m0)
        P.fence()

        def rmsnorm(v, xn):
            m_ = A.mark()
            sq = A.alloc([8, 512], BF16)
            rstd = A.alloc([512])
            for ti, (a, n) in enumerate(TT):
                kx = ck("xT", a, a + n)
                for c in range(8):
                    act(sq[:, c, 0:n], xT[:, c, a:a + n], AF.Square, kx, [("sq", c)])
                pb, kb = PB()
                for c in range(8):
                    mm(pb[:, 0:n], onesm, sq[:, c, 0:n], [("sq", c), "onesm"], kb, start=(c == 0), stop=(c == 7))
                ts(rstd[:, 0:n], pb[:, 0:n], EPS, -0.5, ALU.add, ALU.pow, kb, ["rstd"])
                for c in range(8):
                    stt(xn[:, c, a:a + n], xT[:, c, a:a + n], gain(v, c), rstd[:, 0:n], ALU.mult, ALU.mult,
                        kx + ["rstd", "pcol"], [("xn", ti)])
            A.release(m_)
            P.fence()

        def ffn(l, wi, wo_d, xn):
            m = A.mark()
            hs = A.alloc([6, NT], BF16)
            wg = A.alloc([2, 8, 256], BF16); wu = A.alloc([2, 8, 256], BF16)
            wo = A.alloc([6, 1024], BF16)
            sg = A.alloc([2, 512])
            wiv = wi[l].rearrange("(c p) n -> p c n", p=128)
            slabs = [(0, 6), (6, 12), (12, 18), (18, 22)]
            pair_i = 0
            sgi = 0
            for (j0, j1) in slabs:
                npairs = (j1 - j0) // 2
                for pi in range(npairs):
                    j = j0 + 2 * pi
                    sl = pair_i % 2
                    pair_i += 1
                    P.dma("pool", wg[:, sl], wiv[:, :, j * 128:j * 128 + 256], w=[("wg", sl)])
                    P.dma("pool", wu[:, sl], wiv[:, :, DFF + j * 128:DFF + j * 128 + 256], w=[("wu", sl)])
                    if pi == 0:
                        for jl in range(j1 - j0):
                            P.dma("pool", wo[:, jl, :], wo_d[l, (j0 + jl) * 128:(j0 + jl + 1) * 128, :], w=[("wo", jl)])
                    for jj in range(2):
                        jl = 2 * pi + jj
                        for ti, (a, n) in enumerate(TT):
                            pg, kg = PB()
                            for c in range(8):
                                mm(pg[:, 0:n], wg[:, sl, c, jj * 128:(jj + 1) * 128], xn[:, c, a:a + n],
                                   [("wg", sl), ("xn", ti)], kg, start=(c == 0), stop=(c == 7))
                            pu, ku = PB()
                            for c in range(8):
                                mm(pu[:, 0:n], wu[:, sl, c, jj * 128:(jj + 1) * 128], xn[:, c, a:a + n],
                                   [("wu", sl), ("xn", ti)], ku, start=(c == 0), stop=(c == 7))
                            s = sgi % 2
                            sgi += 1
                            act(sg[:, s, 0:n], pg[:, 0:n], AF.Silu, kg, [("sg", s)])
                            tt(hs[:, jl, a:a + n], sg[:, s, 0:n], pu[:, 0:n], ALU.mult, ku + [("sg", s)], [("hs", jl, ti)])
                nj = j1 - j0
                for ti, (a, n) in enumerate(TT):
                    for dc in range(8):
                        pb, kb = PB()
                        for jl in range(nj):
                            mm(pb[:, 0:n], wo[:, jl, dc * 128:(dc + 1) * 128], hs[:, jl, a:a + n],
                               [("wo", jl), ("hs", jl, ti)], kb, start=(jl == 0), stop=(jl == nj - 1))
                        kx = ck("xT", a, a + n)
                        stt(xT[:, dc, a:a + n], pb[:, 0:n], 0.5, xT[:, dc, a:a + n], ALU.mult, ALU.add, kb + kx, kx)
            A.phase("ffn")
            A.release(m)
            P.fence()

        def store_T(src, n_p, n_f, dst, r, tmp, ktmp):
            pq, kq = PQ()
            tr(pq[0:n_f, 0:n_p], src, ident[0:n_p, 0:n_p], list(r) + ["ident"], kq)
            cp(tmp[0:n_f, 0:n_p], pq[0:n_f, 0:n_p], kq, ktmp, eng="act")
            P.dma("sp", dst, tmp[0:n_f, 0:n_p], r=ktmp, is_out=True)

        def mixer(l, xn):
            m = A.mark()
            oT = A.alloc([4, NT], BF16)
            wring = A.alloc([2, 8, 128], BF16)
            pre = A.alloc([2, 515])
            tmpc = A.alloc([2, 512])
            sqb = A.alloc([2, 512], BF16)
            rnb = A.alloc([1, 512])
            stT = A.alloc([2, 128])
            wiv = win[l].rearrange("(c p) n -> p c n", p=128)
            cnt = {"w": 0, "pre": 0, "tmp": 0, "st": 0}

            def load_w(col0, ncols):
                sl = cnt["w"] % 2
                cnt["w"] += 1
                P.dma("pool", wring[:, sl, :, 0:ncols], wiv[:, :, col0:col0 + ncols], w=[("wr", sl)])
                return sl

            def proj_tile(sl, ncols, ti):
                a, n = TT[ti]
                pb, kb = PB()
                for c in range(8):
                    mm(pb[0:ncols, 0:n], wring[:, sl, c, 0:ncols], xn[:, c, a:a + n], [("wr", sl), ("xn", ti)], kb,
                       start=(c == 0), stop=(c == 7))
                return pb, kb

            def stsl():
                s = cnt["st"] % 2
                cnt["st"] += 1
                return stT[:, s, :], [("stT", s)]

            def apply_wout(row0s, kparts):
                npc = len(row0s)
                mw_ = A.mark()
                womix = A.alloc([4, 1024], BF16)
                for i, r0 in enumerate(row0s):
                    P.dma("pool", womix[0:kparts, i, :], wout[l, r0:r0 + kparts, :], w=[("womix", i)])
                for ti, (a, n) in enumerate(TT):
                    for dc in range(8):
                        pb, kb = PB()
                        for i in range(npc):
                            mm(pb[:, 0:n], womix[0:kparts, i, dc * 128:(dc + 1) * 128], oT[0:kparts, i, a:a + n],
                               [("womix", i)] + ck(("oT", i), a, a + n), kb, start=(i == 0), stop=(i == npc - 1))
                        kx = ck("xT", a, a + n)
                        stt(xT[:, dc, a:a + n], pb[:, 0:n], 1.0, xT[:, dc, a:a + n], ALU.mult, ALU.add, kb + kx, kx)
                A.release(mw_)

            def conv_chunk(col0, wrow, buf3, dstf, nparts=128, bias=None, post=None):
                sl = load_w(col0, nparts)
                last_ps = None
                pbs = {0: proj_tile(sl, nparts, 0)}
                pres = {}

                def evac(ti):
                    a, n = TT[ti]
                    pb, kb = pbs.pop(ti)
                    ps_ = cnt["pre"] % 2
                    cnt["pre"] += 1
                    kp = [("pre", ps_)]
                    if ti == 0:
                        memset(pre[0:nparts, ps_, 0:3], 0.0, kp)
                    else:
                        cp(pre[0:nparts, ps_, 0:3], pre[0:nparts, 1 - ps_, TW:TW + 3], [("pre", 1 - ps_)], kp)
                    cp(pre[0:nparts, ps_, 3:3 + n], pb[0:nparts, 0:n], kb, kp, eng="act")
                    pres[ti] = ps_

                if len(TT) > 1:
                    pbs[1] = proj_tile(sl, nparts, 1)
                evac(0)
                for ti, (a, n) in enumerate(TT):
                    if ti + 2 < len(TT):
                        pbs[ti + 2] = proj_tile(sl, nparts, ti + 2)
                    if ti + 1 < len(TT):
                        evac(ti + 1)
                    ps_ = pres.pop(ti)
                    kp = [("pre", ps_)]
                    npr = n if ti < 4 else SO
                    tsl = cnt["tmp"] % 2
                    cnt["tmp"] += 1
                    kt = [("tmpc", tsl)]
                    t_ = tmpc[0:nparts, tsl, :]
                    ts(t_[:, 0:npr], pre[0:nparts, ps_, 3:3 + npr], wrow(3)[0:nparts], None, ALU.mult, None, kp + ["pcol"], kt)
                    for j in (2, 1, 0):
                        stt(t_[:, 0:npr], pre[0:nparts, ps_, j:j + npr], wrow(j)[0:nparts], t_[:, 0:npr], ALU.mult, ALU.add,
                            kp + kt + ["pcol"], kt)
                    if ti == 4:
                        ts(t_[:, SO:SO + 16], pre[0:nparts, ps_, 3 + SO:3 + SO + 16], wrow(3)[0:nparts], None, ALU.mult, None, kp + ["pcol"], kt)
                        for j in (2, 1, 0):
                            stt(t_[:, SO:SO + 16], buf3[0:nparts, j, :], wrow(j)[0:nparts], t_[:, SO:SO + 16], ALU.mult, ALU.add,
                                kt + ["buf3", "pcol"], kt)
                    if bias is not None:
                        ts(t_[:, 0:n], t_[:, 0:n], bias[0:nparts], None, ALU.add, None, kt + ["pcol"], kt)
                    dstf(ti, a, n, t_[:, 0:n], kt)
                    last_ps = ps_
                if post is not None:
                    for ti, (a, n) in enumerate(TT):
                        post(ti, a, n)
                return last_ps

            def conv_state_out(ps_, nparts, dst_p, dst_s):
                tmp, kt = stsl()
                store_T(pre[0:nparts, ps_, SO:3 + SO], nparts, 3, dst_p, [("pre", ps_)], tmp, kt)
                tmp, kt = stsl()
                store_T(pre[0:nparts, ps_, 3 + SO:3 + SO + 16], nparts, 16, dst_s, [("pre", ps_)], tmp, kt)

            cst48 = A.alloc([128])

            def load_bufT(src48, ncol0, nparts, buf, kbuf):
                P.dma("sp", cst48[0:48, 0:nparts], src48.rearrange("s j n -> (s j) n")[:, ncol0:ncol0 + nparts], w=["cst48"])
                pq, kq = PQ()
                tr(pq[0:nparts, 0:48], cst48[0:48, 0:nparts], ident[0:48, 0:48], ["cst48", "ident"], kq)
                cp(buf[0:nparts].rearrange("p j s -> p s j"), pq[0:nparts, 0:48].rearrange("p (s j) -> p s j", j=3), kq, kbuf)

            mdn = A.mark()
            gtm = A.alloc([18, 4]); btm = A.alloc([18, 4]); Gc = A.alloc([18, 4])
            wab = A.alloc([8, 8], BF16)
            P.dma("sp", o_dnc_s[l, :, 0:2, :], sdnc[l, :, 1:3, :], is_out=True)
            P.dma("pool", wab, wiv[:, :, 1536:1544], w=["wab"])
            blocks = CH + [(NP, NS)]
            memset(gtm, 0.0, ["gtm"]); memset(btm, 0.0, ["btm"])
            for bi, (t0, C) in enumerate(blocks):
                pq, kq = PQ()
                for c in range(8):
                    mm(pq[0:C, 0:8], xn[:, c, t0:t0 + C], wab[:, c, :], ["wab"] + [("xn", i) for i in range(5)], kq,
                       start=(c == 0), stop=(c == 7))
                tt(gtm[0:C, bi, :], pq[0:C, 0:4], abb[0:C, 8 + 4 * l:12 + 4 * l], ALU.add, kq + ["abb"], ["gtm"])
                cp(btm[0:C, bi, :], pq[0:C, 4:8], kq, ["btm"])
            act(gtm, gtm, AF.Exp, ["gtm"], ["gtm"])
            act(gtm, gtm, AF.Ln, ["gtm"], ["gtm"], bias=1.0)
            tt(gtm, gtm, nA[:, 4 * l:4 * l + 4].unsqueeze(1).to_broadcast([128, 18, 4]), ALU.mult, ["gtm", "nA"], ["gtm"])
            act(btm, btm, AF.Sigmoid, ["btm"], ["btm"])
            for bi, (t0, C) in enumerate(blocks):
                pq2, kq2 = PQ()
                mm(pq2[0:C, 0:4], ut[0:C, 0:C], gtm[0:C, bi, :], ["ut", "gtm"], kq2)
                cp(Gc[0:C, bi, :], pq2[0:C, 0:4], kq2, ["Gc"])

            for h in range(4):
                mh = A.mark()
                mrr = AR.mark()
                qT = AR.alloc([NT]); kT = AR.alloc([NT]); vT = A.alloc([NT])
                buf3 = A.alloc([3, 16])
                S = A.alloc([128])
                memset(S, 0.0, ["S"])

                def mk_dst(dst, name, l2scale):
                    def f(ti, a, n, src, kt):
                        kd_ = ck(name, a, a + n)
                        act((dst if l2scale is not None else dst)[:, a:a + n], src, AF.Silu, kt, kd_)
                    return f

                def mk_post(dst, name, l2scale):
                    held = {}

                    def stage1(ti):
                        a, n = TT[ti]
                        kd_ = ck(name, a, a + n)
                        s2 = ti % 2
                        act(sqb[:, s2, 0:n], dst[:, a:a + n], AF.Square, kd_, [("sqb", s2)])
                        pb, kb = PB()
                        mm(pb[:, 0:n], ones1b, sqb[:, s2, 0:n], [("sqb", s2), "ones1b"], kb)
                        held[ti] = (pb, kb)

                    def f(ti, a, n):
                        kd_ = ck(name, a, a + n)
                        if ti == 0:
                            stage1(0)
                        if ti + 1 < len(TT):
                            stage1(ti + 1)
                        pb, kb = held.pop(ti)
                        ts(rnb[:, 0, 0:n], pb[:, 0:n], EPS, -0.5, ALU.add, ALU.pow, kb, [("rnb", 0)])
                        stt(dst[:, a:a + n], dst[:, a:a + n], l2scale, rnb[:, 0, 0:n], ALU.mult, ALU.mult,
                            kd_ + [("rnb", 0)], kd_)
                    return f

                for (which, dst, name, sc) in ((0, qT, "qT", 128.0 ** -0.5), (1, kT, "kT", 1.0), (2, vT, "vT", None)):
                    chn = which * 4 + h
                    load_bufT(sdnc[l], chn * 128, 128, buf3, ["buf3"])
                    wrow = (lambda j, chn=chn: pcol[:, 56 + (l * 4 + j) * 12 + chn:56 + (l * 4 + j) * 12 + chn + 1])
                    ps_ = conv_chunk(chn * 128, wrow, buf3, mk_dst(dst, name, sc),
                                     post=(mk_post(dst, name, sc) if sc is not None else None))
                    conv_state_out(ps_, 128, o_dnc_p[l, :, chn * 128:(chn + 1) * 128], o_dnc_s[l, :, 2, chn * 128:(chn + 1) * 128])

                nwcol = pcol[:, 184 + l:185 + l]

                def dn_post(o_src, ko, t0, C):
                    osb = A_t["osb"][:, 0:C]
                    cp(osb, o_src, ko, ["osb"], eng="act")
                    sq_ = A_t["osq"][:, 0:C]
                    act(sq_, o_src, AF.Square, ko, ["osq"])
                    yield
                    pq, kq = psb[3][:, 0:128], [("psb", 3)]
                    mm(pq[:, 0:C], ones128b, sq_, ["osq", "ones128b"], kq)
                    yield
                    rn_ = A_t["orn"][:, 0:C]
                    ts(rn_, pq[:, 0:C], EPS, -0.5, ALU.add, ALU.pow, kq, ["orn"])
                    yield
                    stt(osb, osb, nwcol, rn_, ALU.mult, ALU.mult, ["osb", "orn", "pcol"], ["osb"])
                    yield
                    cp(oT[:, h, t0:t0 + C], osb, ["osb"], ck(("oT", h), t0, t0 + C))

                A_t = {"osb": A.alloc([128]), "osq": A.alloc([128], BF16), "orn": A.alloc([128])}
                NG = 3
                sets = []
                names = ("gUT", "Gb", "E", "dec", "eGb", "qp", "cols", "kg", "kd", "vtm", "N", "MT", "NT", "R",
                         "Pa", "PTa", "Pb", "PTb")
                RRN = ("kg", "kd", "vtm", "N", "MT", "NT", "R", "Pa", "PTa", "Pb", "PTb")
                S0s = A.alloc([16, 128])
                P.dma("sp", S0s, sdn[l, :, h].rearrange("s d e -> d s e"), w=["S0s"])
                sets.append({k: (AR.alloc([128]) if k in RRN else A.alloc([128])) for k in names})
                msets = A.mark()
                for g in range(1, NG):
                    sets.append({k: (AR.alloc([128]) if k in RRN else A.alloc([128])) for k in names})
                for T_ in sets:
                    T_["nW"], T_["U"] = T_["gUT"], T_["NT"]
                ALIAS = {"nW": "gUT", "U": "NT"}

                def prep(ci, g):
                    t0, C = CH[ci]
                    T_ = sets[g]
                    K = lambda n_: [("set", g, ALIAS.get(n_, n_))]
                    RR = lambda n_: T_[n_]
                    bank = psb[4 + g]
                    kbk = [("psb", 4 + g)]
                    kk = ck("kT", t0, t0 + C); kqk = ck("qT", t0, t0 + C); kv = ck("vT", t0, t0 + C)
                    gcol = gtm[0:C, ci, h:h + 1]; bcol = btm[0:C, ci, h:h + 1]; Gcol = Gc[0:C, ci, h:h + 1]
                    kTr = kT; qTr = qT
                    ts(T_["gUT"][0:C, 0:C], ut[0:C, 0:C], gcol, None, ALU.mult, None, ["ut", "gtm"], K("gUT"))
                    mm(bank[:, 0:C], ones_f[0:C, 0:128], T_["gUT"][0:C, 0:C], K("gUT") + ["ones_f"], kbk)
                    yield
                    cp(T_["Gb"][:, 0:C], bank[:, 0:C], kbk, K("Gb"))
                    stt(T_["E"][0:C, 0:C], bank[0:C, 0:C], Gcol, maskT[0:C, 0:C], ALU.subtract, ALU.add,
                        kbk + ["Gc", "maskT"], K("E"))
                    yield
                    act(T_["dec"][0:C, 0:C], T_["E"][0:C, 0:C], AF.Exp, K("E"), K("dec"))
                    act(T_["eGb"][:, 0:C], T_["Gb"][:, 0:C], AF.Exp, K("Gb"), K("eGb"))
                    cols = T_["cols"]
                    act(cols[0:C, 0:1], Gcol, AF.Exp, ["Gc"], K("cols"))
                    act(cols[0:C, 1:2], Gcol, AF.Exp, ["Gc"] + K("Gb"), K("cols"), scale=-1.0, bias=T_["Gb"][0:C, C - 1:C])
                    tr(bank[0:C, 0:128], kT[:, t0:t0 + C], ident, kk + ["ident"], kbk)
                    tr(bank[0:C, 128:256], vT[:, t0:t0 + C], ident, kv + ["ident"], kbk)
                    mm(bank[0:C, 256:256 + C], kTr[:, t0:t0 + C], kTr[:, t0:t0 + C], kk, kbk)
                    mm(bank[0:C, 384:384 + C], kTr[:, t0:t0 + C], qTr[:, t0:t0 + C], kk + kqk, kbk)
                    yield
                    ts(RR("kg")[0:C, :], bank[0:C, 0:128], cols[0:C, 0:1], None, ALU.mult, None, kbk + K("cols"), K("kg"))
                    ts(RR("kd")[0:C, :], bank[0:C, 0:128], cols[0:C, 1:2], None, ALU.mult, None, kbk + K("cols"), K("kd"))
                    cp(RR("vtm")[0:C, :], bank[0:C, 128:256], kbk, K("vtm"))
                    stt(RR("N")[0:C, 0:C], bank[0:C, 256:256 + C], bcol, T_["dec"][0:C, 0:C], ALU.mult, ALU.mult,
                        kbk + ["btm"] + K("dec"), K("N"))
                    tt(T_["E"][0:C, 0:C], T_["dec"][0:C, 0:C], ident[0:C, 0:C], ALU.add, K("dec") + ["ident"], K("E"), eng="pool")
                    tt(T_["qp"][:, 0:C], qT[:, t0:t0 + C], T_["eGb"][:, 0:C], ALU.mult, kqk + K("eGb"), K("qp"), eng="pool")
                    yield
                    tt(RR("MT")[0:C, 0:C], bank[0:C, 384:384 + C], T_["E"][0:C, 0:C], ALU.mult, kbk + K("E"), K("MT"))
                    tt(RR("R")[0:C, 0:C], ident[0:C, 0:C], T_["N"][0:C, 0:C], ALU.subtract, K("N") + ["ident"], K("R"))
                    yield
                    tr(bank[0:C, 0:C], T_["N"][0:C, 0:C], ident[0:C, 0:C], K("N") + ["ident"], kbk)
                    yield
                    cp(RR("NT")[0:C, 0:C], bank[0:C, 0:C], kbk, K("NT"))
                    yield
                    J = 7 if C == 128 else 4
                    Pc, PTc, kPc, kPTc = "N", "NT", K("N"), K("NT")
                    for j in range(1, J):
                        Pn, PTn = ("Pa", "PTa") if j % 2 else ("Pb", "PTb")
                        kPn, kPTn = K(Pn), K(PTn)
                        mm(bank[0:C, 0:C], RR(Pc)[0:C, 0:C], RR(PTc)[0:C, 0:C], kPc + kPTc, kbk)
                        if j < J - 1:
                            mm(bank[0:C, 128:128 + C], RR(PTc)[0:C, 0:C], RR(Pc)[0:C, 0:C], kPc + kPTc, kbk)
                        if j >= 2:
                            mm(bank[0:C, 256:256 + C], RR(PTc)[0:C, 0:C], RR("R")[0:C, 0:C], kPTc + K("R"), kbk)
                        yield
                        cp(RR(PTn)[0:C, 0:C], bank[0:C, 0:C], kbk, kPTn)
                        if j < J - 1:
                            cp(RR(Pn)[0:C, 0:C], bank[0:C, 128:128 + C], kbk, kPn)
                        if j >= 2:
                            tt(RR("R")[0:C, 0:C], T_["R"][0:C, 0:C], bank[0:C, 256:256 + C], ALU.add, K("R") + kbk, K("R"))
                        Pc, PTc, kPc, kPTc = Pn, PTn, kPn, kPTn
                        yield
                    mm(bank[0:C, 0:C], RR(PTc)[0:C, 0:C], RR("R")[0:C, 0:C], kPTc + K("R"), kbk)
                    yield
                    tt(RR("R")[0:C, 0:C], T_["R"][0:C, 0:C], bank[0:C, 0:C], ALU.add, K("R") + kbk, K("R"))
                    yield
                    mm(bank[:, 0:C], RR("kg")[0:C, :], RR("R")[0:C, 0:C], K("kg") + K("R"), kbk)
                    yield
                    ts(T_["nW"][:, 0:C], bank[:, 0:C], -1.0, None, ALU.mult, None, kbk, K("nW"))

                turn = {"i": 0}

                def seq(ci, g):
                    t0, C = CH[ci]
                    T_ = sets[g]
                    K = lambda n_: [("set", g, ALIAS.get(n_, n_))]
                    RR = lambda n_: T_[n_]
                    while turn["i"] != ci:
                        yield
                    pu, kpu = psb[0][:, 0:128], [("psb", 0)]
                    po, kpo = psb[1][:, 0:128], [("psb", 1)]
                    psn, kps = psb[2][:, 0:128], [("psb", 2)]
                    mm(pu[0:C, :], RR("R")[0:C, 0:C], RR("vtm")[0:C, :], K("R") + K("vtm"), kpu, start=True, stop=False)
                    mm(pu[0:C, :], T_["nW"][:, 0:C], S, K("nW") + ["S"], kpu, start=False, stop=True)
                    mm(po[:, 0:C], S, T_["qp"][:, 0:C], ["S"] + K("qp"), kpo, start=True, stop=False)
                    yield
                    ts(RR("U")[0:C, :], pu[0:C, :], btm[0:C, ci, h:h + 1], None, ALU.mult, None, kpu + ["btm"], K("U"))
                    yield
                    mm(po[:, 0:C], RR("U")[0:C, :], RR("MT")[0:C, 0:C], K("U") + K("MT"), kpo, start=False, stop=True)
                    mm(psn, RR("kd")[0:C, :], RR("U")[0:C, :], K("kd") + K("U"), kps)
                    yield
                    stt(S, S, T_["eGb"][:, C - 1:C], psn, ALU.mult, ALU.add, ["S"] + K("eGb") + kps, ["S"])
                    yield from dn_post(po[:, 0:C], kpo, t0, C)
                    turn["i"] = ci + 1

                def chunk_gen(ci, g):
                    yield from prep(ci, g)
                    yield from seq(ci, g)

                run_pool(chunk_gen, 17, NG)
                P.dma("sp", o_dn_p[l, h], S, r=["S"], is_out=True)

                T0 = sets[0]
                K0 = lambda n_: [("set", 0, ALIAS.get(n_, n_))]
                A.release(msets)
                P.fence()
                T0 = dict(T0)
                for nm_ in ("U", "MT", "N", "kg", "kd"):
                    T0[nm_] = A.alloc([128])
                K0 = lambda n_: [("sset", n_)] if n_ in ("U", "MT", "N", "kg", "kd") else [("set", 0, ALIAS.get(n_, n_))]
                sc_ = slice(NP, NT)
                bi = 17
                ksq = ck("qT", NP, NT); ksk = ck("kT", NP, NT); ksv = ck("vT", NP, NT)

                def rowbc(dst, val_col, kval, kdst):
                    ts(T0["gUT"][0:16, 0:16], ident[0:16, 0:16], val_col, None, ALU.mult, None, ["ident"] + kval, K0("gUT"))
                    pq, kq = PQ()
                    mm(pq[:, 0:16], ones_f[0:16, 0:128], T0["gUT"][0:16, 0:16], K0("gUT") + ["ones_f"], kq)
                    cp(dst, pq[:, 0:16], kq, kdst, eng="act")

                act(T0["cols"][0:16, 0:1], gtm[0:16, bi, h:h + 1], AF.Exp, ["gtm"], K0("cols"))
                eGbc = T0["eGb"][:, 0:16]; bbc = T0["Gb"][:, 0:16]
                rowbc(eGbc, T0["cols"][0:16, 0:1], K0("cols"), K0("eGb"))
                rowbc(bbc, btm[0:16, bi, h:h + 1], ["btm"], K0("Gb"))
                kq_ = T0["qp"][:, 0:32].rearrange("p (s two) -> p s two", two=2)
                cp(kq_[:, :, 0], kT[:, sc_], ksk, K0("qp"))
                cp(kq_[:, :, 1], qT[:, sc_], ksq, K0("qp"))
                pks, kpks = PQ()
                for s in range(16):
                    mm(pks[:, 2 * s:2 * s + 2], S0s[:, s, :], kq_[:, s, :], ["S0s"] + K0("qp"), kpks)
                pksv = pks[:, 0:32].rearrange("p (s two) -> p s two", two=2)
                tt(T0["E"][:, 0:16], qT[:, sc_], kT[:, sc_], ALU.mult, ksq + ksk, K0("E"))
                pqk, kpqk = PQ()
                mm(pqk[:, 0:16], ones_f, T0["E"][:, 0:16], K0("E") + ["ones_f"], kpqk)
                t1 = T0["dec"][:, 0:16]
                tt(t1, pksv[:, :, 0], eGbc, ALU.mult, kpks + K0("eGb"), K0("dec"))
                tt(t1, vT[:, sc_], t1, ALU.subtract, ksv + K0("dec"), K0("dec"))
                UTs = T0["U"][:, 0:16]
                tt(UTs, t1, bbc, ALU.mult, K0("dec") + K0("Gb"), K0("U"))
                t3 = T0["MT"][:, 0:16]
                tt(t3, pksv[:, :, 1], eGbc, ALU.mult, kpks + K0("eGb"), K0("MT"))
                t4 = T0["N"][:, 0:16]
                tt(t4, pqk[:, 0:16], UTs, ALU.mult, kpqk + K0("U"), K0("N"))
                tt(t3, t3, t4, ALU.add, K0("MT") + K0("N"), K0("MT"))
                for _ in dn_post(t3, K0("MT"), NP, NS):
                    pass
                pk, kpk = PQ()
                tr(pk[0:16, :], kT[:, sc_], ident, ksk + ["ident"], kpk)
                cp(T0["kg"][0:16, :], pk[0:16, :], kpk, K0("kg"), eng="act")
                pU, kpU = PQ()
                tr(pU[0:16, :], UTs, ident, K0("U") + ["ident"], kpU)
                cp(T0["kd"][0:16, :], pU[0:16, :], kpU, K0("kd"), eng="act")
                kmask = A.alloc([8, 128])
                Sn = A.alloc([8, 128])
                for hf in range(2):
                    tt(kmask[0:16], T0["kg"][0:16, :].unsqueeze(1).to_broadcast([16, 8, 128]),
                       ident[0:16, hf * 8:hf * 8 + 8].unsqueeze(2).to_broadcast([16, 8, 128]), ALU.mult,
                       K0("kg") + ["ident"], ["kmask"])
                    for sg_ in range(2):
                        pb, kb = PB()
                        for s4 in range(4):
                            s8 = sg_ * 4 + s4
                            mm(pb[:, s4 * 128:(s4 + 1) * 128], kmask[0:16, s8, :], T0["kd"][0:16, :], ["kmask"] + K0("kd"), kb)
                        for s4 in range(4):
                            s8 = sg_ * 4 + s4
                            s = hf * 8 + s8
                            stt(Sn[:, s8, :], S0s[:, s, :], eGbc[:, s:s + 1], pb[:, s4 * 128:(s4 + 1) * 128], ALU.mult, ALU.add,
                                ["S0s"] + K0("eGb") + kb, [("Sn", sg_)])
                    P.dma("sp", o_dn_s[l, hf * 8:hf * 8 + 8, h].rearrange("s d e -> d s e"), Sn, r=[("Sn", 0), ("Sn", 1)], is_out=True)
                sl = load_w(1544 + h * 128, 128)
                for ti, (a, n) in enumerate(TT):
                    pb, kb = proj_tile(sl, 128, ti)
                    s2 = ti % 2
                    act(tmpc[:, s2, 0:n], pb[:, 0:n], AF.Silu, kb, [("tmpc", s2)])
                    ko = ck(("oT", h), a, a + n)
                    tt(oT[:, h, a:a + n], oT[:, h, a:a + n], tmpc[:, s2, 0:n], ALU.mult, ko + [("tmpc", s2)], ko)
                A.release(mh)
                AR.release(mrr)
                P.fence()
            apply_wout([0, 128, 256, 384], 128)
            A.phase("dn")
            A.release(mdn)
            P.fence()
            if stage < 3:
                A.release(m)
                return

            mlru = A.mark()
            P.dma("sp", o_lruc_s[l, :, 0:2, :], slruc[l, :, 1:3, :], is_out=True)
            hst = A.alloc([256])
            P.dma("sp", hst[0:16, :], slru[l], w=["hst"])
            bda = A.alloc([2, 128]); bdx = A.alloc([2, 128])
            memset(bda, 0.0, ["bda"]); memset(bdx, 0.0, ["bdx"])
            for c in range(2):
                for b2 in range(2):
                    nb = c * 2 + b2
                    P.dma("sp", bda[b2 * 64:(b2 + 1) * 64, c, b2 * 64:(b2 + 1) * 64], lwa[l, nb], r=["bda"], w=["bda"])
                    P.dma("sp", bdx[b2 * 64:(b2 + 1) * 64, c, b2 * 64:(b2 + 1) * 64], lwx[l, nb], r=["bdx"], w=["bdx"])
            cA = A.alloc([2])
            act(cA, pcol[:, 180 + 2 * l:182 + 2 * l], AF.Exp, ["pcol"], ["cA"], scale=-1.0)
            act(cA, cA, AF.Ln, ["cA"], ["cA"], bias=1.0)
            ts(cA, cA, -8.0, None, ALU.mult, None, ["cA"], ["cA"])
            for c in range(2):
                mc = A.mark()
                xl = A.alloc([NT]); av = A.alloc([NT]); bv = A.alloc([NT]); hv = xl
                buf3 = A.alloc([3, 16]); h0T = A.alloc([16])
                gt = A.alloc([2, 512])
                load_bufT(slruc[l], c * 128, 128, buf3, ["buf3"])
                pq, kq = PQ()
                tr(pq[:, 0:16], hst[0:16, c * 128:(c + 1) * 128], ident[0:16, 0:16], ["hst", "ident"], kq)
                cp(h0T, pq[:, 0:16], kq, ["h0T"])

                def lru_dst(ti, a, n, src, kt):
                    kx = ck("xl", a, a + n)
                    cp(xl[:, a:a + n], src, kt, kx, eng="act")
                    pr, kr = PB()
                    mm(pr[:, 0:n], bda[:, c, :], xl[:, a:a + n], ["bda"] + kx, kr)
                    pi_, ki = PB()
                    mm(pi_[:, 0:n], bdx[:, c, :], xl[:, a:a + n], ["bdx"] + kx, ki)
                    g0 = gt[:, 0, 0:n]; g1 = gt[:, 1, 0:n]
                    act(g0, pr[:, 0:n], AF.Sigmoid, kr + ["pcol"], [("gt", 0)], bias=pcol[:, 172 + 2 * l + c:173 + 2 * l + c])
                    act(g1, pi_[:, 0:n], AF.Sigmoid, ki + ["pcol"], [("gt", 1)], bias=pcol[:, 176 + 2 * l + c:177 + 2 * l + c])
                    ka = ck("av", a, a + n); kb_ = ck("bv", a, a + n)
                    act(av[:, a:a + n], g0, AF.Exp, [("gt", 0), "cA"], ka, scale=cA[:, c:c + 1])
                    tt(g0, av[:, a:a + n], av[:, a:a + n], ALU.mult, ka, [("gt", 0)])
                    ts(g0, g0, -1.0, 1.0, ALU.mult, ALU.add, [("gt", 0)], [("gt", 0)])
                    ts(g0, g0, 0.0, 0.5, ALU.max, ALU.pow, [("gt", 0)], [("gt", 0)])
                    tt(g1, g1, xl[:, a:a + n], ALU.mult, [("gt", 1)] + kx, [("gt", 1)])
                    tt(bv[:, a:a + n], g0, g1, ALU.mult, [("gt", 0), ("gt", 1)], kb_)

                wrow = (lambda j, c=c: pcol[:, 152 + (l * 4 + j) * 2 + c:152 + (l * 4 + j) * 2 + c + 1])
                ps_ = conv_chunk(2056 + c * 128, wrow, buf3, lru_dst, bias=pcol[:, 168 + 2 * l + c:169 + 2 * l + c])
                conv_state_out(ps_, 128, o_lruc_p[l, :, c * 128:(c + 1) * 128], o_lruc_s[l, :, 2, c * 128:(c + 1) * 128])
                kall_a = ck("av", 0, NT); kall_b = ck("bv", 0, NT)
                P.op("dve", lambda e, av=av, bv=bv, hv=hv: e.tensor_tensor_scan(hv[:, 0:NP], av[:, 0:NP], bv[:, 0:NP], 0.0,
                                                                                 ALU.mult, ALU.add), kall_a + kall_b + ck("xl", 0, NT), ck("xl", 0, NT) + ["hv"])
                tt(hv[:, NP:NT], av[:, NP:NT], h0T, ALU.mult, kall_a + ["h0T"] + ck("xl", NP, NT), ["hvs"] + ck("xl", NP, NT))
                tt(hv[:, NP:NT], hv[:, NP:NT], bv[:, NP:NT], ALU.add, kall_b + ["hvs"], ["hvs"])
                sl = load_w(2312 + c * 128, 128)
                for ti, (a, n) in enumerate(TT):
                    pb, kb = proj_tile(sl, 128, ti)
                    y_ = gt[:, 0, 0:n]; u_ = gt[:, 1, 0:n]
                    cp(y_, pb[:, 0:n], kb, [("gt", 0)], eng="act")
                    tt(u_, y_, y_, ALU.mult, [("gt", 0)], [("gt", 1)])
                    ts(u_, u_, 0.044715, 1.0, ALU.mult, ALU.add, [("gt", 1)], [("gt", 1)])
                    tt(u_, u_, y_, ALU.mult, [("gt", 0), ("gt", 1)], [("gt", 1)])
                    act(u_, u_, AF.Sigmoid, [("gt", 1)], [("gt", 1)], scale=1.5957691216057308)
                    tt(u_, u_, y_, ALU.mult, [("gt", 0), ("gt", 1)], [("gt", 1)])
                    tt(oT[:, c, a:a + n], hv[:, a:a + n], u_, ALU.mult, ["hv", "hvs", ("gt", 1)], ck(("oT", c), a, a + n))
                tmp, kt = stsl()
                store_T(hv[:, NP - 1:NP], 128, 1, o_lru_p[l:l + 1, c * 128:(c + 1) * 128], ["hv"], tmp, kt)
                tmp, kt = stsl()
                store_T(hv[:, NP:NT], 128, 16, o_lru_s[l, :, c * 128:(c + 1) * 128], ["hvs"], tmp, kt)
                A.release(mc)
                P.fence()
            apply_wout([512, 640], 128)
            A.phase("lru")
            A.release(mlru)
            P.fence()
            if stage < 4:
                A.release(m)
                return

            mret = A.mark()
            cosr = A.alloc([2, 512]); sinr = A.alloc([2, 512])
            rcnt = {"i": 0}
            o64bd = A.alloc([128])
            memset(o64bd, 0.0, ["o64bd"])
            memset(o64bd[0:64, 0:64], 1.0 / 64, ["o64bd"])
            memset(o64bd[64:128, 64:128], 1.0 / 64, ["o64bd"])
            gamc = A.alloc([8])
            P.dma("sp", gamc, c_gamc, w=["gamc"])
            for hp in range(2):
                mh = A.mark()
                mrr = AR.mark()
                rq = AR.alloc([NT]); rk = AR.alloc([NT]); rv = A.alloc([NT])
                decT = A.alloc([2, 128])
                reteg = A.alloc([128])
                for hh in range(2):
                    P.dma("sp", decT[:, hh, :], c_retdec[2 * hp + hh], w=["decT"])
                    P.dma("sp", reteg[hh * 64:(hh + 1) * 64, :], c_reteg[:, 2 * hp + hh, :], w=["reteg"])
                Sr = A.alloc([128])
                memset(Sr, 0.0, ["Sr"])
                tq_ = A.alloc([2, 512])
                gC = lambda j: gamc[:, 4 * hp + j:4 * hp + j + 1]
                for (which, dst, name) in ((0, rq, "rq"), (1, rk, "rk")):
                    c0 = 2568 + which * 256 + hp * 128
                    sl = cnt["w"] % 2
                    cnt["w"] += 1
                    sl2 = cnt["w"] % 2
                    cnt["w"] += 1
                    P.dma("pool", wring[:, sl, :, 0:128], wiv[:, :, c0:c0 + 128], w=[("wr", sl)])
                    for hh in range(2):
                        P.dma("pool", wring[:, sl2, :, hh * 64:hh * 64 + 32], wiv[:, :, c0 + hh * 64 + 32:c0 + hh * 64 + 64],
                              r=[("wr", sl2)], w=[("wr", sl2)])
                        P.dma("pool", wring[:, sl2, :, hh * 64 + 32:hh * 64 + 64], wiv[:, :, c0 + hh * 64:c0 + hh * 64 + 32],
                              r=[("wr", sl2)], w=[("wr", sl2)])
                    for ti, (a, n) in enumerate(TT):
                        pb, kb = proj_tile(sl, 128, ti)
                        pb2, kb2 = proj_tile(sl2, 128, ti)
                        s2 = ti % 2
                        rs = rcnt["i"] % 2
                        rcnt["i"] += 1
                        for hh in range(2):
                            P.dma("sp", cosr[hh * 64:(hh + 1) * 64, rs, 0:n], c_cos[:, a:a + n], w=[("cosr", rs)])
                            P.dma("sp", sinr[hh * 64:(hh + 1) * 64, rs, 0:n], c_sin[:, a:a + n], w=[("sinr", rs)])
                        tt(tq_[:, s2, 0:n], pb[:, 0:n], cosr[:, rs, 0:n], ALU.mult, kb + [("cosr", rs)], [("tq", s2)])
                        tt(dst[:, a:a + n], pb2[:, 0:n], sinr[:, rs, 0:n], ALU.mult, kb2 + [("sinr", rs)], ck(name, a, a + n))
                        tt(dst[:, a:a + n], dst[:, a:a + n], tq_[:, s2, 0:n], ALU.add, ck(name, a, a + n) + [("tq", s2)],
                           ck(name, a, a + n))
                sl = load_w(3080 + hp * 128, 128)
                for ti, (a, n) in enumerate(TT):
                    pb, kb = proj_tile(sl, 128, ti)
                    cp(rv[:, a:a + n], pb[:, 0:n], kb, ck("rv", a, a + n), eng="act")

                RNG = 3
                rsets = [{k: A.alloc([128]) for k in ("qp", "ktm", "vtm", "MT0", "MT1", "o", "cen")} for _ in range(RNG)]
                for gi_, T_ in enumerate(rsets):
                    T_["sq"] = T_["o"]
                    T_["rn"] = T_["o"]
                    T_["qbd"] = AR.alloc([2, 128])
                    memset(T_["qbd"], 0.0, [("rset", gi_, "qbd")])
                RAL = {"sq": "o", "rn": "o"}

                def ret_post(o_src, ko, t0, C, T_, K, bank, kbk):
                    cp(T_["o"][:, 0:C], o_src, ko, K("o"))
                    yield
                    mm(bank[:, 0:C], o64bd, T_["o"][:, 0:C], K("o") + ["o64bd"], kbk)
                    yield
                    tt(T_["cen"][:, 0:C], T_["o"][:, 0:C], bank[:, 0:C], ALU.subtract, K("o") + kbk, K("cen"))
                    tt(T_["sq"][:, 0:C], T_["cen"][:, 0:C], T_["cen"][:, 0:C], ALU.mult, K("cen"), K("sq"))
                    yield
                    mm(bank[:, 0:C], o64bd, T_["sq"][:, 0:C], K("sq") + ["o64bd"], kbk)
                    yield
                    ts(T_["rn"][:, 0:C], bank[:, 0:C], EPS, -0.5, ALU.add, ALU.pow, kbk, K("rn"))
                    yield
                    tt(oT[:, hp, t0:t0 + C], T_["cen"][:, 0:C], T_["rn"][:, 0:C], ALU.mult, K("cen") + K("rn"),
                       ck(("oT", hp), t0, t0 + C))

                rturn = {"i": 0}

                def ret_chunk(ci, g):
                    t0, C = CH[ci]
                    T_ = rsets[g]
                    K = lambda n_, g=g: [("rset", g, RAL.get(n_, n_))]
                    bank = psb[4 + g]
                    kbk = [("psb", 4 + g)]
                    kq_ = ck("rq", t0, t0 + C); kk_ = ck("rk", t0, t0 + C); kv_ = ck("rv", t0, t0 + C)
                    tt(T_["qp"][:, 0:C], rq[:, t0:t0 + C], reteg[:, 0:C], ALU.mult, kq_ + ["reteg"], K("qp"), eng="pool")
                    tr(bank[0:C, 0:128], rk[:, t0:t0 + C], ident, kk_ + ["ident"], kbk)
                    tr(bank[0:C, 128:256], rv[:, t0:t0 + C], ident, kv_ + ["ident"], kbk)
                    for hh in range(2):
                        b0 = hh * 64
                        cp(T_["qbd"][b0:b0 + 64, hh, 0:C], rq[b0:b0 + 64, t0:t0 + C], kq_, K("qbd"), eng="act")
                    mm(bank[0:C, 256:256 + 2 * C], rk[:, t0:t0 + C], T_["qbd"][:, :, 0:C], kk_ + K("qbd"), kbk)
                    yield
                    jc = 0 if C == 128 else 1
                    for hh in range(2):
                        h = 2 * hp + hh
                        kcol = retkd[0:C, 2 * h + jc:2 * h + jc + 1]
                        ts(T_["ktm"][0:C, hh * 64:(hh + 1) * 64], bank[0:C, hh * 64:(hh + 1) * 64], kcol, None, ALU.mult, None,
                           kbk + ["retkd"], K("ktm"))
                    cp(T_["vtm"][0:C, :], bank[0:C, 128:256], kbk, K("vtm"))
                    for hh in range(2):
                        tt(T_["MT%d" % hh][0:C, 0:C], bank[0:C, 256 + hh * C:256 + (hh + 1) * C], decT[0:C, hh, 0:C], ALU.mult,
                           kbk + ["decT"], K("MT%d" % hh))
                    yield
                    while rturn["i"] != ci:
                        yield
                    mm(bank[:, 0:C], Sr, T_["qp"][:, 0:C], ["Sr"] + K("qp"), kbk, start=True, stop=False)
                    for hh in range(2):
                        b0 = hh * 64
                        mm(bank[b0:b0 + 64, 0:C], T_["vtm"][0:C, b0:b0 + 64], T_["MT%d" % hh][0:C, 0:C],
                           K("vtm") + K("MT%d" % hh), kbk, start=False, stop=True)
                    for hh in range(2):
                        b0 = hh * 64
                        mm(bank[b0:b0 + 64, 256 + b0:320 + b0], T_["ktm"][0:C, b0:b0 + 64], T_["vtm"][0:C, b0:b0 + 64],
                           K("ktm") + K("vtm"), kbk)
                    yield
                    for hh in range(2):
                        b0 = hh * 64
                        stt(Sr[b0:b0 + 64, b0:b0 + 64], Sr[b0:b0 + 64, b0:b0 + 64], gC(jc)[b0:b0 + 64], bank[b0:b0 + 64, 256 + b0:320 + b0],
                            ALU.mult, ALU.add, ["Sr", "gamc"] + kbk, ["Sr"])
                    rturn["i"] = ci + 1
                    yield from ret_post(bank[:, 0:C], kbk, t0, C, T_, K, bank, kbk)

                run_pool(ret_chunk, 17, RNG)
                for hh in range(2):
                    P.dma("sp", o_ret_p[l, 2 * hp + hh], Sr[hh * 64:(hh + 1) * 64, hh * 64:(hh + 1) * 64], r=["Sr"], is_out=True)

                T_ = rsets[0]
                K = lambda n_: [("rset", 0, RAL.get(n_, n_))]
                sc_ = slice(NP, NT)
                ksq = ck("rq", NP, NT); ksk = ck("rk", NP, NT); ksv = ck("rv", NP, NT)
                S0r = cosr.rearrange("p a (b c) -> p (a b) c", c=64)
                Snr = sinr.rearrange("p a (b c) -> p (a b) c", c=64)
                KS0 = [("cosr", 0), ("cosr", 1)]
                KSN = [("sinr", 0), ("sinr", 1)]
                for hh in range(2):
                    P.dma("sp", S0r[hh * 64:(hh + 1) * 64], sret[l, :, 2 * hp + hh].rearrange("s d e -> d s e"), w=KS0)
                pqs2 = [PQ(), PQ()]
                for hh in range(2):
                    b0 = hh * 64
                    pqs, kpqs = pqs2[hh]
                    for s_ in range(16):
                        mm(pqs[b0:b0 + 64, s_:s_ + 1], S0r[b0:b0 + 64, s_, :], rq[b0:b0 + 64, NP + s_:NP + s_ + 1], KS0 + ksq, kpqs)
                tt(T_["sq"][:, 0:16], rq[:, sc_], rk[:, sc_], ALU.mult, ksq + ksk, K("sq"))
                pqk, kpqk = PQ()
                mm(pqk[:, 0:16], o64bd, T_["sq"][:, 0:16], K("sq") + ["o64bd"], kpqk)
                stt(T_["cen"][:, 0:16], pqk[:, 0:16], 8.0, rv[:, sc_], ALU.mult, ALU.mult, kpqk + ksv, K("cen"))
                for hh in range(2):
                    b0 = hh * 64
                    pqs, kpqs = pqs2[hh]
                    stt(T_["o"][b0:b0 + 64, 0:16], pqs[b0:b0 + 64, 0:16], gC(2)[b0:b0 + 64], T_["cen"][b0:b0 + 64, 0:16], ALU.mult, ALU.add,
                        kpqs + K("cen") + ["gamc"], K("o"))
                T1 = rsets[1]
                K1 = lambda n_: [("rset", 1, RAL.get(n_, n_))]
                for _ in ret_post(T_["o"][:, 0:16], K("o"), NP, NS, T1, K1, psb[7], [("psb", 7)]):
                    pass
                pk, kpk = PQ()
                tr(pk[0:16, 0:128], rk[:, sc_], ident, ksk + ["ident"], kpk)
                ts(T_["ktm"][0:16, :], pk[0:16, 0:128], 0.125, None, ALU.mult, None, kpk, K("ktm"))
                pv, kpv = PQ()
                tr(pv[0:16, 0:128], rv[:, sc_], ident, ksv + ["ident"], kpv)
                cp(T_["vtm"][0:16, :], pv[0:16, 0:128], kpv, K("vtm"), eng="act")
                kmask = AR.alloc([8, 128])
                for sg_ in range(2):
                    tt(kmask[0:16], T_["ktm"][0:16, :].unsqueeze(1).to_broadcast([16, 8, 128]),
                       ident[0:16, sg_ * 8:sg_ * 8 + 8].unsqueeze(2).to_broadcast([16, 8, 128]), ALU.mult,
                       K("ktm") + ["ident"], ["kmask"])
                    pb, kb = PB()
                    for hh in range(2):
                        b0 = hh * 64
                        for s8 in range(8):
                            mm(pb[b0:b0 + 64, s8 * 64:(s8 + 1) * 64], kmask[0:16, s8, b0:b0 + 64], T_["vtm"][0:16, b0:b0 + 64],
                               ["kmask"] + K("vtm"), kb)
                    stt(Snr[:, sg_ * 8:(sg_ + 1) * 8, :], S0r[:, sg_ * 8:(sg_ + 1) * 8, :], gC(2),
                        pb.rearrange("p (s e) -> p s e", e=64), ALU.mult, ALU.add, KS0 + kb + ["gamc"], KSN)
                for hh in range(2):
                    P.dma("sp", o_ret_s[l, :, 2 * hp + hh].rearrange("s d e -> d s e"), Snr[hh * 64:(hh + 1) * 64], r=KSN, is_out=True)
                sl = load_w(3336 + hp * 128, 128)
                for ti, (a, n) in enumerate(TT):
                    pb, kb = proj_tile(sl, 128, ti)
                    s2 = ti % 2
                    act(tq_[:, s2, 0:n], pb[:, 0:n], AF.Silu, kb, [("tq", s2)])
                    ko = ck(("oT", hp), a, a + n)
                    tt(oT[:, hp, a:a + n], oT[:, hp, a:a + n], tq_[:, s2, 0:n], ALU.mult, ko + [("tq", s2)], ko)
                A.release(mh)
                AR.release(mrr)
                P.fence()
            apply_wout([768, 896], 128)
            A.phase("ret")
            A.release(mret)
            A.release(m)
            P.fence()

        xn = A.alloc([8, NT], BF16)
        for l in range(2):
            if stage >= 1:
                rmsnorm(0 + l, xn)
                ffn(l, w1i, w1o, xn)
            if stage >= 2:
                rmsnorm(2 + l, xn)
                mixer(l, xn)
            if stage >= 5:
                rmsnorm(4 + l, xn)
                ffn(l, w2i, w2o, xn)
            if stage < 6:
                break

        P.fence()
        yt = A.alloc([8, 128]); ostg = A.alloc([2, 1024])
        sq = A.alloc([8, 512], BF16)
        rstd = A.alloc([512])
        oblocks = [(16 + 128 * i, 128, yp[i * 128:(i + 1) * 128, :]) for i in range(16)] + [(NP, NS, ys[:, :])]
        for bi, (a, n, dst) in enumerate(oblocks):
            kx = ck("xT", a, a + n)
            for c in range(8):
                act(sq[:, c, 0:n], xT[:, c, a:a + n], AF.Square, kx, [("sq", c)])
            pq, kq = PQ()
            for c in range(8):
                mm(pq[:, 0:n], onesm, sq[:, c, 0:n], [("sq", c), "onesm"], kq, start=(c == 0), stop=(c == 7))
            ts(rstd[:, 0:n], pq[:, 0:n], EPS, -0.5, ALU.add, ALU.pow, kq, ["rstd"])
            for c in range(8):
                stt(yt[:, c, 0:n], xT[:, c, a:a + n], gain(6, c), rstd[:, 0:n], ALU.mult, ALU.mult, kx + ["rstd", "pcol"], ["yt"])
            sl = bi % 2
            for half in range(2):
                pb, kb = PB()
                for q in range(4):
                    c = half * 4 + q
                    tr(pb[0:n, q * 128:(q + 1) * 128], yt[:, c, 0:n], ident, ["yt", "ident"], kb)
                cp(ostg[0:n, sl, half * 512:(half + 1) * 512], pb[0:n, :], kb, [("ostg", sl, half)], eng=("act" if half else "dve"))
            P.dma("sp", dst, ostg[0:n, sl, :], r=[("ostg", sl, 0), ("ostg", sl, 1)], is_out=True)
        A.phase('final')
        P.arena_hw = A.peaks + [('rr', AR.hw)]
        P.emit()
    return nc, P


def _consts():
    i = np.arange(128)
    c = {}
    c["c_ident"] = np.eye(128, dtype=np.float32)
    c["c_ut"] = (i[:, None] <= i[None, :]).astype(np.float32)
    c["c_mask"] = np.where(i[None, :] > i[:, None], 0.0, -1e30).astype(np.float32)
    gam = (1.0 - 2.0 ** (-5.0 - np.arange(4))).astype(np.float64)
    dec = np.zeros((4, 128, 128), np.float64)
    diff = (i[None, :] - i[:, None]).astype(np.float64)
    for h in range(4):
        dec[h] = np.where(diff >= 0, 0.125 * gam[h] ** np.maximum(diff, 0), 0.0)
    c["c_retdec"] = dec.astype(np.float32)
    eg = np.zeros((64, 4, 128), np.float64)
    for h in range(4):
        eg[:, h, :] = gam[h] ** (i[None, :] + 1.0)
    c["c_reteg"] = eg.astype(np.float32)
    kd = np.zeros((128, 8), np.float64)
    for h in range(4):
        kd[:, 2 * h] = 0.125 * gam[h] ** (127.0 - i)
        kd[:16, 2 * h + 1] = 0.125 * gam[h] ** (15.0 - i[:16])
    c["c_retkd"] = kd.astype(np.float32)
    gc = np.zeros((128, 8), np.float64)
    for hp in range(2):
        for hh in range(2):
            g_ = gam[2 * hp + hh]
            gc[hh * 64:(hh + 1) * 64, 4 * hp + 0] = g_ ** 128
            gc[hh * 64:(hh + 1) * 64, 4 * hp + 1] = g_ ** 16
            gc[hh * 64:(hh + 1) * 64, 4 * hp + 2] = g_
    c["c_gamc"] = gc.astype(np.float32)
    pos = np.concatenate([np.arange(NP), np.full(NS, 16384)]).astype(np.float32)
    inv = (10000.0 ** (-np.arange(32, dtype=np.float32) / 32)).astype(np.float32)
    ang = (pos[None, :] * inv[:, None]).astype(np.float32).astype(np.float64)
    cos = np.cos(ang); sin = np.sin(ang)
    c["c_cos"] = np.concatenate([cos, cos], 0).astype(np.float32)
    c["c_sin"] = np.concatenate([-sin, sin], 0).astype(np.float32)
    return c


_CACHE = {}


def _get_nc(stage):
    if stage not in _CACHE:
        _CACHE[stage] = build(stage)[0]
    return _CACHE[stage]


def kernel(x_prompt, x_sample, state_dn, state_dn_conv, state_lru, state_lru_conv, state_ret,
           meta_tokens, norm_ffn1, w_ffn1_in, w_ffn1_out, norm_mix, w_in, dn_conv_w, dn_a_log,
           dn_dt_bias, dn_norm_w, lru_conv_w, lru_conv_b, lru_wa, lru_ba, lru_wx, lru_bx, lru_lambda,
           w_out, norm_ffn2, w_ffn2_in, w_ffn2_out, norm_final, _stage=99):
    f = lambda a: np.ascontiguousarray(np.asarray(a, dtype=np.float32))
    nc = _get_nc(_stage)
    pvec = np.concatenate([
        f(norm_ffn1).reshape(16, 128), f(norm_mix).reshape(16, 128), f(norm_ffn2).reshape(16, 128),
        f(norm_final).reshape(8, 128),
        f(dn_conv_w).reshape(96, 128), f(lru_conv_w).reshape(16, 128), f(lru_conv_b).reshape(4, 128),
        f(lru_ba).reshape(4, 128), f(lru_bx).reshape(4, 128), f(lru_lambda).reshape(4, 128),
        f(dn_norm_w).reshape(2, 128)], axis=0)
    abv = np.concatenate([f(dn_a_log).reshape(-1), f(dn_dt_bias).reshape(-1)]).reshape(1, 16)
    shared = {
        "meta": f(meta_tokens), "w1i": f(w_ffn1_in), "w1o": f(w_ffn1_out), "w2i": f(w_ffn2_in), "w2o": f(w_ffn2_out),
        "win": f(w_in), "wout": f(w_out), "lwa": f(lru_wa), "lwx": f(lru_wx), "pvec": f(pvec), "abv": f(abv),
    }
    shared.update(_consts())
    xp = f(x_prompt); xs = f(x_sample)
    sdn = f(state_dn); sdnc = f(state_dn_conv); slru = f(state_lru); slruc = f(state_lru_conv); sret = f(state_ret)
    in_maps = []
    for c in range(NCORES):
        s = slice(c * NS, (c + 1) * NS)
        m = dict(shared)
        m.update({"xp": xp[c], "xs": f(xs[s, 0, :]), "sdn": f(sdn[:, s]), "sdnc": f(sdnc[:, s]), "slru": f(slru[:, s]),
                  "slruc": f(slruc[:, s]), "sret": f(sret[:, s])})
        in_maps.append(m)
    res = run_bass_kernel_spmd(nc, in_maps, core_ids=list(range(NCORES)))
    R = res.results
    y_prompt = np.stack([R[c]["yp"] for c in range(NCORES)], 0)
    y_sample = np.concatenate([R[c]["ys"] for c in range(NCORES)], 0)[:, None, :]
    stp = lambda k: np.stack([R[c][k] for c in range(NCORES)], 1)
    cat = lambda k: np.concatenate([R[c][k] for c in range(NCORES)], 1)
    return (y_prompt, y_sample, stp("dn_p"), stp("dnc_p"), stp("lru_p"), stp("lruc_p"), stp("ret_p"),
            cat("dn_s"), cat("dnc_s"), cat("lru_s"), cat("lruc_s"), cat("ret_s"))
```

```python
import math
from contextlib import ExitStack
import numpy as np
import concourse.bass as bass
import concourse.mybir as mybir
from concourse.bass_utils import run_bass_kernel_spmd

F32 = mybir.dt.float32
F32R = mybir.dt.float32r
BF16 = mybir.dt.bfloat16
AF = mybir.ActivationFunctionType
ALU = mybir.AluOpType

D = 1024
NT = 2080
NP = 2064
NS = 16
DFF = 2816
DIN = 3592
EPS = 1e-6
TW = 416
TT = [(TW * i, TW) for i in range(5)]
SO = NP - TT[4][0]
CH = [(0, 16)] + [(16 + 128 * i, 128) for i in range(16)]
NCORES = 8
DEBUG_SITES = False


class Op:
    __slots__ = ("eng", "fn", "deps", "signal", "seq", "is_dma", "sem", "cnt", "prev_same_sem")

    def __init__(self, eng, fn, is_dma):
        self.eng = eng
        self.fn = fn
        self.deps = set()
        self.signal = False
        self.seq = None
        self.is_dma = is_dma
        self.sem = None
        self.cnt = None
        self.prev_same_sem = None


class Prog:
    ENGS = ("pe", "act", "dve", "pool", "sp")

    def __init__(self, nc, n_dma_sems=10):
        self.nc = nc
        self.ops = []
        self.last_w = {}
        self.readers = {}
        self.n_dma_sems = n_dma_sems
        self.dma_rr = {e: 0 for e in self.ENGS}
        self.dma_last = {}
        self.dma_cnt = {}
        self.out_dmas = []
        self.fence_deps = []

    def _record(self, o, r, w):
        deps = set(self.fence_deps)
        raw = set()
        for k in r:
            lw = self.last_w.get(k)
            if lw is not None:
                deps.add(lw)
                raw.add(lw)
            if isinstance(k, tuple) and k[0] == "psb":
                for rd in self.readers.get(k, {}).values():
                    if rd.eng != o.eng:
                        deps.add(rd)
        for k in w:
            lw = self.last_w.get(k)
            if lw is not None:
                deps.add(lw)
            for rd in self.readers.get(k, {}).values():
                deps.add(rd)
        for d in deps:
            if d is o:
                continue
            if (not d.is_dma) and (not o.is_dma) and d.eng == o.eng:
                if o.eng == "pe":
                    continue
            o.deps.add(d)
            d.signal = True
        ek = o.sem if o.is_dma else o.eng
        for k in r:
            self.readers.setdefault(k, {})[ek] = o
        for k in w:
            self.last_w[k] = o
            self.readers[k] = {}
        self.ops.append(o)
        return o

    def op(self, eng, fn, r=(), w=()):
        if DEBUG_SITES:
            import traceback
            try:
                fn._site = [(f.lineno, f.name) for f in traceback.extract_stack(limit=5)[:-1]]
            except Exception:
                pass
        return self._record(Op(eng, fn, False), r, w)

    def dma(self, eng, out, in_, r=(), w=(), is_out=False, **kw):
        def fn(e):
            return e.dma_start(out=out, in_=in_, **kw)
        o = Op(eng, fn, True)
        slot = self.dma_rr[eng]
        self.dma_rr[eng] = (slot + 1) % self.n_dma_sems
        key = (eng, slot)
        o.sem = key
        self.dma_cnt[key] = self.dma_cnt.get(key, 0) + 16
        o.cnt = self.dma_cnt[key]
        o.prev_same_sem = self.dma_last.get(key)
        self.dma_last[key] = o
        o.signal = True
        self._record(o, r, w)
        if is_out:
            self.out_dmas.append(o)
        return o

    def fence(self):
        last = {}
        for o in self.ops:
            k = o.sem if o.is_dma else o.eng
            last[k] = o
        self.fence_deps = list(last.values())
        for d in self.fence_deps:
            d.signal = True

    def emit(self):
        nc = self.nc
        with ExitStack() as st:
            esem = {e: st.enter_context(nc.semaphore("s_" + e)) for e in self.ENGS}
            dsem = {}
            for key in self.dma_cnt:
                dsem[key] = st.enter_context(nc.semaphore("d_%s_%d" % key))
            cnt = {e: 0 for e in self.ENGS}
            for o in self.ops:
                if not o.is_dma and o.signal:
                    cnt[o.eng] += 1
                    o.seq = cnt[o.eng]
            self.stats = dict(cnt)
            fin = Op("sp", None, False)
            fin.deps = set(self.out_dmas)
            streams = {e: [o for o in self.ops if o.eng == e] for e in self.ENGS}
            streams["sp"].append(fin)
            block = st.enter_context(nc.Block())

            def run(e_name, eng):
                waited = {}
                for o in streams[e_name]:
                    need = {}
                    deps = list(o.deps)
                    if o.is_dma and o.prev_same_sem is not None:
                        deps.append(o.prev_same_sem)
                    for d in deps:
                        if d.is_dma:
                            s, v, sk = dsem[d.sem], d.cnt, d.sem
                        else:
                            s, v, sk = esem[d.eng], d.seq, d.eng
                        if waited.get(sk, 0) >= v:
                            continue
                        if sk not in need or need[sk][1] < v:
                            need[sk] = (s, v)
                    for sk, (s, v) in need.items():
                        eng.wait_ge(s, v)
                        waited[sk] = v
                    if o.fn is None:
                        continue
                    try:
                        ins = o.fn(eng)
                    except Exception:
                        print("EMIT FAILURE at op recorded from:", getattr(o.fn, "_site", None))
                        raise
                    if o.is_dma:
                        ins.then_inc(dsem[o.sem], 16)
                    elif o.signal:
                        ins.then_inc(esem[o.eng], 1)

            @block.tensor
            def _(eng):
                run("pe", eng)

            @block.scalar
            def _(eng):
                run("act", eng)

            @block.vector
            def _(eng):
                run("dve", eng)

            @block.gpsimd
            def _(eng):
                run("pool", eng)

            @block.sync
            def _(eng):
                run("sp", eng)


class Arena:
    def __init__(self, t, nbytes):
        self.t = t
        self.n = nbytes
        self.off = 0

    def alloc(self, free_shape, dt=F32, parts=128):
        n = 1
        for s in free_shape:
            n *= s
        isz = 4 if dt == F32 else 2
        sz = (n * isz + 31) // 32 * 32
        assert self.off + sz <= self.n, ("arena overflow", self.off, sz, self.n)
        v = self.t[:, self.off // 4:(self.off + sz) // 4]
        if dt != F32:
            v = v.bitcast(dt)
        v = v[:, 0:n]
        if len(free_shape) == 2:
            v = v.rearrange("p (a b) -> p a b", a=free_shape[0])
        elif len(free_shape) == 3:
            v = v.rearrange("p (a b c) -> p a b c", a=free_shape[0], b=free_shape[1])
        self.off += sz
        self.hw = max(getattr(self, 'hw', 0), self.off)
        return v[0:parts] if parts != 128 else v

    def mark(self):
        return self.off

    def release(self, m):
        self.off = m

    def phase(self, name):
        self.peaks = getattr(self, "peaks", [])
        self.peaks.append((name, getattr(self, "hw", 0)))
        self.hw = self.off


def ck(name, a, b):
    a = max(a, 0)
    return [(name, i) for i in range(a // 128, (b - 1) // 128 + 1)]


def run_pool(make_gen, n_items, width):
    free = list(range(width))
    active = []
    nxt = 0
    while nxt < n_items or active:
        while free and nxt < n_items:
            g = free.pop(0)
            active.append((make_gen(nxt, g), g))
            nxt += 1
        still = []
        for gen, g in active:
            try:
                next(gen)
                still.append((gen, g))
            except StopIteration:
                free.append(g)
        active = still


def run_interleaved(gens):
    gens = list(gens)
    while gens:
        nxt = []
        for g in gens:
            try:
                next(g)
                nxt.append(g)
            except StopIteration:
                pass
        gens = nxt


def build(stage=99):
    nc = bass.Bass("TRN2", target_bir_lowering=False)

    def din(name, shape):
        return nc.dram_tensor(name, list(shape), F32, kind="ExternalInput").ap()

    def dout(name, shape):
        return nc.dram_tensor(name, list(shape), F32, kind="ExternalOutput").ap()

    xp = din("xp", [2048, D]); xs = din("xs", [NS, D]); meta = din("meta", [16, D])
    sdn = din("sdn", [2, NS, 4, 128, 128]); sdnc = din("sdnc", [2, NS, 3, 1536])
    slru = din("slru", [2, NS, 256]); slruc = din("slruc", [2, NS, 3, 256])
    sret = din("sret", [2, NS, 4, 64, 64])
    w1i = din("w1i", [2, D, 2 * DFF]); w1o = din("w1o", [2, DFF, D])
    w2i = din("w2i", [2, D, 2 * DFF]); w2o = din("w2o", [2, DFF, D])
    win = din("win", [2, D, DIN]); wout = din("wout", [2, D, D])
    lwa = din("lwa", [2, 4, 64, 64]); lwx = din("lwx", [2, 4, 64, 64])
    pvec = din("pvec", [186, 128]); abv = din("abv", [1, 16])
    c_ident = din("c_ident", [128, 128]); c_ut = din("c_ut", [128, 128]); c_mask = din("c_mask", [128, 128])
    c_retdec = din("c_retdec", [4, 128, 128]); c_reteg = din("c_reteg", [64, 4, 128])
    c_gamc = din("c_gamc", [128, 8])
    c_retkd = din("c_retkd", [128, 8]); c_cos = din("c_cos", [64, NT]); c_sin = din("c_sin", [64, NT])

    yp = dout("yp", [2048, D]); ys = dout("ys", [NS, D])
    o_dn_p = dout("dn_p", [2, 4, 128, 128]); o_dnc_p = dout("dnc_p", [2, 3, 1536])
    o_lru_p = dout("lru_p", [2, 256]); o_lruc_p = dout("lruc_p", [2, 3, 256])
    o_ret_p = dout("ret_p", [2, 4, 64, 64])
    o_dn_s = dout("dn_s", [2, NS, 4, 128, 128]); o_dnc_s = dout("dnc_s", [2, NS, 3, 1536])
    o_lru_s = dout("lru_s", [2, NS, 256]); o_lruc_s = dout("lruc_s", [2, NS, 3, 256])
    o_ret_s = dout("ret_s", [2, NS, 4, 64, 64])

    P = Prog(nc)
    ARENA_BYTES = 206 * 1024
    with ExitStack() as st:
        arena_t = st.enter_context(nc.sbuf_tensor("arena", [128, ARENA_BYTES // 4], F32))
        A = Arena(arena_t, ARENA_BYTES)
        class _Sub:
            alloc = staticmethod(lambda *a, **k: A.alloc(*a, **k))
            mark = staticmethod(lambda: A.mark())
            release = staticmethod(lambda m: None)
            hw = 0
        AR = _Sub()
        psb = [st.enter_context(nc.psum_tensor("psb%d" % i, [128, 512], F32)) for i in range(8)]
        st_ps = {"b": 0, "q": 0}

        def PB():
            i = st_ps["b"]
            st_ps["b"] = (i + 1) % 4
            return psb[i], [("psb", i)]

        def PQ():
            i = st_ps["q"]
            st_ps["q"] = (i + 1) % 4
            return psb[4 + i][:, 0:128], [("psb", 4 + i)]

        def mm(out, lhsT, rhs, r, w, start=True, stop=True):
            P.op("pe", lambda e: e.matmul(out, lhsT, rhs, start=start, stop=stop), r, w)

        def tr(out, in_, idn, r, w):
            P.op("pe", lambda e: e.transpose(out, in_, idn), r, w)

        def act(out, in_, func, r, w, scale=1.0, bias=0.0):
            P.op("act", lambda e: e.activation(out, in_, func, bias=bias, scale=scale), r, w)

        def tt(out, a, b, op, r, w, eng="dve"):
            P.op(eng, lambda e: e.tensor_tensor(out, a, b, op), r, w)

        def ts(out, a, s1, s2, op0, op1, r, w, eng="dve"):
            if op1 == ALU.pow and s2 == -0.5:
                assert op0 == ALU.add
                P.op("act", lambda e: e.activation(out, a, AF.Ln, bias=epsc[0:out.shape[0], :], scale=1.0), list(r) + ["epsc"], w)
                P.op("act", lambda e: e.activation(out, out, AF.Exp, scale=-0.5), w, w)
                return
            if op1 == ALU.pow and s2 == 0.5:
                assert op0 == ALU.max
                P.op("dve", lambda e: e.tensor_scalar(out, a, 1e-18, None, ALU.max), r, w)
                P.op("act", lambda e: e.activation(out, out, AF.Ln), w, w)
                P.op("act", lambda e: e.activation(out, out, AF.Exp, scale=0.5), w, w)
                return
            if s2 is None:
                P.op(eng, lambda e: e.tensor_scalar(out, a, s1, None, op0), r, w)
            else:
                P.op(eng, lambda e: e.tensor_scalar(out, a, s1, s2, op0, op1), r, w)

        def stt(out, a, s, b, op0, op1, r, w, eng="dve"):
            P.op(eng, lambda e: e.scalar_tensor_tensor(out, a, s, b, op0, op1), r, w)

        def cp(out, in_, r, w, eng="dve"):
            if eng == "act":
                P.op("act", lambda e: e.copy(out, in_), r, w)
            else:
                P.op(eng, lambda e: e.tensor_copy(out, in_), r, w)

        def memset(ap, val, w, eng="dve"):
            P.op(eng, lambda e: e.memset(ap, val), (), w)

        ident = A.alloc([128]); ut = A.alloc([128]); maskT = A.alloc([128])
        ones_f = A.alloc([128]); onesm = A.alloc([128], BF16); ones128b = A.alloc([128], BF16)
        ones1b = A.alloc([128], BF16); o64 = A.alloc([64], parts=64)
        pcol = A.alloc([186]); abb = A.alloc([16]); nA = A.alloc([8])
        retkd = A.alloc([8])
        epsc = A.alloc([1])
        xT = A.alloc([8, NT])
        P.dma("sp", ident, c_ident, w=["ident"])
        P.dma("sp", ut, c_ut, w=["ut"])
        P.dma("sp", maskT, c_mask, w=["maskT"])
        P.dma("sp", retkd, c_retkd, w=["retkd"])
        P.dma("sp", abb, abv.partition_broadcast(128), w=["abb"])
        memset(ones_f, 1.0, ["ones_f"]); memset(onesm, 1.0 / 1024, ["onesm"])
        memset(ones128b, 1.0 / 128, ["ones128b"]); memset(ones1b, 1.0, ["ones1b"])
        memset(o64, 1.0 / 64, ["o64"])
        memset(epsc, EPS, ["epsc"])
        act(nA, abb[:, 0:8], AF.Exp, ["abb"], ["nA"])
        ts(nA, nA, -1.0, None, ALU.mult, None, ["nA"], ["nA"])

        m0 = A.mark()
        stg = A.alloc([2, 1024])
        for (r0, nr) in ((0, 128), (128, 58)):
            P.dma("sp", stg[0:nr, 0, 0:128], pvec[r0:r0 + nr, :], w=[("stg", 0)])
            pq, kq = PQ()
            tr(pq[:, 0:nr], stg[0:nr, 0, 0:128], ident[0:nr, 0:nr], [("stg", 0), "ident"], kq)
            cp(pcol[:, r0:r0 + nr], pq[:, 0:nr], kq, ["pcol"])

        def gain(v, c):
            return pcol[:, v * 8 + c:v * 8 + c + 1]

        srcs = [(xp[i * 128:(i + 1) * 128, :], 128, 16 + 128 * i) for i in range(16)]
        srcs += [(meta[:, :], 16, 0), (xs[:, :], NS, NP)]
        for i, (src, n, col) in enumerate(srcs):
            sl = i % 2
            P.dma("sp", stg[0:n, sl, :], src, w=[("stg", sl)])
            for half in range(2):
                pb, kb = PB()
                for q in range(4):
                    c = half * 4 + q
                    tr(pb[:, q * 128:q * 128 + n], stg[0:n, sl, c * 128:(c + 1) * 128], ident[0:n, 0:n],
                       [("stg", sl), "ident"], kb)
                src_v = pb.rearrange("p (q t) -> p q t", q=4)[:, :, 0:n]
                cp(xT[:, half * 4:half * 4 + 4, col:col + n], src_v, kb, ck("xT", col, col + n),
                   eng=("act" if half else "dve"))
        A.release(m0)
        P.fence()

        def rmsnorm(v, xn):
            m_ = A.mark()
            sq = A.alloc([8, 512], BF16)
            rstd = A.alloc([512])
            for ti, (a, n) in enumerate(TT):
                kx = ck("xT", a, a + n)
                for c in range(8):
                    act(sq[:, c, 0:n], xT[:, c, a:a + n], AF.Square, kx, [("sq", c)])
                pb, kb = PB()
                for c in range(8):
                    mm(pb[:, 0:n], onesm, sq[:, c, 0:n], [("sq", c), "onesm"], kb, start=(c == 0), stop=(c == 7))
                ts(rstd[:, 0:n], pb[:, 0:n], EPS, -0.5, ALU.add, ALU.pow, kb, ["rstd"])
                for c in range(8):
                    stt(xn[:, c, a:a + n], xT[:, c, a:a + n], gain(v, c), rstd[:, 0:n], ALU.mult, ALU.mult,
                        kx + ["rstd", "pcol"], [("xn", ti)])
            A.release(m_)
            P.fence()

        def ffn(l, wi, wo_d, xn):
            m = A.mark()
            hs = A.alloc([6, NT], BF16)
            wg = A.alloc([2, 8, 256], BF16); wu = A.alloc([2, 8, 256], BF16)
            wo = A.alloc([6, 1024], BF16)
            sg = A.alloc([2, 512])
            wiv = wi[l].rearrange("(c p) n -> p c n", p=128)
            slabs = [(0, 6), (6, 12), (12, 18), (18, 22)]
            pair_i = 0
            sgi = 0
            for (j0, j1) in slabs:
                npairs = (j1 - j0) // 2
                for pi in range(npairs):
                    j = j0 + 2 * pi
                    sl = pair_i % 2
                    pair_i += 1
                    P.dma("pool", wg[:, sl], wiv[:, :, j * 128:j * 128 + 256], w=[("wg", sl)])
                    P.dma("pool", wu[:, sl], wiv[:, :, DFF + j * 128:DFF + j * 128 + 256], w=[("wu", sl)])
                    if pi == 0:
                        for jl in range(j1 - j0):
                            P.dma("pool", wo[:, jl, :], wo_d[l, (j0 + jl) * 128:(j0 + jl + 1) * 128, :], w=[("wo", jl)])
                    for jj in range(2):
                        jl = 2 * pi + jj
                        for ti, (a, n) in enumerate(TT):
                            pg, kg = PB()
                            for c in range(8):
                                mm(pg[:, 0:n], wg[:, sl, c, jj * 128:(jj + 1) * 128], xn[:, c, a:a + n],
                                   [("wg", sl), ("xn", ti)], kg, start=(c == 0), stop=(c == 7))
                            pu, ku = PB()
                            for c in range(8):
                                mm(pu[:, 0:n], wu[:, sl, c, jj * 128:(jj + 1) * 128], xn[:, c, a:a + n],
                                   [("wu", sl), ("xn", ti)], ku, start=(c == 0), stop=(c == 7))
                            s = sgi % 2
                            sgi += 1
                            act(sg[:, s, 0:n], pg[:, 0:n], AF.Silu, kg, [("sg", s)])
                            tt(hs[:, jl, a:a + n], sg[:, s, 0:n], pu[:, 0:n], ALU.mult, ku + [("sg", s)], [("hs", jl, ti)])
                nj = j1 - j0
                for ti, (a, n) in enumerate(TT):
                    for dc in range(8):
                        pb, kb = PB()
                        for jl in range(nj):
                            mm(pb[:, 0:n], wo[:, jl, dc * 128:(dc + 1) * 128], hs[:, jl, a:a + n],
                               [("wo", jl), ("hs", jl, ti)], kb, start=(jl == 0), stop=(jl == nj - 1))
                        kx = ck("xT", a, a + n)
                        stt(xT[:, dc, a:a + n], pb[:, 0:n], 0.5, xT[:, dc, a:a + n], ALU.mult, ALU.add, kb + kx, kx)
            A.phase("ffn")
            A.release(m)
            P.fence()

        def store_T(src, n_p, n_f, dst, r, tmp, ktmp):
            pq, kq = PQ()
            tr(pq[0:n_f, 0:n_p], src, ident[0:n_p, 0:n_p], list(r) + ["ident"], kq)
            cp(tmp[0:n_f, 0:n_p], pq[0:n_f, 0:n_p], kq, ktmp, eng="act")
            P.dma("sp", dst, tmp[0:n_f, 0:n_p], r=ktmp, is_out=True)

        def mixer(l, xn):
            m = A.mark()
            oT = A.alloc([4, NT], BF16)
            wring = A.alloc([2, 8, 128], BF16)
            pre = A.alloc([2, 515])
            tmpc = A.alloc([2, 512])
            sqb = A.alloc([2, 512], BF16)
            rnb = A.alloc([1, 512])
            stT = A.alloc([2, 128])
            wiv = win[l].rearrange("(c p) n -> p c n", p=128)
            cnt = {"w": 0, "pre": 0, "tmp": 0, "st": 0}

            def load_w(col0, ncols):
                sl = cnt["w"] % 2
                cnt["w"] += 1
                P.dma("pool", wring[:, sl, :, 0:ncols], wiv[:, :, col0:col0 + ncols], w=[("wr", sl)])
                return sl

            def proj_tile(sl, ncols, ti):
                a, n = TT[ti]
                pb, kb = PB()
                for c in range(8):
                    mm(pb[0:ncols, 0:n], wring[:, sl, c, 0:ncols], xn[:, c, a:a + n], [("wr", sl), ("xn", ti)], kb,
                       start=(c == 0), stop=(c == 7))
                return pb, kb

            def stsl():
                s = cnt["st"] % 2
                cnt["st"] += 1
                return stT[:, s, :], [("stT", s)]

            def apply_wout(row0s, kparts):
                npc = len(row0s)
                mw_ = A.mark()
                womix = A.alloc([4, 1024], BF16)
                for i, r0 in enumerate(row0s):
                    P.dma("pool", womix[0:kparts, i, :], wout[l, r0:r0 + kparts, :], w=[("womix", i)])
                for ti, (a, n) in enumerate(TT):
                    for dc in range(8):
                        pb, kb = PB()
                        for i in range(npc):
                            mm(pb[:, 0:n], womix[0:kparts, i, dc * 128:(dc + 1) * 128], oT[0:kparts, i, a:a + n],
                               [("womix", i)] + ck(("oT", i), a, a + n), kb, start=(i == 0), stop=(i == npc - 1))
                        kx = ck("xT", a, a + n)
                        stt(xT[:, dc, a:a + n], pb[:, 0:n], 1.0, xT[:, dc, a:a + n], ALU.mult, ALU.add, kb + kx, kx)
                A.release(mw_)

            def conv_chunk(col0, wrow, buf3, dstf, nparts=128, bias=None, post=None):
                sl = load_w(col0, nparts)
                last_ps = None
                pbs = {0: proj_tile(sl, nparts, 0)}
                pres = {}

                def evac(ti):
                    a, n = TT[ti]
                    pb, kb = pbs.pop(ti)
                    ps_ = cnt["pre"] % 2
                    cnt["pre"] += 1
                    kp = [("pre", ps_)]
                    if ti == 0:
                        memset(pre[0:nparts, ps_, 0:3], 0.0, kp)
                    else:
                        cp(pre[0:nparts, ps_, 0:3], pre[0:nparts, 1 - ps_, TW:TW + 3], [("pre", 1 - ps_)], kp)
                    cp(pre[0:nparts, ps_, 3:3 + n], pb[0:nparts, 0:n], kb, kp, eng="act")
                    pres[ti] = ps_

                if len(TT) > 1:
                    pbs[1] = proj_tile(sl, nparts, 1)
                evac(0)
                for ti, (a, n) in enumerate(TT):
                    if ti + 2 < len(TT):
                        pbs[ti + 2] = proj_tile(sl, nparts, ti + 2)
                    if ti + 1 < len(TT):
                        evac(ti + 1)
                    ps_ = pres.pop(ti)
                    kp = [("pre", ps_)]
                    npr = n if ti < 4 else SO
                    tsl = cnt["tmp"] % 2
                    cnt["tmp"] += 1
                    kt = [("tmpc", tsl)]
                    t_ = tmpc[0:nparts, tsl, :]
                    ts(t_[:, 0:npr], pre[0:nparts, ps_, 3:3 + npr], wrow(3)[0:nparts], None, ALU.mult, None, kp + ["pcol"], kt)
                    for j in (2, 1, 0):
                        stt(t_[:, 0:npr], pre[0:nparts, ps_, j:j + npr], wrow(j)[0:nparts], t_[:, 0:npr], ALU.mult, ALU.add,
                            kp + kt + ["pcol"], kt)
                    if ti == 4:
                        ts(t_[:, SO:SO + 16], pre[0:nparts, ps_, 3 + SO:3 + SO + 16], wrow(3)[0:nparts], None, ALU.mult, None, kp + ["pcol"], kt)
                        for j in (2, 1, 0):
                            stt(t_[:, SO:SO + 16], buf3[0:nparts, j, :], wrow(j)[0:nparts], t_[:, SO:SO + 16], ALU.mult, ALU.add,
                                kt + ["buf3", "pcol"], kt)
                    if bias is not None:
                        ts(t_[:, 0:n], t_[:, 0:n], bias[0:nparts], None, ALU.add, None, kt + ["pcol"], kt)
                    dstf(ti, a, n, t_[:, 0:n], kt)
                    last_ps = ps_
                if post is not None:
                    for ti, (a, n) in enumerate(TT):
                        post(ti, a, n)
                return last_ps

            def conv_state_out(ps_, nparts, dst_p, dst_s):
                tmp, kt = stsl()
                store_T(pre[0:nparts, ps_, SO:3 + SO], nparts, 3, dst_p, [("pre", ps_)], tmp, kt)
                tmp, kt = stsl()
                store_T(pre[0:nparts, ps_, 3 + SO:3 + SO + 16], nparts, 16, dst_s, [("pre", ps_)], tmp, kt)

            cst48 = A.alloc([128])

            def load_bufT(src48, ncol0, nparts, buf, kbuf):
                P.dma("sp", cst48[0:48, 0:nparts], src48.rearrange("s j n -> (s j) n")[:, ncol0:ncol0 + nparts], w=["cst48"])
                pq, kq = PQ()
                tr(pq[0:nparts, 0:48], cst48[0:48, 0:nparts], ident[0:48, 0:48], ["cst48", "ident"], kq)
                cp(buf[0:nparts].rearrange("p j s -> p s j"), pq[0:nparts, 0:48].rearrange("p (s j) -> p s j", j=3), kq, kbuf)

            mdn = A.mark()
            gtm = A.alloc([18, 4]); btm = A.alloc([18, 4]); Gc = A.alloc([18, 4])
            wab = A.alloc([8, 8], BF16)
            P.dma("sp", o_dnc_s[l, :, 0:2, :], sdnc[l, :, 1:3, :], is_out=True)
            P.dma("pool", wab, wiv[:, :, 1536:1544], w=["wab"])
            blocks = CH + [(NP, NS)]
            memset(gtm, 0.0, ["gtm"]); memset(btm, 0.0, ["btm"])
            for bi, (t0, C) in enumerate(blocks):
                pq, kq = PQ()
                for c in range(8):
                    mm(pq[0:C, 0:8], xn[:, c, t0:t0 + C], wab[:, c, :], ["wab"] + [("xn", i) for i in range(5)], kq,
                       start=(c == 0), stop=(c == 7))
                tt(gtm[0:C, bi, :], pq[0:C, 0:4], abb[0:C, 8 + 4 * l:12 + 4 * l], ALU.add, kq + ["abb"], ["gtm"])
                cp(btm[0:C, bi, :], pq[0:C, 4:8], kq, ["btm"])
            act(gtm, gtm, AF.Exp, ["gtm"], ["gtm"])
            act(gtm, gtm, AF.Ln, ["gtm"], ["gtm"], bias=1.0)
            tt(gtm, gtm, nA[:, 4 * l:4 * l + 4].unsqueeze(1).to_broadcast([128, 18, 4]), ALU.mult, ["gtm", "nA"], ["gtm"])
            act(btm, btm, AF.Sigmoid, ["btm"], ["btm"])
            for bi, (t0, C) in enumerate(blocks):
                pq2, kq2 = PQ()
                mm(pq2[0:C, 0:4], ut[0:C, 0:C], gtm[0:C, bi, :], ["ut", "gtm"], kq2)
                cp(Gc[0:C, bi, :], pq2[0:C, 0:4], kq2, ["Gc"])

            for h in range(4):
                mh = A.mark()
                mrr = AR.mark()
                qT = AR.alloc([NT]); kT = AR.alloc([NT]); vT = A.alloc([NT])
                buf3 = A.alloc([3, 16])
                S = A.alloc([128])
                memset(S, 0.0, ["S"])

                def mk_dst(dst, name, l2scale):
                    def f(ti, a, n, src, kt):
                        kd_ = ck(name, a, a + n)
                        act((dst if l2scale is not None else dst)[:, a:a + n], src, AF.Silu, kt, kd_)
                    return f

                def mk_post(dst, name, l2scale):
                    held = {}

                    def stage1(ti):
                        a, n = TT[ti]
                        kd_ = ck(name, a, a + n)
                        s2 = ti % 2
                        act(sqb[:, s2, 0:n], dst[:, a:a + n], AF.Square, kd_, [("sqb", s2)])
                        pb, kb = PB()
                        mm(pb[:, 0:n], ones1b, sqb[:, s2, 0:n], [("sqb", s2), "ones1b"], kb)
                        held[ti] = (pb, kb)

                    def f(ti, a, n):
                        kd_ = ck(name, a, a + n)
                        if ti == 0:
                            stage1(0)
                        if ti + 1 < len(TT):
                            stage1(ti + 1)
                        pb, kb = held.pop(ti)
                        ts(rnb[:, 0, 0:n], pb[:, 0:n], EPS, -0.5, ALU.add, ALU.pow, kb, [("rnb", 0)])
                        stt(dst[:, a:a + n], dst[:, a:a + n], l2scale, rnb[:, 0, 0:n], ALU.mult, ALU.mult,
                            kd_ + [("rnb", 0)], kd_)
                    return f

                for (which, dst, name, sc) in ((0, qT, "qT", 128.0 ** -0.5), (1, kT, "kT", 1.0), (2, vT, "vT", None)):
                    chn = which * 4 + h
                    load_bufT(sdnc[l], chn * 128, 128, buf3, ["buf3"])
                    wrow = (lambda j, chn=chn: pcol[:, 56 + (l * 4 + j) * 12 + chn:56 + (l * 4 + j) * 12 + chn + 1])
                    ps_ = conv_chunk(chn * 128, wrow, buf3, mk_dst(dst, name, sc),
                                     post=(mk_post(dst, name, sc) if sc is not None else None))
                    conv_state_out(ps_, 128, o_dnc_p[l, :, chn * 128:(chn + 1) * 128], o_dnc_s[l, :, 2, chn * 128:(chn + 1) * 128])

                nwcol = pcol[:, 184 + l:185 + l]

                def dn_post(o_src, ko, t0, C):
                    osb = A_t["osb"][:, 0:C]
                    cp(osb, o_src, ko, ["osb"], eng="act")
                    sq_ = A_t["osq"][:, 0:C]
                    act(sq_, o_src, AF.Square, ko, ["osq"])
                    yield
                    pq, kq = psb[3][:, 0:128], [("psb", 3)]
                    mm(pq[:, 0:C], ones128b, sq_, ["osq", "ones128b"], kq)
                    yield
                    rn_ = A_t["orn"][:, 0:C]
                    ts(rn_, pq[:, 0:C], EPS, -0.5, ALU.add, ALU.pow, kq, ["orn"])
                    yield
                    stt(osb, osb, nwcol, rn_, ALU.mult, ALU.mult, ["osb", "orn", "pcol"], ["osb"])
                    yield
                    cp(oT[:, h, t0:t0 + C], osb, ["osb"], ck(("oT", h), t0, t0 + C))

                A_t = {"osb": A.alloc([128]), "osq": A.alloc([128], BF16), "orn": A.alloc([128])}
                NG = 3
                sets = []
                names = ("gUT", "Gb", "E", "dec", "eGb", "qp", "cols", "kg", "kd", "vtm", "N", "MT", "NT", "R",
                         "Pa", "PTa", "Pb", "PTb")
                RRN = ("kg", "kd", "vtm", "N", "MT", "NT", "R", "Pa", "PTa", "Pb", "PTb")
                S0s = A.alloc([16, 128])
                P.dma("sp", S0s, sdn[l, :, h].rearrange("s d e -> d s e"), w=["S0s"])
                sets.append({k: (AR.alloc([128]) if k in RRN else A.alloc([128])) for k in names})
                msets = A.mark()
                for g in range(1, NG):
                    sets.append({k: (AR.alloc([128]) if k in RRN else A.alloc([128])) for k in names})
                for T_ in sets:
                    T_["nW"], T_["U"] = T_["gUT"], T_["NT"]
                ALIAS = {"nW": "gUT", "U": "NT"}

                def prep(ci, g):
                    t0, C = CH[ci]
                    T_ = sets[g]
                    K = lambda n_: [("set", g, ALIAS.get(n_, n_))]
                    RR = lambda n_: T_[n_]
                    bank = psb[4 + g]
                    kbk = [("psb", 4 + g)]
                    kk = ck("kT", t0, t0 + C); kqk = ck("qT", t0, t0 + C); kv = ck("vT", t0, t0 + C)
                    gcol = gtm[0:C, ci, h:h + 1]; bcol = btm[0:C, ci, h:h + 1]; Gcol = Gc[0:C, ci, h:h + 1]
                    kTr = kT; qTr = qT
                    ts(T_["gUT"][0:C, 0:C], ut[0:C, 0:C], gcol, None, ALU.mult, None, ["ut", "gtm"], K("gUT"))
                    mm(bank[:, 0:C], ones_f[0:C, 0:128], T_["gUT"][0:C, 0:C], K("gUT") + ["ones_f"], kbk)
                    yield
                    cp(T_["Gb"][:, 0:C], bank[:, 0:C], kbk, K("Gb"))
                    stt(T_["E"][0:C, 0:C], bank[0:C, 0:C], Gcol, maskT[0:C, 0:C], ALU.subtract, ALU.add,
                        kbk + ["Gc", "maskT"], K("E"))
                    yield
                    act(T_["dec"][0:C, 0:C], T_["E"][0:C, 0:C], AF.Exp, K("E"), K("dec"))
                    act(T_["eGb"][:, 0:C], T_["Gb"][:, 0:C], AF.Exp, K("Gb"), K("eGb"))
                    cols = T_["cols"]
                    act(cols[0:C, 0:1], Gcol, AF.Exp, ["Gc"], K("cols"))
                    act(cols[0:C, 1:2], Gcol, AF.Exp, ["Gc"] + K("Gb"), K("cols"), scale=-1.0, bias=T_["Gb"][0:C, C - 1:C])
                    tr(bank[0:C, 0:128], kT[:, t0:t0 + C], ident, kk + ["ident"], kbk)
                    tr(bank[0:C, 128:256], vT[:, t0:t0 + C], ident, kv + ["ident"], kbk)
                    mm(bank[0:C, 256:256 + C], kTr[:, t0:t0 + C], kTr[:, t0:t0 + C], kk, kbk)
                    mm(bank[0:C, 384:384 + C], kTr[:, t0:t0 + C], qTr[:, t0:t0 + C], kk + kqk, kbk)
                    yield
                    ts(RR("kg")[0:C, :], bank[0:C, 0:128], cols[0:C, 0:1], None, ALU.mult, None, kbk + K("cols"), K("kg"))
                    ts(RR("kd")[0:C, :], bank[0:C, 0:128], cols[0:C, 1:2], None, ALU.mult, None, kbk + K("cols"), K("kd"))
                    cp(RR("vtm")[0:C, :], bank[0:C, 128:256], kbk, K("vtm"))
                    stt(RR("N")[0:C, 0:C], bank[0:C, 256:256 + C], bcol, T_["dec"][0:C, 0:C], ALU.mult, ALU.mult,
                        kbk + ["btm"] + K("dec"), K("N"))
                    tt(T_["E"][0:C, 0:C], T_["dec"][0:C, 0:C], ident[0:C, 0:C], ALU.add, K("dec") + ["ident"], K("E"), eng="pool")
                    tt(T_["qp"][:, 0:C], qT[:, t0:t0 + C], T_["eGb"][:, 0:C], ALU.mult, kqk + K("eGb"), K("qp"), eng="pool")
                    yield
                    tt(RR("MT")[0:C, 0:C], bank[0:C, 384:384 + C], T_["E"][0:C, 0:C], ALU.mult, kbk + K("E"), K("MT"))
                    tt(RR("R")[0:C, 0:C], ident[0:C, 0:C], T_["N"][0:C, 0:C], ALU.subtract, K("N") + ["ident"], K("R"))
                    yield
                    tr(bank[0:C, 0:C], T_["N"][0:C, 0:C], ident[0:C, 0:C], K("N") + ["ident"], kbk)
                    yield
                    cp(RR("NT")[0:C, 0:C], bank[0:C, 0:C], kbk, K("NT"))
                    yield
                    J = 7 if C == 128 else 4
                    Pc, PTc, kPc, kPTc = "N", "NT", K("N"), K("NT")
                    for j in range(1, J):
                        Pn, PTn = ("Pa", "PTa") if j % 2 else ("Pb", "PTb")
                        kPn, kPTn = K(Pn), K(PTn)
                        mm(bank[0:C, 0:C], RR(Pc)[0:C, 0:C], RR(PTc)[0:C, 0:C], kPc + kPTc, kbk)
                        if j < J - 1:
                            mm(bank[0:C, 128:128 + C], RR(PTc)[0:C, 0:C], RR(Pc)[0:C, 0:C], kPc + kPTc, kbk)
                        if j >= 2:
                            mm(bank[0:C, 256:256 + C], RR(PTc)[0:C, 0:C], RR("R")[0:C, 0:C], kPTc + K("R"), kbk)
                        yield
                        cp(RR(PTn)[0:C, 0:C], bank[0:C, 0:C], kbk, kPTn)
                        if j < J - 1:
                            cp(RR(Pn)[0:C, 0:C], bank[0:C, 128:128 + C], kbk, kPn)
                        if j >= 2:
                            tt(RR("R")[0:C, 0:C], T_["R"][0:C, 0:C], bank[0:C, 256:256 + C], ALU.add, K("R") + kbk, K("R"))
                        Pc, PTc, kPc, kPTc = Pn, PTn, kPn, kPTn
                        yield
                    mm(bank[0:C, 0:C], RR(PTc)[0:C, 0:C], RR("R")[0:C, 0:C], kPTc + K("R"), kbk)
                    yield
                    tt(RR("R")[0:C, 0:C], T_["R"][0:C, 0:C], bank[0:C, 0:C], ALU.add, K("R") + kbk, K("R"))
                    yield
                    mm(bank[:, 0:C], RR("kg")[0:C, :], RR("R")[0:C, 0:C], K("kg") + K("R"), kbk)
                    yield
                    ts(T_["nW"][:, 0:C], bank[:, 0:C], -1.0, None, ALU.mult, None, kbk, K("nW"))

                turn = {"i": 0}

                def seq(ci, g):
                    t0, C = CH[ci]
                    T_ = sets[g]
                    K = lambda n_: [("set", g, ALIAS.get(n_, n_))]
                    RR = lambda n_: T_[n_]
                    while turn["i"] != ci:
                        yield
                    pu, kpu = psb[0][:, 0:128], [("psb", 0)]
                    po, kpo = psb[1][:, 0:128], [("psb", 1)]
                    psn, kps = psb[2][:, 0:128], [("psb", 2)]
                    mm(pu[0:C, :], RR("R")[0:C, 0:C], RR("vtm")[0:C, :], K("R") + K("vtm"), kpu, start=True, stop=False)
                    mm(pu[0:C, :], T_["nW"][:, 0:C], S, K("nW") + ["S"], kpu, start=False, stop=True)
                    mm(po[:, 0:C], S, T_["qp"][:, 0:C], ["S"] + K("qp"), kpo, start=True, stop=False)
                    yield
                    ts(RR("U")[0:C, :], pu[0:C, :], btm[0:C, ci, h:h + 1], None, ALU.mult, None, kpu + ["btm"], K("U"))
                    yield
                    mm(po[:, 0:C], RR("U")[0:C, :], RR("MT")[0:C, 0:C], K("U") + K("MT"), kpo, start=False, stop=True)
                    mm(psn, RR("kd")[0:C, :], RR("U")[0:C, :], K("kd") + K("U"), kps)
                    yield
                    stt(S, S, T_["eGb"][:, C - 1:C], psn, ALU.mult, ALU.add, ["S"] + K("eGb") + kps, ["S"])
                    yield from dn_post(po[:, 0:C], kpo, t0, C)
                    turn["i"] = ci + 1

                def chunk_gen(ci, g):
                    yield from prep(ci, g)
                    yield from seq(ci, g)

                run_pool(chunk_gen, 17, NG)
                P.dma("sp", o_dn_p[l, h], S, r=["S"], is_out=True)

                T0 = sets[0]
                K0 = lambda n_: [("set", 0, ALIAS.get(n_, n_))]
                pass
                pass
                T0 = dict(T0)
                for nm_ in ("U", "MT", "N", "kg", "kd"):
                    T0[nm_] = A.alloc([128])
                K0 = lambda n_: [("sset", n_)] if n_ in ("U", "MT", "N", "kg", "kd") else [("set", 0, ALIAS.get(n_, n_))]
                sc_ = slice(NP, NT)
                bi = 17
                ksq = ck("qT", NP, NT); ksk = ck("kT", NP, NT); ksv = ck("vT", NP, NT)

                def rowbc(dst, val_col, kval, kdst):
                    ts(T0["gUT"][0:16, 0:16], ident[0:16, 0:16], val_col, None, ALU.mult, None, ["ident"] + kval, K0("gUT"))
                    pq, kq = PQ()
                    mm(pq[:, 0:16], ones_f[0:16, 0:128], T0["gUT"][0:16, 0:16], K0("gUT") + ["ones_f"], kq)
                    cp(dst, pq[:, 0:16], kq, kdst, eng="act")

                act(T0["cols"][0:16, 0:1], gtm[0:16, bi, h:h + 1], AF.Exp, ["gtm"], K0("cols"))
                eGbc = T0["eGb"][:, 0:16]; bbc = T0["Gb"][:, 0:16]
                rowbc(eGbc, T0["cols"][0:16, 0:1], K0("cols"), K0("eGb"))
                rowbc(bbc, btm[0:16, bi, h:h + 1], ["btm"], K0("Gb"))
                kq_ = T0["qp"][:, 0:32].rearrange("p (s two) -> p s two", two=2)
                cp(kq_[:, :, 0], kT[:, sc_], ksk, K0("qp"))
                cp(kq_[:, :, 1], qT[:, sc_], ksq, K0("qp"))
                pks, kpks = PQ()
                for s in range(16):
                    mm(pks[:, 2 * s:2 * s + 2], S0s[:, s, :], kq_[:, s, :], ["S0s"] + K0("qp"), kpks)
                pksv = pks[:, 0:32].rearrange("p (s two) -> p s two", two=2)
                tt(T0["E"][:, 0:16], qT[:, sc_], kT[:, sc_], ALU.mult, ksq + ksk, K0("E"))
                pqk, kpqk = PQ()
                mm(pqk[:, 0:16], ones_f, T0["E"][:, 0:16], K0("E") + ["ones_f"], kpqk)
                t1 = T0["dec"][:, 0:16]
                tt(t1, pksv[:, :, 0], eGbc, ALU.mult, kpks + K0("eGb"), K0("dec"))
                tt(t1, vT[:, sc_], t1, ALU.subtract, ksv + K0("dec"), K0("dec"))
                UTs = T0["U"][:, 0:16]
                tt(UTs, t1, bbc, ALU.mult, K0("dec") + K0("Gb"), K0("U"))
                t3 = T0["MT"][:, 0:16]
                tt(t3, pksv[:, :, 1], eGbc, ALU.mult, kpks + K0("eGb"), K0("MT"))
                t4 = T0["N"][:, 0:16]
                tt(t4, pqk[:, 0:16], UTs, ALU.mult, kpqk + K0("U"), K0("N"))
                tt(t3, t3, t4, ALU.add, K0("MT") + K0("N"), K0("MT"))
                for _ in dn_post(t3, K0("MT"), NP, NS):
                    pass
                pk, kpk = PQ()
                tr(pk[0:16, :], kT[:, sc_], ident, ksk + ["ident"], kpk)
                cp(T0["kg"][0:16, :], pk[0:16, :], kpk, K0("kg"), eng="act")
                pU, kpU = PQ()
                tr(pU[0:16, :], UTs, ident, K0("U") + ["ident"], kpU)
                cp(T0["kd"][0:16, :], pU[0:16, :], kpU, K0("kd"), eng="act")
                kmask = A.alloc([8, 128])
                for hf in range(2):
                    tt(kmask[0:16], T0["kg"][0:16, :].unsqueeze(1).to_broadcast([16, 8, 128]),
                       ident[0:16, hf * 8:hf * 8 + 8].unsqueeze(2).to_broadcast([16, 8, 128]), ALU.mult,
                       K0("kg") + ["ident"], ["kmask"])
                    for sg_ in range(2):
                        pb, kb = PB()
                        for s4 in range(4):
                            s8 = sg_ * 4 + s4
                            mm(pb[:, s4 * 128:(s4 + 1) * 128], kmask[0:16, s8, :], T0["kd"][0:16, :], ["kmask"] + K0("kd"), kb)
                        for s4 in range(4):
                            s8 = sg_ * 4 + s4
                            s = hf * 8 + s8
                            stt(S0s[:, s, :], S0s[:, s, :], eGbc[:, s:s + 1], pb[:, s4 * 128:(s4 + 1) * 128], ALU.mult, ALU.add,
                                ["S0s"] + K0("eGb") + kb, ["S0s"])
                P.dma("sp", o_dn_s[l, :, h].rearrange("s d e -> d s e"), S0s, r=["S0s"], is_out=True)
                sl = load_w(1544 + h * 128, 128)
                for ti, (a, n) in enumerate(TT):
                    pb, kb = proj_tile(sl, 128, ti)
                    s2 = ti % 2
                    act(tmpc[:, s2, 0:n], pb[:, 0:n], AF.Silu, kb, [("tmpc", s2)])
                    ko = ck(("oT", h), a, a + n)
                    tt(oT[:, h, a:a + n], oT[:, h, a:a + n], tmpc[:, s2, 0:n], ALU.mult, ko + [("tmpc", s2)], ko)
                A.release(mh)
                AR.release(mrr)
                pass
            P.fence()
            apply_wout([0, 128, 256, 384], 128)
            A.phase("dn")
            A.release(mdn)
            P.fence()
            if stage < 3:
                A.release(m)
                return

            mlru = A.mark()
            P.dma("sp", o_lruc_s[l, :, 0:2, :], slruc[l, :, 1:3, :], is_out=True)
            hst = A.alloc([256])
            P.dma("sp", hst[0:16, :], slru[l], w=["hst"])
            bda = A.alloc([2, 128]); bdx = A.alloc([2, 128])
            memset(bda, 0.0, ["bda"]); memset(bdx, 0.0, ["bdx"])
            for c in range(2):
                for b2 in range(2):
                    nb = c * 2 + b2
                    P.dma("sp", bda[b2 * 64:(b2 + 1) * 64, c, b2 * 64:(b2 + 1) * 64], lwa[l, nb], r=["bda"], w=["bda"])
                    P.dma("sp", bdx[b2 * 64:(b2 + 1) * 64, c, b2 * 64:(b2 + 1) * 64], lwx[l, nb], r=["bdx"], w=["bdx"])
            cA = A.alloc([2])
            act(cA, pcol[:, 180 + 2 * l:182 + 2 * l], AF.Exp, ["pcol"], ["cA"], scale=-1.0)
            act(cA, cA, AF.Ln, ["cA"], ["cA"], bias=1.0)
            ts(cA, cA, -8.0, None, ALU.mult, None, ["cA"], ["cA"])
            for c in range(2):
                mc = A.mark()
                xl = A.alloc([NT]); av = A.alloc([NT]); bv = A.alloc([NT]); hv = xl
                buf3 = A.alloc([3, 16]); h0T = A.alloc([16])
                gt = A.alloc([2, 512])
                load_bufT(slruc[l], c * 128, 128, buf3, ["buf3"])
                pq, kq = PQ()
                tr(pq[:, 0:16], hst[0:16, c * 128:(c + 1) * 128], ident[0:16, 0:16], ["hst", "ident"], kq)
                cp(h0T, pq[:, 0:16], kq, ["h0T"])

                def lru_dst(ti, a, n, src, kt):
                    kx = ck("xl", a, a + n)
                    cp(xl[:, a:a + n], src, kt, kx, eng="act")
                    pr, kr = PB()
                    mm(pr[:, 0:n], bda[:, c, :], xl[:, a:a + n], ["bda"] + kx, kr)
                    pi_, ki = PB()
                    mm(pi_[:, 0:n], bdx[:, c, :], xl[:, a:a + n], ["bdx"] + kx, ki)
                    g0 = gt[:, 0, 0:n]; g1 = gt[:, 1, 0:n]
                    act(g0, pr[:, 0:n], AF.Sigmoid, kr + ["pcol"], [("gt", 0)], bias=pcol[:, 172 + 2 * l + c:173 + 2 * l + c])
                    act(g1, pi_[:, 0:n], AF.Sigmoid, ki + ["pcol"], [("gt", 1)], bias=pcol[:, 176 + 2 * l + c:177 + 2 * l + c])
                    ka = ck("av", a, a + n); kb_ = ck("bv", a, a + n)
                    act(av[:, a:a + n], g0, AF.Exp, [("gt", 0), "cA"], ka, scale=cA[:, c:c + 1])
                    tt(g0, av[:, a:a + n], av[:, a:a + n], ALU.mult, ka, [("gt", 0)])
                    ts(g0, g0, -1.0, 1.0, ALU.mult, ALU.add, [("gt", 0)], [("gt", 0)])
                    ts(g0, g0, 0.0, 0.5, ALU.max, ALU.pow, [("gt", 0)], [("gt", 0)])
                    tt(g1, g1, xl[:, a:a + n], ALU.mult, [("gt", 1)] + kx, [("gt", 1)])
                    tt(bv[:, a:a + n], g0, g1, ALU.mult, [("gt", 0), ("gt", 1)], kb_)

                wrow = (lambda j, c=c: pcol[:, 152 + (l * 4 + j) * 2 + c:152 + (l * 4 + j) * 2 + c + 1])
                ps_ = conv_chunk(2056 + c * 128, wrow, buf3, lru_dst, bias=pcol[:, 168 + 2 * l + c:169 + 2 * l + c])
                conv_state_out(ps_, 128, o_lruc_p[l, :, c * 128:(c + 1) * 128], o_lruc_s[l, :, 2, c * 128:(c + 1) * 128])
                kall_a = ck("av", 0, NT); kall_b = ck("bv", 0, NT)
                P.op("dve", lambda e, av=av, bv=bv, hv=hv: e.tensor_tensor_scan(hv[:, 0:NP], av[:, 0:NP], bv[:, 0:NP], 0.0,
                                                                                 ALU.mult, ALU.add), kall_a + kall_b + ck("xl", 0, NT), ck("xl", 0, NT) + ["hv"])
                tt(hv[:, NP:NT], av[:, NP:NT], h0T, ALU.mult, kall_a + ["h0T"] + ck("xl", NP, NT), ["hvs"] + ck("xl", NP, NT))
                tt(hv[:, NP:NT], hv[:, NP:NT], bv[:, NP:NT], ALU.add, kall_b + ["hvs"], ["hvs"])
                sl = load_w(2312 + c * 128, 128)
                for ti, (a, n) in enumerate(TT):
                    pb, kb = proj_tile(sl, 128, ti)
                    y_ = gt[:, 0, 0:n]; u_ = gt[:, 1, 0:n]
                    cp(y_, pb[:, 0:n], kb, [("gt", 0)], eng="act")
                    tt(u_, y_, y_, ALU.mult, [("gt", 0)], [("gt", 1)])
                    ts(u_, u_, 0.044715, 1.0, ALU.mult, ALU.add, [("gt", 1)], [("gt", 1)])
                    tt(u_, u_, y_, ALU.mult, [("gt", 0), ("gt", 1)], [("gt", 1)])
                    act(u_, u_, AF.Sigmoid, [("gt", 1)], [("gt", 1)], scale=1.5957691216057308)
                    tt(u_, u_, y_, ALU.mult, [("gt", 0), ("gt", 1)], [("gt", 1)])
                    tt(oT[:, c, a:a + n], hv[:, a:a + n], u_, ALU.mult, ["hv", "hvs", ("gt", 1)], ck(("oT", c), a, a + n))
                tmp, kt = stsl()
                store_T(hv[:, NP - 1:NP], 128, 1, o_lru_p[l:l + 1, c * 128:(c + 1) * 128], ["hv"], tmp, kt)
                tmp, kt = stsl()
                store_T(hv[:, NP:NT], 128, 16, o_lru_s[l, :, c * 128:(c + 1) * 128], ["hvs"], tmp, kt)
                A.release(mc)
                pass
            P.fence()
            apply_wout([512, 640], 128)
            A.phase("lru")
            A.release(mlru)
            P.fence()
            if stage < 4:
                A.release(m)
                return

            mret = A.mark()
            cosr = A.alloc([2, 512]); sinr = A.alloc([2, 512])
            rcnt = {"i": 0}
            o64bd = A.alloc([128])
            memset(o64bd, 0.0, ["o64bd"])
            memset(o64bd[0:64, 0:64], 1.0 / 64, ["o64bd"])
            memset(o64bd[64:128, 64:128], 1.0 / 64, ["o64bd"])
            gamc = A.alloc([8])
            P.dma("sp", gamc, c_gamc, w=["gamc"])
            for hp in range(2):
                mh = A.mark()
                mrr = AR.mark()
                rq = AR.alloc([NT]); rk = AR.alloc([NT]); rv = A.alloc([NT])
                decT = A.alloc([2, 128])
                reteg = A.alloc([128])
                for hh in range(2):
                    P.dma("sp", decT[:, hh, :], c_retdec[2 * hp + hh], w=["decT"])
                    P.dma("sp", reteg[hh * 64:(hh + 1) * 64, :], c_reteg[:, 2 * hp + hh, :], w=["reteg"])
                Sr = A.alloc([128])
                memset(Sr, 0.0, ["Sr"])
                tq_ = A.alloc([2, 512])
                gC = lambda j: gamc[:, 4 * hp + j:4 * hp + j + 1]
                for (which, dst, name) in ((0, rq, "rq"), (1, rk, "rk")):
                    c0 = 2568 + which * 256 + hp * 128
                    sl = cnt["w"] % 2
                    cnt["w"] += 1
                    sl2 = cnt["w"] % 2
                    cnt["w"] += 1
                    P.dma("pool", wring[:, sl, :, 0:128], wiv[:, :, c0:c0 + 128], w=[("wr", sl)])
                    for hh in range(2):
                        P.dma("pool", wring[:, sl2, :, hh * 64:hh * 64 + 32], wiv[:, :, c0 + hh * 64 + 32:c0 + hh * 64 + 64],
                              r=[("wr", sl2)], w=[("wr", sl2)])
                        P.dma("pool", wring[:, sl2, :, hh * 64 + 32:hh * 64 + 64], wiv[:, :, c0 + hh * 64:c0 + hh * 64 + 32],
                              r=[("wr", sl2)], w=[("wr", sl2)])
                    for ti, (a, n) in enumerate(TT):
                        pb, kb = proj_tile(sl, 128, ti)
                        pb2, kb2 = proj_tile(sl2, 128, ti)
                        s2 = ti % 2
                        rs = rcnt["i"] % 2
                        rcnt["i"] += 1
                        for hh in range(2):
                            P.dma("sp", cosr[hh * 64:(hh + 1) * 64, rs, 0:n], c_cos[:, a:a + n], w=[("cosr", rs)])
                            P.dma("sp", sinr[hh * 64:(hh + 1) * 64, rs, 0:n], c_sin[:, a:a + n], w=[("sinr", rs)])
                        tt(tq_[:, s2, 0:n], pb[:, 0:n], cosr[:, rs, 0:n], ALU.mult, kb + [("cosr", rs)], [("tq", s2)])
                        tt(dst[:, a:a + n], pb2[:, 0:n], sinr[:, rs, 0:n], ALU.mult, kb2 + [("sinr", rs)], ck(name, a, a + n))
                        tt(dst[:, a:a + n], dst[:, a:a + n], tq_[:, s2, 0:n], ALU.add, ck(name, a, a + n) + [("tq", s2)],
                           ck(name, a, a + n))
                sl = load_w(3080 + hp * 128, 128)
                for ti, (a, n) in enumerate(TT):
                    pb, kb = proj_tile(sl, 128, ti)
                    cp(rv[:, a:a + n], pb[:, 0:n], kb, ck("rv", a, a + n), eng="act")

                RNG = 3
                rsets = [{k: A.alloc([128]) for k in ("qp", "ktm", "vtm", "MT0", "MT1", "o", "cen")} for _ in range(RNG)]
                for gi_, T_ in enumerate(rsets):
                    T_["sq"] = T_["o"]
                    T_["rn"] = T_["o"]
                    T_["qbd"] = AR.alloc([2, 128])
                    memset(T_["qbd"], 0.0, [("rset", gi_, "qbd")])
                RAL = {"sq": "o", "rn": "o"}

                def ret_post(o_src, ko, t0, C, T_, K, bank, kbk):
                    cp(T_["o"][:, 0:C], o_src, ko, K("o"))
                    yield
                    mm(bank[:, 0:C], o64bd, T_["o"][:, 0:C], K("o") + ["o64bd"], kbk)
                    yield
                    tt(T_["cen"][:, 0:C], T_["o"][:, 0:C], bank[:, 0:C], ALU.subtract, K("o") + kbk, K("cen"))
                    tt(T_["sq"][:, 0:C], T_["cen"][:, 0:C], T_["cen"][:, 0:C], ALU.mult, K("cen"), K("sq"))
                    yield
                    mm(bank[:, 0:C], o64bd, T_["sq"][:, 0:C], K("sq") + ["o64bd"], kbk)
                    yield
                    ts(T_["rn"][:, 0:C], bank[:, 0:C], EPS, -0.5, ALU.add, ALU.pow, kbk, K("rn"))
                    yield
                    tt(oT[:, hp, t0:t0 + C], T_["cen"][:, 0:C], T_["rn"][:, 0:C], ALU.mult, K("cen") + K("rn"),
                       ck(("oT", hp), t0, t0 + C))

                rturn = {"i": 0}

                def ret_chunk(ci, g):
                    t0, C = CH[ci]
                    T_ = rsets[g]
                    K = lambda n_, g=g: [("rset", g, RAL.get(n_, n_))]
                    bank = psb[4 + g]
                    kbk = [("psb", 4 + g)]
                    kq_ = ck("rq", t0, t0 + C); kk_ = ck("rk", t0, t0 + C); kv_ = ck("rv", t0, t0 + C)
                    tt(T_["qp"][:, 0:C], rq[:, t0:t0 + C], reteg[:, 0:C], ALU.mult, kq_ + ["reteg"], K("qp"), eng="pool")
                    tr(bank[0:C, 0:128], rk[:, t0:t0 + C], ident, kk_ + ["ident"], kbk)
                    tr(bank[0:C, 128:256], rv[:, t0:t0 + C], ident, kv_ + ["ident"], kbk)
                    for hh in range(2):
                        b0 = hh * 64
                        cp(T_["qbd"][b0:b0 + 64, hh, 0:C], rq[b0:b0 + 64, t0:t0 + C], kq_, K("qbd"), eng="act")
                    mm(bank[0:C, 256:256 + 2 * C], rk[:, t0:t0 + C], T_["qbd"][:, :, 0:C], kk_ + K("qbd"), kbk)
                    yield
                    jc = 0 if C == 128 else 1
                    for hh in range(2):
                        h = 2 * hp + hh
                        kcol = retkd[0:C, 2 * h + jc:2 * h + jc + 1]
                        ts(T_["ktm"][0:C, hh * 64:(hh + 1) * 64], bank[0:C, hh * 64:(hh + 1) * 64], kcol, None, ALU.mult, None,
                           kbk + ["retkd"], K("ktm"))
                    cp(T_["vtm"][0:C, :], bank[0:C, 128:256], kbk, K("vtm"))
                    for hh in range(2):
                        tt(T_["MT%d" % hh][0:C, 0:C], bank[0:C, 256 + hh * C:256 + (hh + 1) * C], decT[0:C, hh, 0:C], ALU.mult,
                           kbk + ["decT"], K("MT%d" % hh))
                    yield
                    while rturn["i"] != ci:
                        yield
                    mm(bank[:, 0:C], Sr, T_["qp"][:, 0:C], ["Sr"] + K("qp"), kbk, start=True, stop=False)
                    for hh in range(2):
                        b0 = hh * 64
                        mm(bank[b0:b0 + 64, 0:C], T_["vtm"][0:C, b0:b0 + 64], T_["MT%d" % hh][0:C, 0:C],
                           K("vtm") + K("MT%d" % hh), kbk, start=False, stop=True)
                    for hh in range(2):
                        b0 = hh * 64
                        mm(bank[b0:b0 + 64, 256 + b0:320 + b0], T_["ktm"][0:C, b0:b0 + 64], T_["vtm"][0:C, b0:b0 + 64],
                           K("ktm") + K("vtm"), kbk)
                    yield
                    for hh in range(2):
                        b0 = hh * 64
                        stt(Sr[b0:b0 + 64, b0:b0 + 64], Sr[b0:b0 + 64, b0:b0 + 64], gC(jc)[b0:b0 + 64], bank[b0:b0 + 64, 256 + b0:320 + b0],
                            ALU.mult, ALU.add, ["Sr", "gamc"] + kbk, ["Sr"])
                    rturn["i"] = ci + 1
                    yield from ret_post(bank[:, 0:C], kbk, t0, C, T_, K, bank, kbk)

                run_pool(ret_chunk, 17, RNG)
                for hh in range(2):
                    P.dma("sp", o_ret_p[l, 2 * hp + hh], Sr[hh * 64:(hh + 1) * 64, hh * 64:(hh + 1) * 64], r=["Sr"], is_out=True)

                T_ = rsets[0]
                K = lambda n_: [("rset", 0, RAL.get(n_, n_))]
                sc_ = slice(NP, NT)
                ksq = ck("rq", NP, NT); ksk = ck("rk", NP, NT); ksv = ck("rv", NP, NT)
                S0r = cosr.rearrange("p a (b c) -> p (a b) c", c=64)
                Snr = sinr.rearrange("p a (b c) -> p (a b) c", c=64)
                KS0 = [("cosr", 0), ("cosr", 1)]
                KSN = [("sinr", 0), ("sinr", 1)]
                for hh in range(2):
                    P.dma("sp", S0r[hh * 64:(hh + 1) * 64], sret[l, :, 2 * hp + hh].rearrange("s d e -> d s e"), w=KS0)
                pqs2 = [PQ(), PQ()]
                for hh in range(2):
                    b0 = hh * 64
                    pqs, kpqs = pqs2[hh]
                    for s_ in range(16):
                        mm(pqs[b0:b0 + 64, s_:s_ + 1], S0r[b0:b0 + 64, s_, :], rq[b0:b0 + 64, NP + s_:NP + s_ + 1], KS0 + ksq, kpqs)
                tt(T_["sq"][:, 0:16], rq[:, sc_], rk[:, sc_], ALU.mult, ksq + ksk, K("sq"))
                pqk, kpqk = PQ()
                mm(pqk[:, 0:16], o64bd, T_["sq"][:, 0:16], K("sq") + ["o64bd"], kpqk)
                stt(T_["cen"][:, 0:16], pqk[:, 0:16], 8.0, rv[:, sc_], ALU.mult, ALU.mult, kpqk + ksv, K("cen"))
                for hh in range(2):
                    b0 = hh * 64
                    pqs, kpqs = pqs2[hh]
                    stt(T_["o"][b0:b0 + 64, 0:16], pqs[b0:b0 + 64, 0:16], gC(2)[b0:b0 + 64], T_["cen"][b0:b0 + 64, 0:16], ALU.mult, ALU.add,
                        kpqs + K("cen") + ["gamc"], K("o"))
                T1 = rsets[1]
                K1 = lambda n_: [("rset", 1, RAL.get(n_, n_))]
                for _ in ret_post(T_["o"][:, 0:16], K("o"), NP, NS, T1, K1, psb[7], [("psb", 7)]):
                    pass
                pk, kpk = PQ()
                tr(pk[0:16, 0:128], rk[:, sc_], ident, ksk + ["ident"], kpk)
                ts(T_["ktm"][0:16, :], pk[0:16, 0:128], 0.125, None, ALU.mult, None, kpk, K("ktm"))
                pv, kpv = PQ()
                tr(pv[0:16, 0:128], rv[:, sc_], ident, ksv + ["ident"], kpv)
                cp(T_["vtm"][0:16, :], pv[0:16, 0:128], kpv, K("vtm"), eng="act")
                kmask = AR.alloc([8, 128])
                for sg_ in range(2):
                    tt(kmask[0:16], T_["ktm"][0:16, :].unsqueeze(1).to_broadcast([16, 8, 128]),
                       ident[0:16, sg_ * 8:sg_ * 8 + 8].unsqueeze(2).to_broadcast([16, 8, 128]), ALU.mult,
                       K("ktm") + ["ident"], ["kmask"])
                    pb, kb = PB()
                    for hh in range(2):
                        b0 = hh * 64
                        for s8 in range(8):
                            mm(pb[b0:b0 + 64, s8 * 64:(s8 + 1) * 64], kmask[0:16, s8, b0:b0 + 64], T_["vtm"][0:16, b0:b0 + 64],
                               ["kmask"] + K("vtm"), kb)
                    stt(Snr[:, sg_ * 8:(sg_ + 1) * 8, :], S0r[:, sg_ * 8:(sg_ + 1) * 8, :], gC(2),
                        pb.rearrange("p (s e) -> p s e", e=64), ALU.mult, ALU.add, KS0 + kb + ["gamc"], KSN)
                for hh in range(2):
                    P.dma("sp", o_ret_s[l, :, 2 * hp + hh].rearrange("s d e -> d s e"), Snr[hh * 64:(hh + 1) * 64], r=KSN, is_out=True)
                sl = load_w(3336 + hp * 128, 128)
                for ti, (a, n) in enumerate(TT):
                    pb, kb = proj_tile(sl, 128, ti)
                    s2 = ti % 2
                    act(tq_[:, s2, 0:n], pb[:, 0:n], AF.Silu, kb, [("tq", s2)])
                    ko = ck(("oT", hp), a, a + n)
                    tt(oT[:, hp, a:a + n], oT[:, hp, a:a + n], tq_[:, s2, 0:n], ALU.mult, ko + [("tq", s2)], ko)
                A.release(mh)
                AR.release(mrr)
                pass
            P.fence()
            apply_wout([768, 896], 128)
            A.phase("ret")
            A.release(mret)
            A.release(m)
            P.fence()

        xn = A.alloc([8, NT], BF16)
        for l in range(2):
            if stage >= 1:
                rmsnorm(0 + l, xn)
                ffn(l, w1i, w1o, xn)
            if stage >= 2:
                rmsnorm(2 + l, xn)
                mixer(l, xn)
            if stage >= 5:
                rmsnorm(4 + l, xn)
                ffn(l, w2i, w2o, xn)
            if stage < 6:
                break

        P.fence()
        yt = A.alloc([8, 128]); ostg = A.alloc([2, 1024])
        sq = A.alloc([8, 512], BF16)
        rstd = A.alloc([512])
        oblocks = [(16 + 128 * i, 128, yp[i * 128:(i + 1) * 128, :]) for i in range(16)] + [(NP, NS, ys[:, :])]
        for bi, (a, n, dst) in enumerate(oblocks):
            kx = ck("xT", a, a + n)
            for c in range(8):
                act(sq[:, c, 0:n], xT[:, c, a:a + n], AF.Square, kx, [("sq", c)])
            pq, kq = PQ()
            for c in range(8):
                mm(pq[:, 0:n], onesm, sq[:, c, 0:n], [("sq", c), "onesm"], kq, start=(c == 0), stop=(c == 7))
            ts(rstd[:, 0:n], pq[:, 0:n], EPS, -0.5, ALU.add, ALU.pow, kq, ["rstd"])
            for c in range(8):
                stt(yt[:, c, 0:n], xT[:, c, a:a + n], gain(6, c), rstd[:, 0:n], ALU.mult, ALU.mult, kx + ["rstd", "pcol"], ["yt"])
            sl = bi % 2
            for half in range(2):
                pb, kb = PB()
                for q in range(4):
                    c = half * 4 + q
                    tr(pb[0:n, q * 128:(q + 1) * 128], yt[:, c, 0:n], ident, ["yt", "ident"], kb)
                cp(ostg[0:n, sl, half * 512:(half + 1) * 512], pb[0:n, :], kb, [("ostg", sl, half)], eng=("act" if half else "dve"))
            P.dma("sp", dst, ostg[0:n, sl, :], r=[("ostg", sl, 0), ("ostg", sl, 1)], is_out=True)
        A.phase('final')
        P.arena_hw = A.peaks + [('rr', AR.hw)]
        P.emit()
    return nc, P


def _consts():
    i = np.arange(128)
    c = {}
    c["c_ident"] = np.eye(128, dtype=np.float32)
    c["c_ut"] = (i[:, None] <= i[None, :]).astype(np.float32)
    c["c_mask"] = np.where(i[None, :] > i[:, None], 0.0, -1e30).astype(np.float32)
    gam = (1.0 - 2.0 ** (-5.0 - np.arange(4))).astype(np.float64)
    dec = np.zeros((4, 128, 128), np.float64)
    diff = (i[None, :] - i[:, None]).astype(np.float64)
    for h in range(4):
        dec[h] = np.where(diff >= 0, 0.125 * gam[h] ** np.maximum(diff, 0), 0.0)
    c["c_retdec"] = dec.astype(np.float32)
    eg = np.zeros((64, 4, 128), np.float64)
    for h in range(4):
        eg[:, h, :] = gam[h] ** (i[None, :] + 1.0)
    c["c_reteg"] = eg.astype(np.float32)
    kd = np.zeros((128, 8), np.float64)
    for h in range(4):
        kd[:, 2 * h] = 0.125 * gam[h] ** (127.0 - i)
        kd[:16, 2 * h + 1] = 0.125 * gam[h] ** (15.0 - i[:16])
    c["c_retkd"] = kd.astype(np.float32)
    gc = np.zeros((128, 8), np.float64)
    for hp in range(2):
        for hh in range(2):
            g_ = gam[2 * hp + hh]
            gc[hh * 64:(hh + 1) * 64, 4 * hp + 0] = g_ ** 128
            gc[hh * 64:(hh + 1) * 64, 4 * hp + 1] = g_ ** 16
            gc[hh * 64:(hh + 1) * 64, 4 * hp + 2] = g_
    c["c_gamc"] = gc.astype(np.float32)
    pos = np.concatenate([np.arange(NP), np.full(NS, 16384)]).astype(np.float32)
    inv = (10000.0 ** (-np.arange(32, dtype=np.float32) / 32)).astype(np.float32)
    ang = (pos[None, :] * inv[:, None]).astype(np.float32).astype(np.float64)
    cos = np.cos(ang); sin = np.sin(ang)
    c["c_cos"] = np.concatenate([cos, cos], 0).astype(np.float32)
    c["c_sin"] = np.concatenate([-sin, sin], 0).astype(np.float32)
    return c


_CACHE = {}


def _get_nc(stage):
    if stage not in _CACHE:
        _CACHE[stage] = build(stage)[0]
    return _CACHE[stage]


def kernel(x_prompt, x_sample, state_dn, state_dn_conv, state_lru, state_lru_conv, state_ret,
           meta_tokens, norm_ffn1, w_ffn1_in, w_ffn1_out, norm_mix, w_in, dn_conv_w, dn_a_log,
           dn_dt_bias, dn_norm_w, lru_conv_w, lru_conv_b, lru_wa, lru_ba, lru_wx, lru_bx, lru_lambda,
           w_out, norm_ffn2, w_ffn2_in, w_ffn2_out, norm_final, _stage=99):
    f = lambda a: np.ascontiguousarray(np.asarray(a, dtype=np.float32))
    nc = _get_nc(_stage)
    pvec = np.concatenate([
        f(norm_ffn1).reshape(16, 128), f(norm_mix).reshape(16, 128), f(norm_ffn2).reshape(16, 128),
        f(norm_final).reshape(8, 128),
        f(dn_conv_w).reshape(96, 128), f(lru_conv_w).reshape(16, 128), f(lru_conv_b).reshape(4, 128),
        f(lru_ba).reshape(4, 128), f(lru_bx).reshape(4, 128), f(lru_lambda).reshape(4, 128),
        f(dn_norm_w).reshape(2, 128)], axis=0)
    abv = np.concatenate([f(dn_a_log).reshape(-1), f(dn_dt_bias).reshape(-1)]).reshape(1, 16)
    shared = {
        "meta": f(meta_tokens), "w1i": f(w_ffn1_in), "w1o": f(w_ffn1_out), "w2i": f(w_ffn2_in), "w2o": f(w_ffn2_out),
        "win": f(w_in), "wout": f(w_out), "lwa": f(lru_wa), "lwx": f(lru_wx), "pvec": f(pvec), "abv": f(abv),
    }
    shared.update(_consts())
    xp = f(x_prompt); xs = f(x_sample)
    sdn = f(state_dn); sdnc = f(state_dn_conv); slru = f(state_lru); slruc = f(state_lru_conv); sret = f(state_ret)
    in_maps = []
    for c in range(NCORES):
        s = slice(c * NS, (c + 1) * NS)
        m = dict(shared)
        m.update({"xp": xp[c], "xs": f(xs[s, 0, :]), "sdn": f(sdn[:, s]), "sdnc": f(sdnc[:, s]), "slru": f(slru[:, s]),
                  "slruc": f(slruc[:, s]), "sret": f(sret[:, s])})
        in_maps.append(m)
    res = run_bass_kernel_spmd(nc, in_maps, core_ids=list(range(NCORES)))
    R = res.results
    y_prompt = np.stack([R[c]["yp"] for c in range(NCORES)], 0)
    y_sample = np.concatenate([R[c]["ys"] for c in range(NCORES)], 0)[:, None, :]
    stp = lambda k: np.stack([R[c][k] for c in range(NCORES)], 1)
    cat = lambda k: np.concatenate([R[c][k] for c in range(NCORES)], 1)
    return (y_prompt, y_sample, stp("dn_p"), stp("dnc_p"), stp("lru_p"), stp("lruc_p"), stp("ret_p"),
            cat("dn_s"), cat("dnc_s"), cat("lru_s"), cat("lruc_s"), cat("ret_s"))
```
